# Optimizing a Trainium2 kernel written in Bass

```python
import math
import jax
import jax.numpy as jnp
from jax import lax
import numpy as np

D_MODEL = 1024
BATCH = 16
SEQ = 2048
DEPTH = 1
DEC_BATCH = 2
DEC_SEQ = 8192
PAST_LEN = 128

SSD_D_INNER = 2 * D_MODEL
SSD_HEAD_DIM = 64
SSD_N_HEADS = SSD_D_INNER // SSD_HEAD_DIM
SSD_N_GROUPS = 4
SSD_D_STATE = 128
SSD_XBC = SSD_D_INNER + 2 * SSD_N_GROUPS * SSD_D_STATE
SSD_CONV = 5
SSD_CHUNK = 64
DT_MIN = 1e-3
DT_MAX = 1e-1

GLA_N_HEADS = 4
GLA_DK = D_MODEL // 2
GLA_DV = D_MODEL
GLA_HEAD_K = GLA_DK // GLA_N_HEADS
GLA_HEAD_V = GLA_DV // GLA_N_HEADS
GLA_GATE_RANK = 16
GLA_GATE_NORM = 16.0
GLA_CHUNK = 64

D_FF = 2816
FFN_CONV = 3
EPS = 1e-6

IN_SPLITS = (SSD_D_INNER, SSD_XBC, 2 * SSD_N_HEADS, GLA_DK, GLA_DK, GLA_DV, GLA_DV, 2 * GLA_GATE_RANK, 2 * D_MODEL)
IN_PROJ_DIM = sum(IN_SPLITS)

kernel_name = "hybrid_bidir_ssd_gla_convffn"


def _rms(x):
    xf = x.astype(jnp.float32)
    return xf * lax.rsqrt(jnp.mean(xf * xf, axis=-1, keepdims=True) + EPS)


def rmsnorm(x, w):
    return (_rms(x) * w.astype(jnp.float32)).astype(x.dtype)


def _flip(a):
    return jnp.flip(a, axis=1)


def dwconv_centred(x, w, b):
    k_w, ch = w.shape
    pad = (k_w - 1) // 2
    y = lax.conv_general_dilated(x, w[:, None, :].astype(x.dtype), window_strides=(1,), padding=((pad, pad),),
                                 dimension_numbers=('NWC', 'WIO', 'NWC'), feature_group_count=ch)
    return y + b.astype(x.dtype)


def ssd_scan(x, dt, a_log, bmat, cmat):
    bsz, seqlen, n_heads, hd = x.shape
    n_groups, d_state = bmat.shape[2], bmat.shape[3]
    hpg = n_heads // n_groups
    q_len = SSD_CHUNK
    n_chunks = seqlen // q_len
    a = (-jnp.exp(a_log) * dt).reshape(bsz, n_chunks, q_len, n_groups, hpg)
    xd = (x * dt[..., None]).reshape(bsz, n_chunks, q_len, n_groups, hpg, hd)
    bm = bmat.reshape(bsz, n_chunks, q_len, n_groups, d_state)
    cm = cmat.reshape(bsz, n_chunks, q_len, n_groups, d_state)
    a_cs = jnp.cumsum(a, axis=2)
    a_t = jnp.moveaxis(a_cs, 2, -1)
    seg = a_t[..., :, None] - a_t[..., None, :]
    mask = jnp.tril(jnp.ones((q_len, q_len), dtype=bool))
    decay_in = jnp.exp(jnp.where(mask, seg, -jnp.inf))
    cb = jnp.einsum('bcign,bcjgn->bcgij', cm, bm)
    y_diag = jnp.einsum('bcgij,bcgeij,bcjgep->bcigep', cb, decay_in, xd)
    a_last = a_cs[:, :, -1:]
    decay_to_end = jnp.exp(a_last - a_cs)
    states = jnp.einsum('bcqgn,bcqge,bcqgep->bcgepn', bm, decay_to_end, xd)
    chunk_decay = jnp.exp(a_last[:, :, 0])

    def step(h, inp):
        s, d = inp
        return d[..., None, None] * h + s, h

    h0 = jnp.zeros((bsz, n_groups, hpg, hd, d_state), dtype=x.dtype)
    _, h_in = lax.scan(step, h0, (jnp.moveaxis(states, 1, 0), jnp.moveaxis(chunk_decay, 1, 0)))
    h_in = jnp.moveaxis(h_in, 0, 1)
    y_off = jnp.einsum('bcqgn,bcgepn,bcqge->bcqgep', cm, h_in, jnp.exp(a_cs))
    return (y_diag + y_off).reshape(bsz, seqlen, n_heads, hd)


def gla_scan(q, k, v, log_a):
    bsz, seqlen, n_heads, dk = q.shape
    dv = v.shape[-1]
    q_len = GLA_CHUNK
    n_chunks = seqlen // q_len
    q = q.reshape(bsz, n_chunks, q_len, n_heads, dk)
    k = k.reshape(bsz, n_chunks, q_len, n_heads, dk)
    v = v.reshape(bsz, n_chunks, q_len, n_heads, dv)
    b_cs = jnp.cumsum(log_a.reshape(bsz, n_chunks, q_len, n_heads, dk), axis=2)
    q_in = q * jnp.exp(b_cs)
    k_in = k * jnp.exp(-b_cs)
    att = jnp.einsum('bcihd,bcjhd->bchij', q_in, k_in)
    mask = jnp.tril(jnp.ones((q_len, q_len), dtype=bool))
    att = jnp.where(mask, att, 0.0)
    o_intra = jnp.einsum('bchij,bcjhv->bcihv', att, v)
    b_last = b_cs[:, :, -1:]
    k_dec = k * jnp.exp(b_last - b_cs)
    states = jnp.einsum('bcqhd,bcqhv->bchdv', k_dec, v)
    chunk_decay = jnp.exp(b_last[:, :, 0])

    def step(h, inp):
        s, d = inp
        return d[..., None] * h + s, h

    h0 = jnp.zeros((bsz, n_heads, dk, dv), dtype=q.dtype)
    _, h_in = lax.scan(step, h0, (jnp.moveaxis(states, 1, 0), jnp.moveaxis(chunk_decay, 1, 0)))
    h_in = jnp.moveaxis(h_in, 0, 1)
    o_inter = jnp.einsum('bcqhd,bchdv->bcqhv', q_in, h_in)
    return (o_intra + o_inter).reshape(bsz, seqlen, n_heads, dv)


def encoder_layer(x, norm1_w, w_in, ssd_conv_w, ssd_conv_b, ssd_a_log, ssd_dt_bias, ssd_d, ssd_norm_w, w_ssd_out,
                  gla_gate_w2, gla_gate_b, gla_norm_w, w_gla_out, w_o, norm2_w, w_ffn_up, ffn_conv_w, ffn_conv_b,
                  w_ffn_down):
    f32 = jnp.float32
    bsz, seqlen, _ = x.shape
    h = rmsnorm(x, norm1_w)
    proj = h @ w_in
    offs = [int(o) for o in np.cumsum(IN_SPLITS)[:-1]]
    z, xbc, dt_raw, q, k, v, g, gk, gates = jnp.split(proj, offs, axis=-1)

    xbc = jax.nn.silu(dwconv_centred(xbc, ssd_conv_w, ssd_conv_b)).astype(f32)
    xs, bm, cm = jnp.split(xbc, [SSD_D_INNER, SSD_D_INNER + SSD_N_GROUPS * SSD_D_STATE], axis=-1)
    xs = xs.reshape(bsz, seqlen, SSD_N_HEADS, SSD_HEAD_DIM)
    bm = bm.reshape(bsz, seqlen, SSD_N_GROUPS, SSD_D_STATE)
    cm = cm.reshape(bsz, seqlen, SSD_N_GROUPS, SSD_D_STATE)
    dt = jax.nn.softplus(dt_raw.astype(f32).reshape(bsz, seqlen, 2, SSD_N_HEADS) + ssd_dt_bias.astype(f32))
    a_log = ssd_a_log.astype(f32)
    y_f = ssd_scan(xs, dt[:, :, 0], a_log[0], bm, cm)
    y_b = _flip(ssd_scan(_flip(xs), _flip(dt[:, :, 1]), a_log[1], _flip(bm), _flip(cm)))
    y = y_f + y_b + ssd_d.astype(f32)[:, None] * xs
    y = y.reshape(bsz, seqlen, SSD_D_INNER) * jax.nn.silu(z.astype(f32))
    y = _rms(y.reshape(bsz, seqlen, SSD_N_GROUPS, -1)).reshape(bsz, seqlen, SSD_D_INNER) * ssd_norm_w.astype(f32)
    branch_a = y.astype(x.dtype) @ w_ssd_out

    q = q.astype(f32).reshape(bsz, seqlen, GLA_N_HEADS, GLA_HEAD_K) * (GLA_HEAD_K ** -0.5)
    k = k.astype(f32).reshape(bsz, seqlen, GLA_N_HEADS, GLA_HEAD_K)
    v = v.astype(f32).reshape(bsz, seqlen, GLA_N_HEADS, GLA_HEAD_V)
    gk = gk.astype(f32).reshape(bsz, seqlen, 2, GLA_GATE_RANK)
    log_a = jax.nn.log_sigmoid(jnp.einsum('blzr,zrk->blzk', gk, gla_gate_w2.astype(f32)) + gla_gate_b.astype(f32)) / GLA_GATE_NORM
    log_a = log_a.reshape(bsz, seqlen, 2, GLA_N_HEADS, GLA_HEAD_K)
    o = gla_scan(q, k, v, log_a[:, :, 0]) + _flip(gla_scan(_flip(q), _flip(k), _flip(v), _flip(log_a[:, :, 1])))
    o = _rms(o).reshape(bsz, seqlen, GLA_DV) * gla_norm_w.astype(f32) * jax.nn.silu(g.astype(f32))
    branch_b = o.astype(x.dtype) @ w_gla_out

    gate_a, gate_b = jnp.split(gates, 2, axis=-1)
    mixed = jax.nn.sigmoid(gate_a) * branch_a + jax.nn.sigmoid(gate_b) * branch_b
    x = x + mixed @ w_o

    h = rmsnorm(x, norm2_w)
    u = dwconv_centred(h @ w_ffn_up, ffn_conv_w, ffn_conv_b)
    u_gate, u_val = jnp.split(u, 2, axis=-1)
    x = x + (jax.nn.silu(u_gate) * u_val) @ w_ffn_down
    return x


def trunk(x, norm1_w, w_in, ssd_conv_w, ssd_conv_b, ssd_a_log, ssd_dt_bias, ssd_d, ssd_norm_w, w_ssd_out,
          gla_gate_w2, gla_gate_b, gla_norm_w, w_gla_out, w_o, norm2_w, w_ffn_up, ffn_conv_w, ffn_conv_b,
          w_ffn_down, final_norm_w):
    for i in range(DEPTH):
        x = encoder_layer(x, norm1_w[i], w_in[i], ssd_conv_w[i], ssd_conv_b[i], ssd_a_log[i], ssd_dt_bias[i],
                          ssd_d[i], ssd_norm_w[i], w_ssd_out[i], gla_gate_w2[i], gla_gate_b[i], gla_norm_w[i],
                          w_gla_out[i], w_o[i], norm2_w[i], w_ffn_up[i], ffn_conv_w[i], ffn_conv_b[i], w_ffn_down[i])
    return rmsnorm(x, final_norm_w)


def setup_inputs(seed: int = 0) -> dict:
    key = jax.random.key(seed)
    ks = jax.random.split(key, 24)
    nrm = jax.random.normal
    f32 = jnp.float32
    u_dt = jax.random.uniform(ks[7], (DEPTH, 2, SSD_N_HEADS), dtype=f32)
    dt0 = jnp.exp(u_dt * (math.log(DT_MAX) - math.log(DT_MIN)) + math.log(DT_MIN))
    return {
        "x_prompt": nrm(ks[0], (BATCH, SEQ, D_MODEL), f32),
        "x_sample": nrm(ks[1], (DEC_BATCH, DEC_SEQ, D_MODEL), f32),
        "norm1_w": 1.0 + 0.02 * nrm(ks[2], (DEPTH, D_MODEL), f32),
        "w_in": nrm(ks[3], (DEPTH, D_MODEL, IN_PROJ_DIM), f32) * D_MODEL ** -0.5,
        "ssd_conv_w": nrm(ks[4], (DEPTH, SSD_CONV, SSD_XBC), f32) * SSD_CONV ** -0.5,
        "ssd_conv_b": 0.02 * nrm(ks[5], (DEPTH, SSD_XBC), f32),
        "ssd_a_log": jnp.log(jax.random.uniform(ks[6], (DEPTH, 2, SSD_N_HEADS), f32, minval=1.0, maxval=16.0)),
        "ssd_dt_bias": dt0 + jnp.log(-jnp.expm1(-dt0)),
        "ssd_d": 1.0 + 0.1 * nrm(ks[8], (DEPTH, SSD_N_HEADS), f32),
        "ssd_norm_w": 1.0 + 0.02 * nrm(ks[9], (DEPTH, SSD_D_INNER), f32),
        "w_ssd_out": nrm(ks[10], (DEPTH, SSD_D_INNER, D_MODEL), f32) * SSD_D_INNER ** -0.5,
        "gla_gate_w2": nrm(ks[11], (DEPTH, 2, GLA_GATE_RANK, GLA_DK), f32) * GLA_GATE_RANK ** -0.5,
        "gla_gate_b": 0.1 * nrm(ks[12], (DEPTH, 2, GLA_DK), f32),
        "gla_norm_w": 1.0 + 0.02 * nrm(ks[13], (DEPTH, GLA_DV), f32),
        "w_gla_out": nrm(ks[14], (DEPTH, GLA_DV, D_MODEL), f32) * GLA_DV ** -0.5,
        "w_o": nrm(ks[15], (DEPTH, D_MODEL, D_MODEL), f32) * D_MODEL ** -0.5,
        "norm2_w": 1.0 + 0.02 * nrm(ks[16], (DEPTH, D_MODEL), f32),
        "w_ffn_up": nrm(ks[17], (DEPTH, D_MODEL, 2 * D_FF), f32) * D_MODEL ** -0.5,
        "ffn_conv_w": nrm(ks[18], (DEPTH, FFN_CONV, 2 * D_FF), f32) * FFN_CONV ** -0.5,
        "ffn_conv_b": 0.02 * nrm(ks[19], (DEPTH, 2 * D_FF), f32),
        "w_ffn_down": nrm(ks[20], (DEPTH, D_FF, D_MODEL), f32) * D_FF ** -0.5,
        "final_norm_w": 1.0 + 0.02 * nrm(ks[21], (D_MODEL,), f32),
    }


def reference(x_prompt, x_sample, norm1_w, w_in, ssd_conv_w, ssd_conv_b, ssd_a_log, ssd_dt_bias, ssd_d, ssd_norm_w,
              w_ssd_out, gla_gate_w2, gla_gate_b, gla_norm_w, w_gla_out, w_o, norm2_w, w_ffn_up, ffn_conv_w,
              ffn_conv_b, w_ffn_down, final_norm_w):
    y_prompt = trunk(x_prompt, norm1_w, w_in, ssd_conv_w, ssd_conv_b, ssd_a_log, ssd_dt_bias, ssd_d, ssd_norm_w,
                     w_ssd_out, gla_gate_w2, gla_gate_b, gla_norm_w, w_gla_out, w_o, norm2_w, w_ffn_up, ffn_conv_w,
                     ffn_conv_b, w_ffn_down, final_norm_w)
    y_sample = trunk(x_sample, norm1_w, w_in, ssd_conv_w, ssd_conv_b, ssd_a_log, ssd_dt_bias, ssd_d, ssd_norm_w,
                     w_ssd_out, gla_gate_w2, gla_gate_b, gla_norm_w, w_gla_out, w_o, norm2_w, w_ffn_up, ffn_conv_w,
                     ffn_conv_b, w_ffn_down, final_norm_w)
    return (y_prompt, y_sample)
```

```python
import numpy as np
from contextlib import ExitStack
import concourse.bass as bass
import concourse.mybir as mybir
from concourse.bass_utils import run_bass_kernel_spmd

F32 = mybir.dt.float32
BF16 = mybir.dt.bfloat16
AF = mybir.ActivationFunctionType
ALU = mybir.AluOpType

D_MODEL = 1024
D_IN = 10336
D_FF = 2816
EPS = 1e-6
O_Z, O_XBC, O_DT, O_Q, O_K, O_V, O_G, O_GK, O_GT = 0, 2048, 5120, 5184, 5696, 6208, 7232, 8256, 8288


class Tok:
    __slots__ = ("w", "r")

    def __init__(self):
        self.w = None
        self.r = {}


class DSem:
    def __init__(self, key, h):
        self.key = key
        self.h = h
        self.count = 0


class TL:
    def __init__(self, t, ds=None):
        self.t = t
        self.tok = Tok()
        self.ds = ds

    def __getitem__(self, k):
        return self.t[k]


class Ctx:
    def __init__(self, nc):
        self.nc = nc
        self.eng = {"pe": nc.tensor, "dve": nc.vector, "act": nc.scalar, "pool": nc.gpsimd, "sp": nc.sync}
        self.esem = {e: nc.alloc_semaphore(name="e_" + e) for e in ("pe", "dve", "act", "pool")}
        self.ecnt = {e: 0 for e in self.esem}
        self.seen = {e: {} for e in self.eng}
        self.dsems = {}
        self.free_ds = []
        self.phase_ds = []
        self.nds = 0
        self.ninst = 0

    def new_ds(self):
        if self.free_ds:
            d = self.free_ds.pop()
        else:
            key = "d%d" % self.nds
            self.nds += 1
            d = DSem(key, self.nc.alloc_semaphore(name=key))
            self.dsems[key] = d
        self.phase_ds.append(d)
        return d

    def end_phase(self):
        self.barrier()
        self.free_ds.extend(self.phase_ds)
        self.phase_ds = []

    def tile(self, es, name, shape, dt, dma=False):
        self.ntile = getattr(self, "ntile", 0) + 1
        t = es.enter_context(self.nc.sbuf_tensor("%s_%d" % (name, self.ntile), shape, dt))
        return TL(t, self.new_ds() if dma else None)

    def semh(self, k):
        return self.esem[k] if k in self.esem else self.dsems[k].h

    def cur(self, k):
        return self.ecnt[k] if k in self.ecnt else self.dsems[k].count

    def _waits(self, e, R, W):
        need = {}

        def add(ev, kind):
            if ev is None:
                return
            k, v = ev
            if k == e and (e == "pe" or kind != "raw"):
                return
            if k in self.dsems:
                v = self.dsems[k].count
            if need.get(k, 0) < v:
                need[k] = v

        for t in R:
            add(t.tok.w, "raw")
        for t in W:
            add(t.tok.w, "waw")
            for k, v in t.tok.r.items():
                add((k, v), "war")
        sn = self.seen[e]
        for k, v in need.items():
            if sn.get(k, 0) < v:
                self.eng[e].wait_ge(self.semh(k), v)
                sn[k] = v
                self.ninst += 1

    def op(self, e, fn, R=(), W=(), inc=True):
        self._waits(e, R, W)
        ins = fn(self.eng[e])
        self.ninst += 1
        if inc:
            self.ecnt[e] += 1
            ins.then_inc(self.esem[e], 1)
            v = self.ecnt[e]
        else:
            v = self.ecnt[e] + 1
        for t in R:
            if t.tok.r.get(e, 0) < v:
                t.tok.r[e] = v
        for t in W:
            t.tok.w = (e, v)
            t.tok.r = {}
        return ins

    def dma(self, out, in_, R=(), W=(), ds=None, q="sp", **kw):
        self._waits(q, R, W)
        ins = self.eng[q].dma_start(out=out, in_=in_, **kw)
        ins.then_inc(ds.h, 16)
        ds.count += 16
        self.ninst += 1
        for t in R:
            t.tok.r[ds.key] = ds.count
        for t in W:
            t.tok.w = (ds.key, ds.count)
            t.tok.r = {}
        return ins

    def barrier(self):
        keys = list(self.esem.keys()) + [k for k, d in self.dsems.items() if d.count > 0]
        for e in self.eng:
            sn = self.seen[e]
            for k in keys:
                v = self.cur(k)
                if k == e or v == 0:
                    continue
                if sn.get(k, 0) < v:
                    self.eng[e].wait_ge(self.semh(k), v)
                    sn[k] = v
                    self.ninst += 1

    def mm(self, out, lhsT, rhs, start, stop, R, W, inc=False):
        return self.op("pe", lambda E: E.matmul(out, lhsT=lhsT, rhs=rhs, start=start, stop=stop), R, W, inc)

    def tr(self, out, in_, ident, R, W, inc=False):
        return self.op("pe", lambda E: E.transpose(out, in_, ident), R, W, inc)

    def act(self, out, in_, func, R, W, bias=None, scale=None, accum_out=None):
        kw = {}
        if bias is not None:
            kw["bias"] = bias
        if scale is not None:
            kw["scale"] = scale
        if accum_out is not None:
            kw["accum_out"] = accum_out
        return self.op("act", lambda E: E.activation(out=out, in_=in_, func=func, **kw), R, W)

    def tt(self, e, out, in0, in1, op, R, W):
        return self.op(e, lambda E: E.tensor_tensor(out=out, in0=in0, in1=in1, op=op), R, W)

    def ts(self, e, out, in0, s1, s2, op0, op1, R, W):
        if s2 is None:
            return self.op(e, lambda E: E.tensor_scalar(out=out, in0=in0, scalar1=s1, scalar2=None, op0=op0), R, W)
        return self.op(e, lambda E: E.tensor_scalar(out=out, in0=in0, scalar1=s1, scalar2=s2, op0=op0, op1=op1), R, W)

    def stt(self, e, out, in0, scalar, in1, op0, op1, R, W):
        return self.op(e, lambda E: E.scalar_tensor_tensor(out=out, in0=in0, scalar=scalar, in1=in1, op0=op0, op1=op1), R, W)

    def cp(self, e, out, in_, R, W):
        if e == "act":
            return self.op(e, lambda E: E.copy(out=out, in_=in_), R, W)
        return self.op(e, lambda E: E.tensor_copy(out=out, in_=in_), R, W)


CST_ITEMS = [
    ("ident", 128), ("Af", 128), ("Ab", 128), ("Cf", 128), ("Cb", 128), ("Asf", 128), ("Asb", 128),
    ("Csf", 128), ("Csb", 128), ("Bf", 64), ("Bb", 64), ("ch0", 128), ("ch1", 128),
    ("n1w", 1024), ("n2w", 1024), ("fnw", 1024), ("snw", 2048), ("gnw", 1024), ("dtb", 64), ("alog", 64),
    ("dD", 32), ("cw", 120), ("cb", 24), ("fcw", 132), ("fcb", 44), ("w2a", 1024), ("flags", 8),
]
CST_OFF = {}
_o = 0
for _n, _w in CST_ITEMS:
    CST_OFF[_n] = (_o, _w)
    _o += _w
CST_W = _o


def build_consts(inp, flags):
    c = np.zeros((128, CST_W), np.float32)

    def put(name, arr):
        o, w = CST_OFF[name]
        c[:, o:o + w] = arr

    k = np.arange(128)
    same = np.ones((128, 128), bool)
    Af = ((k[:, None] > k[None, :]) & same).astype(np.float32)
    Ab = ((k[:, None] < k[None, :]) & same).astype(np.float32)
    Cf = ((k[:, None] <= k[None, :]) & same).astype(np.float32)
    Cb = ((k[:, None] >= k[None, :]) & same).astype(np.float32)
    put("ident", np.eye(128, dtype=np.float32))
    put("Af", Af); put("Ab", Ab); put("Cf", Cf); put("Cb", Cb)
    put("Asf", Af * (-1.0 / 16)); put("Asb", Ab * (-1.0 / 16)); put("Csf", Cf * (-1.0 / 16)); put("Csb", Cb * (-1.0 / 16))
    il = np.arange(64)
    put("Bf", ((k[:, None] % 64) <= il[None, :]).astype(np.float32))
    put("Bb", ((k[:, None] % 64) >= il[None, :]).astype(np.float32))
    put("ch0", np.ones((128, 128), np.float32))
    put("ch1", np.repeat((k >= 64).astype(np.float32)[:, None], 128, 1))
    rep = lambda v: np.broadcast_to(np.asarray(v, np.float32).reshape(1, -1), (128, np.asarray(v).size))
    put("n1w", rep(inp["norm1_w"][0])); put("n2w", rep(inp["norm2_w"][0])); put("fnw", rep(inp["final_norm_w"]))
    put("snw", rep(inp["ssd_norm_w"][0])); put("gnw", rep(inp["gla_norm_w"][0]))
    put("dtb", rep(inp["ssd_dt_bias"][0])); put("alog", rep(inp["ssd_a_log"][0])); put("dD", rep(inp["ssd_d"][0]))
    cw = inp["ssd_conv_w"][0]
    put("cw", cw.reshape(5, 24, 128).transpose(2, 1, 0).reshape(128, 120))
    put("cb", inp["ssd_conv_b"][0].reshape(24, 128).T)
    fw = inp["ffn_conv_w"][0]
    put("fcw", fw.reshape(3, 44, 128).transpose(2, 1, 0).reshape(128, 132))
    put("fcb", inp["ffn_conv_b"][0].reshape(44, 128).T)
    w2a = np.zeros((128, 2, 512), np.float32)
    w2a[0:16] = inp["gla_gate_w2"][0].transpose(1, 0, 2)
    w2a[32] = inp["gla_gate_b"][0]
    put("w2a", w2a.reshape(128, 1024))
    put("flags", np.broadcast_to(np.asarray(flags, np.float32).reshape(1, 8), (128, 8)))
    return c


class Cfg:
    def __init__(self, SEG=2048, NSEG=4, debug=False, phases=(0, 1, 2, 3, 4, 5)):
        self.SEG, self.NSEG = SEG, NSEG
        self.T = SEG * NSEG
        self.NB = self.T // 512
        self.NT = self.T // 128
        self.debug = debug
        self.phases = phases


WEIGHTS = [("w_in", 1024, D_IN), ("w_ssd_out", 2048, 1024), ("w_gla_out", 1024, 1024), ("w_o", 1024, 1024),
           ("w_ffn_up", 1024, 2 * D_FF), ("w_ffn_down", D_FF, 1024)]


def build(cfg):
    nc = bass.Bass("TRN2", target_bir_lowering=False)
    T, NB = cfg.T, cfg.NB
    D = {}
    D["xm"] = nc.dram_tensor("xm", [T, 1024], F32, kind="ExternalInput").ap()
    D["xh"] = nc.dram_tensor("xh", [NB * 4, 1024], F32, kind="ExternalInput").ap()
    D["cst"] = nc.dram_tensor("cst", [128, CST_W], F32, kind="ExternalInput").ap()
    for n, r, cc in WEIGHTS:
        D[n] = nc.dram_tensor(n, [r, cc], F32, kind="ExternalInput").ap()
        D[n + "_b"] = nc.dram_tensor(n + "_b", [r, cc], BF16, kind="Internal").ap()
    D["y"] = nc.dram_tensor("y", [T, 1024], F32, kind="ExternalOutput").ap()
    sk = "ExternalOutput" if cfg.debug else "Internal"

    def scr(name, shape, dt):
        D[name] = nc.dram_tensor(name, shape, dt, kind=sk).ap()

    scr("zs", [T, 2048], BF16); scr("gts", [T, 2048], BF16); scr("vs", [T, 1024], BF16); scr("gs", [T, 1024], BF16)
    scr("xs", [T, 2048], BF16); scr("Bm", [T, 512], BF16); scr("km", [T, 512], BF16)
    scr("BT", [NB, 512, 512], BF16); scr("CT", [NB, 512, 512], BF16); scr("qT", [NB, 512, 512], BF16); scr("kT", [NB, 512, 512], BF16)
    scr("la", [T, 1024], BF16); scr("dts", [T, 64], F32)
    scr("yb", [T, 2048], BF16); scr("ob", [T, 1024], BF16)
    scr("yn", [T, 2048], BF16); scr("on", [T, 1024], BF16)
    scr("x1", [T, 1024], F32); scr("h2T", [1024, T], BF16)

    c = Ctx(nc)
    c.banks = [TL(nc.alloc_psum_tensor("bank%d" % i, [128, 512], F32)) for i in range(8)]
    if 0 in cfg.phases:
        phase0(c, D, cfg)
    if 1 in cfg.phases:
        phase1(c, D, cfg)
    if 2 in cfg.phases:
        scan_phase(c, D, cfg, "b")
    if 3 in cfg.phases:
        scan_phase(c, D, cfg, "f")
    if 4 in cfg.phases:
        phase_d1(c, D, cfg)
    if 5 in cfg.phases:
        phase_ffn(c, D, cfg)
    c.barrier()
    return nc, c


def cst_ap(D, name, rows=128):
    o, w = CST_OFF[name]
    return D["cst"][0:rows, o:o + w]


def load_cst(c, es, D, name, dt=F32, rows=128):
    o, w = CST_OFF[name]
    t32 = c.tile(es, "c_" + name, [128, w], F32, dma=True)
    c.dma(t32[0:rows, :], cst_ap(D, name, rows), W=[t32], ds=t32.ds)
    if dt == F32:
        return t32
    tb = c.tile(es, "cb_" + name, [128, w], BF16)
    c.cp("pool", tb[0:rows, :], t32[0:rows, :], R=[t32], W=[tb])
    return tb


def phase0(c, D, cfg):
    ds_in = c.new_ds()
    ds = c.new_ds()
    for n, r, cc in WEIGHTS:
        for r0 in range(0, r, 128):
            c.dma(D[n + "_b"][r0:r0 + 128, :], D[n][r0:r0 + 128, :], ds=(ds_in if n == "w_in" else ds), q="pool")
    c.ds_win = ds_in


def phase1(c, D, cfg):
    nc = c.nc
    NB = cfg.NB
    with ExitStack() as es:
        S = lambda n, sh, dt, dma=False: c.tile(es, n, sh, dt, dma)
        identf = load_cst(c, es, D, "ident")
        ident = S("identb", [128, 128], BF16)
        c.cp("pool", ident[:, :], identf[:, :], R=[identf], W=[ident])
        n1w = load_cst(c, es, D, "n1w")
        cw = load_cst(c, es, D, "cw")
        dg = S("dg", [128, 120, 128], BF16)
        for idx in range(120):
            c.ts("dve", dg[:, idx, :], identf[:, :], cw[:, idx:idx + 1], None, ALU.mult, None, R=[identf, cw], W=[dg])
        xin = [S("xin%d" % i, [128, 2, 260], BF16) for i in range(2)]
        cb = load_cst(c, es, D, "cb")
        dtb = load_cst(c, es, D, "dtb")
        w2a = load_cst(c, es, D, "w2a", BF16, rows=64)
        xt = [S("xt%d" % i, [128, 4, 1024], F32, True) for i in range(2)]
        xh = [S("xh%d" % i, [4, 1024], F32, True) for i in range(2)]
        hb = [S("hb%d" % i, [128, 4, 1024], BF16) for i in range(2)]
        hbh = [S("hbh%d" % i, [4, 1024], BF16) for i in range(2)]
        hT = [S("hT%d" % i, [128, 8, 516], BF16) for i in range(2)]
        wt = [S("wt%d" % i, [128, 8, 512], BF16, True) for i in range(3)]
        st = [S("st%d" % i, [128, 4, 512], BF16, True) for i in range(3)]
        xo = [S("xo%d" % i, [128, 4, 512], BF16, True) for i in range(2)]
        acc = [S("acc%d" % i, [128, 2, 256], F32) for i in range(3)]
        junk = S("junk", [128, 1024], BF16)
        ss = [S("ss%d" % i, [128, 8], F32) for i in range(2)]
        rstd = [S("rstd%d" % i, [128, 8], F32) for i in range(2)]
        gkT = [S("gkT%d" % z, [64, 512], BF16) for z in range(2)]
        lat = [S("lat%d" % i, [128, 1024], BF16, True) for i in range(2)]
        et = [S("et%d" % i, [128, 1024], F32) for i in range(2)]
        dtt = S("dtt", [128, 4, 64], F32, True)
        dte = S("dte", [128, 4, 64], F32)
        banks = c.banks
        st_i = [0]
        wt_i = [0]
        bk_i = [0]

        def nbank():
            b = banks[bk_i[0] % 8]
            bk_i[0] += 1
            return b

        for z in range(2):
            c.op("pool", lambda E, z=z: E.memset(gkT[z][0:64, :], 0.0), W=[gkT[z]])
            c.op("pool", lambda E, z=z: E.memset(gkT[z][32:33, :], 1.0), W=[gkT[z]])

        def load_x(b):
            s = b % 2
            c.dma(xt[s][:, :, :], D["xm"][b * 512:(b + 1) * 512, :].rearrange("(j p) d -> p j d", p=128), W=[xt[s]], ds=xt[s].ds)
            c.dma(xh[s][0:4, :], D["xh"][b * 4:(b + 1) * 4, :], W=[xh[s]], ds=xh[s].ds)

        def prepA(b):
            s = b % 2
            for j in range(4):
                c.act(junk[:, :], xt[s][:, j, :], AF.Square, R=[xt[s]], W=[junk, ss[s]], accum_out=ss[s][:, j:j + 1])
            c.act(junk[0:4, :], xh[s][0:4, :], AF.Square, R=[xh[s]], W=[junk, ss[s]], accum_out=ss[s][0:4, 4:5])
            c.act(rstd[s][:, 0:5], ss[s][:, 0:5], AF.Ln, R=[ss[s]], W=[rstd[s]], scale=1.0 / 1024, bias=EPS)
            c.act(rstd[s][:, 0:5], rstd[s][:, 0:5], AF.Exp, R=[rstd[s]], W=[rstd[s]], scale=-0.5)
            for j in range(4):
                c.stt("dve", hb[s][:, j, :], xt[s][:, j, :], rstd[s][:, j:j + 1], n1w[:, :], ALU.mult, ALU.mult,
                      R=[xt[s], rstd[s], n1w], W=[hb[s]])
            c.stt("dve", hbh[s][0:4, :], xh[s][0:4, :], rstd[s][0:4, 4:5], n1w[0:4, :], ALU.mult, ALU.mult,
                  R=[xh[s], rstd[s], n1w], W=[hbh[s]])

        def prepB(b):
            s = b % 2
            for j in range(4):
                bk = nbank()
                pv = bk[:, :].bitcast(BF16).rearrange("p (a b) -> p a b", a=8)
                for kc in range(8):
                    c.tr(pv[:, kc, :], hb[s][:, j, kc * 128:(kc + 1) * 128], ident[:, :], R=[hb[s], ident], W=[bk], inc=(kc == 7))
                c.cp("dve" if j % 2 == 0 else "act", hT[s][:, :, 2 + 128 * j:2 + 128 * (j + 1)], pv[:, :, :], R=[bk], W=[hT[s]])
            bk = nbank()
            pv = bk[:, :].bitcast(BF16).rearrange("p (a b) -> p a b", a=8)
            for kc in range(8):
                c.tr(pv[:, kc, 0:4], hbh[s][0:4, kc * 128:(kc + 1) * 128], ident[0:4, 0:4], R=[hbh[s], ident], W=[bk], inc=(kc == 7))
            c.cp("dve", hT[s][:, :, 0:2], pv[:, :, 0:2], R=[bk], W=[hT[s]])
            c.cp("dve", hT[s][:, :, 514:516], pv[:, :, 2:4], R=[bk], W=[hT[s]])

        def load_w(c0, w):
            t = wt[wt_i[0] % 3]
            wt_i[0] += 1
            c.dma(t[:, :, 0:w], D["w_in_b"][:, c0:c0 + w].rearrange("(kc p) c -> p kc c", p=128), W=[t], ds=t.ds)
            return t

        def tok_group(t, b, c0, w, func, dst, dcol, scale=None):
            s = b % 2
            so = st[st_i[0] % 3]
            st_i[0] += 1
            for j in range(4):
                bk = nbank()
                for kc in range(8):
                    c.mm(bk[:, 0:w], hT[s][:, kc, 2 + 128 * j:2 + 128 * (j + 1)], t[:, kc, 0:w], kc == 0, kc == 7,
                         R=[hT[s], t], W=[bk], inc=(kc == 7))
                if func is None:
                    c.cp("dve", so[:, j, 0:w], bk[:, 0:w], R=[bk], W=[so])
                else:
                    c.act(so[:, j, 0:w], bk[:, 0:w], func, R=[bk], W=[so], scale=scale)
            c.dma(dst[b * 512:(b + 1) * 512, dcol:dcol + w].rearrange("(j p) c -> p j c", p=128), so[:, :, 0:w], R=[so], ds=so.ds)

        def fm_store(b, src, dst):
            c.dma(dst[b, :, :].rearrange("(cc p) t -> p cc t", p=128), src[:, :, :], R=[src], ds=src.ds)

        def tm_from_fm(b, src, dst, dcol):
            so = st[st_i[0] % 3]
            st_i[0] += 1
            for j in range(4):
                bk = nbank()
                pv = bk[:, :].bitcast(BF16).rearrange("p (a b) -> p a b", a=8)
                for cc in range(4):
                    c.tr(pv[:, cc, :], src[:, cc, j * 128:(j + 1) * 128], ident[:, :], R=[src, ident], W=[bk], inc=(cc == 3))
                c.cp("dve" if j % 2 == 0 else "act", so[:, j, :].rearrange("p (a b) -> p a b", a=4), pv[:, 0:4, :], R=[bk], W=[so])
            c.dma(dst[b * 512:(b + 1) * 512, dcol:dcol + 512].rearrange("(j p) c -> p j c", p=128), so[:, :, :], R=[so], ds=so.ds)

        def xbc_group(t, b, g):
            s = b % 2
            o = xo[g % 2]
            for cc in range(4):
                ch = g * 4 + cc
                bks = [nbank(), nbank()]
                for hf in range(2):
                    for kc in range(8):
                        c.mm(bks[hf][:, 0:260], t[:, kc, cc * 128:(cc + 1) * 128], hT[s][:, kc, hf * 256:hf * 256 + 260], kc == 0, kc == 7,
                             R=[hT[s], t], W=[bks[hf]], inc=(kc == 7))
                xi = xin[ch % 2]
                c.cp("act", xi[:, 0, :], bks[0][:, 0:260], R=[bks[0]], W=[xi])
                c.cp("dve", xi[:, 1, :], bks[1][:, 0:260], R=[bks[1]], W=[xi])
                bk2 = nbank()
                for hf in range(2):
                    for k in range(5):
                        c.mm(bk2[:, hf * 256:(hf + 1) * 256], dg[:, ch * 5 + k, :], xi[:, hf, k:k + 256], k == 0, k == 4,
                             R=[dg, xi], W=[bk2], inc=(k == 4 and hf == 1))
                c.act(o[:, cc, :], bk2[:, :], AF.Silu, R=[bk2, cb], W=[o], bias=cb[:, ch:ch + 1])
            if g < 4:
                tm_from_fm(b, o, D["xs"], g * 512)
            elif g == 4:
                fm_store(b, o, D["BT"])
                tm_from_fm(b, o, D["Bm"], 0)
            else:
                fm_store(b, o, D["CT"])

        def fm_group(t, b, c0, dst, scale):
            s = b % 2
            o = xo[0] if dst is D["qT"] else xo[1]
            for cc in range(4):
                bk = nbank()
                for kc in range(8):
                    c.mm(bk[:, :], t[:, kc, cc * 128:(cc + 1) * 128], hT[s][:, kc, 2:514], kc == 0, kc == 7, R=[hT[s], t], W=[bk], inc=(kc == 7))
                c.act(o[:, cc, :], bk[:, :], AF.Copy, R=[bk], W=[o], scale=scale)
            fm_store(b, o, dst)
            if dst is D["kT"]:
                tm_from_fm(b, o, D["km"], 0)

        def gk_group(t, b):
            s = b % 2
            for z in range(2):
                bk = nbank()
                for kc in range(8):
                    c.mm(bk[0:16, :], t[:, kc, z * 16:(z + 1) * 16], hT[s][:, kc, 2:514], kc == 0, kc == 7, R=[hT[s], t], W=[bk], inc=(kc == 7))
                c.cp("dve", gkT[z][0:16, :], bk[0:16, :], R=[bk], W=[gkT[z]])
            for j in range(4):
                l = lat[j % 2]
                e = et[j % 2]
                for z in range(2):
                    bk = nbank()
                    c.mm(bk[:, :], gkT[z][0:33, j * 128:(j + 1) * 128], w2a[0:33, z * 512:(z + 1) * 512], True, True, R=[gkT[z], w2a], W=[bk], inc=True)
                    c.act(e[:, z * 512:(z + 1) * 512], bk[:, :], AF.Exp, R=[bk], W=[e], scale=-1.0)
                c.act(l[:, :], e[:, :], AF.Ln, R=[e], W=[l], bias=1.0)
                c.dma(D["la"][b * 512 + j * 128:b * 512 + (j + 1) * 128, :], l[:, :], R=[l], ds=l.ds)

        def dt_group(t, b):
            s = b % 2
            for j in range(4):
                bk = nbank()
                for kc in range(8):
                    c.mm(bk[:, 0:64], hT[s][:, kc, 2 + 128 * j:2 + 128 * (j + 1)], t[:, kc, 0:64], kc == 0, kc == 7, R=[hT[s], t], W=[bk], inc=(kc == 7))
                c.tt("dve", dte[:, j, :], bk[:, 0:64], dtb[:, :], ALU.add, R=[bk, dtb], W=[dte])
            c.act(dte[:, :, :], dte[:, :, :], AF.Exp, R=[dte], W=[dte])
            c.act(dtt[:, :, :], dte[:, :, :], AF.Ln, R=[dte], W=[dtt], bias=1.0)
            c.dma(D["dts"][b * 512:(b + 1) * 512, :].rearrange("(j p) c -> p j c", p=128), dtt[:, :, :], R=[dtt], ds=dtt.ds)

        seq = []
        for b in range(NB):
            if b + 1 < NB:
                seq.append((None, lambda t, b=b: load_x(b + 1)))
            for g in range(4):
                seq.append(((O_Z + g * 512, 512), lambda t, b=b, g=g: tok_group(t, b, O_Z + g * 512, 512, AF.Silu, D["zs"], g * 512)))
            if b + 1 < NB:
                seq.append((None, lambda t, b=b: prepA(b + 1)))
            seq.append(((O_GK, 32), lambda t, b=b: gk_group(t, b)))
            seq.append(((O_DT, 64), lambda t, b=b: dt_group(t, b)))
            if b + 1 < NB:
                seq.append((None, lambda t, b=b: prepB(b + 1)))
            for g in range(2):
                seq.append(((O_G + g * 512, 512), lambda t, b=b, g=g: tok_group(t, b, O_G + g * 512, 512, AF.Silu, D["gs"], g * 512)))
            for g in range(6):
                seq.append(((O_XBC + g * 512, 512), lambda t, b=b, g=g: xbc_group(t, b, g)))
            for g in range(4):
                seq.append(((O_GT + g * 512, 512), lambda t, b=b, g=g: tok_group(t, b, O_GT + g * 512, 512, AF.Tanh, D["gts"], g * 512, scale=0.5)))
            for g in range(2):
                seq.append(((O_V + g * 512, 512), lambda t, b=b, g=g: tok_group(t, b, O_V + g * 512, 512, None, D["vs"], g * 512)))
            seq.append(((O_Q, 512), lambda t, b=b: fm_group(t, b, O_Q, D["qT"], 128.0 ** -0.5)))
            seq.append(((O_K, 512), lambda t, b=b: fm_group(t, b, O_K, D["kT"], 1.0)))
        loaded = {}

        def ensure(i):
            if i < len(seq) and seq[i][0] is not None and i not in loaded:
                loaded[i] = load_w(*seq[i][0])

        load_x(0)
        if getattr(c, "ds_win", None) is not None:
            c.eng["sp"].wait_ge(c.ds_win.h, c.ds_win.count)
            c.seen["sp"][c.ds_win.key] = c.ds_win.count
        prepA(0)
        prepB(0)
        for i, (wl, fn) in enumerate(seq):
            ensure(i)
            ensure(i + 1)
            ensure(i + 2)
            fn(loaded.pop(i) if wl is not None else None)
    c.end_phase()


def scan_phase(c, D, cfg, d):
    NT = cfg.NT
    TPS = cfg.SEG // 128
    fwd = d == "f"
    z = 0 if fwd else 1
    with ExitStack() as es:
        S = lambda n, sh, dt, dma=False: c.tile(es, n, sh, dt, dma)
        MA = load_cst(c, es, D, "A" + d, BF16)
        MAs = load_cst(c, es, D, "As" + d, BF16)
        MC = load_cst(c, es, D, "C" + d, F32)
        MCs = load_cst(c, es, D, "Cs" + d, BF16)
        ones = load_cst(c, es, D, "ch0", BF16)
        alog = load_cst(c, es, D, "alog")
        flags = load_cst(c, es, D, "flags")
        MCb = S("MCb", [128, 128], BF16)
        c.cp("pool", MCb[:, :], MC[:, :], R=[MC], W=[MCb])
        negA = S("negA", [128, 32], F32)
        c.act(negA[:, :], alog[:, z * 32:(z + 1) * 32], AF.Exp, R=[alog], W=[negA])
        c.ts("dve", negA[:, :], negA[:, :], -1.0, None, ALU.mult, None, R=[negA], W=[negA])
        identf = load_cst(c, es, D, "ident")
        ident = S("identb", [128, 128], BF16)
        c.cp("pool", ident[:, :], identf[:, :], R=[identf], W=[ident])
        if fwd:
            dD = load_cst(c, es, D, "dD")
            Dg = S("Dg", [128, 32, 128], BF16)
            for h in range(32):
                c.ts("dve", Dg[:, h, :], identf[:, :], dD[:, h:h + 1], None, ALU.mult, None, R=[identf, dD], W=[Dg])
        NS = 3
        xs = [S("xs%d" % i, [128, 2048], BF16, True) for i in range(NS)]
        Bm = [S("Bm%d" % i, [128, 512], BF16, True) for i in range(NS)]
        km = [S("km%d" % i, [128, 512], BF16, True) for i in range(NS)]
        vv = [S("vv%d" % i, [128, 1024], BF16, True) for i in range(NS)]
        la = [S("la%d" % i, [128, 512], BF16, True) for i in range(NS)]
        dt = [S("dt%d" % i, [128, 32], F32, True) for i in range(NS)]
        BT = [S("BT%d" % i, [128, 4, 128], BF16, True) for i in range(NS)]
        CT = [S("CT%d" % i, [128, 4, 128], BF16, True) for i in range(NS)]
        qT = [S("qT%d" % i, [128, 4, 128], BF16, True) for i in range(NS)]
        kT = [S("kT%d" % i, [128, 4, 128], BF16, True) for i in range(NS)]
        if fwd:
            ybl = [S("ybl%d" % i, [128, 2048], BF16, True) for i in range(2)]
            obl = [S("obl%d" % i, [128, 1024], BF16, True) for i in range(2)]
            yn = [S("yn%d" % i, [128, 2048], BF16, True) for i in range(2)]
            on = [S("on%d" % i, [128, 1024], BF16, True) for i in range(2)]
        else:
            ybs = [S("ybs%d" % i, [128, 2048], BF16, True) for i in range(2)]
            obs = [S("obs%d" % i, [128, 1024], BF16, True) for i in range(2)]
        abf = [S("abf%d" % i, [128, 32], BF16) for i in range(2)]
        a32 = [S("a32_%d" % i, [128, 32], F32) for i in range(2)]
        ty16 = [S("ty16_%d" % i, [128, 512], BF16) for i in range(2)]
        zer = S("zer", [128, 128], BF16)
        c.op("pool", lambda E: E.memset(zer[:, :], 0.0), W=[zer])
        Eall = [S("Eall%d" % i, [128, 96], F32) for i in range(2)]
        Rt = [S("Rt%d" % i, [128, 32, 128], BF16) for i in range(2)]
        xd = [S("xd%d" % i, [128, 2048], BF16) for i in range(2)]
        xdd = [S("xdd%d" % i, [128, 2048], BF16) for i in range(2)]
        CBm = [S("CBm%d" % i, [128, 4, 128], BF16) for i in range(2)]
        Ep = [S("Ep%d" % i, [128, 512], BF16) for i in range(2)]
        Em = [S("Em%d" % i, [128, 512], BF16) for i in range(2)]
        Er = [S("Er%d" % i, [128, 512], BF16) for i in range(2)]
        gdec = [S("gdec%d" % i, [128, 4], F32) for i in range(2)]
        qin = [S("qin%d" % i, [128, 4, 128], BF16) for i in range(2)]
        kin = [S("kin%d" % i, [128, 4, 128], BF16) for i in range(2)]
        kdec = [S("kdec%d" % i, [128, 512], BF16) for i in range(2)]
        attm = [S("attm%d" % i, [128, 4, 128], BF16) for i in range(2)]
        Eg = [S("Eg%d" % i, [128, 512], BF16) for i in range(3)]
        WT = [S("WT%d" % i, [128, 4, 128], BF16) for i in range(3)]
        tmpy = [S("tmpy%d" % i, [128, 512], F32) for i in range(2)]
        tmps = [S("tmps%d" % i, [128, 512], F32) for i in range(2)]
        h32 = S("h32", [128, 2048], F32)
        hbf = S("hbf", [128, 2048], BF16)
        gh32 = S("gh32", [128, 1024], F32)
        ghbf = S("ghbf", [128, 1024], BF16)
        for tl in (h32, hbf, gh32, ghbf):
            c.op("pool", lambda E, tl=tl: E.memset(tl[:, :], 0.0), W=[tl])
        banks = c.banks
        bk_i = [0]

        def nbank():
            b = banks[bk_i[0] % 8]
            bk_i[0] += 1
            return b

        order = list(range(NT)) if fwd else list(range(NT - 1, -1, -1))
        last = 127 if fwd else 0

        def loadA(it):
            t = order[it]
            s = it % NS
            r = slice(t * 128, (t + 1) * 128)
            c.dma(dt[s][:, :], D["dts"][r, z * 32:(z + 1) * 32], W=[dt[s]], ds=dt[s].ds)
            c.dma(xs[s][:, :], D["xs"][r, :], W=[xs[s]], ds=xs[s].ds)
            c.dma(la[s][:, :], D["la"][r, z * 512:(z + 1) * 512], W=[la[s]], ds=la[s].ds)
            b, j = t // 4, t % 4
            for nm, tl in (("BT", BT[s]), ("CT", CT[s]), ("qT", qT[s]), ("kT", kT[s])):
                c.dma(tl[:, :, :], D[nm][b, :, j * 128:(j + 1) * 128].rearrange("(cc p) t -> p cc t", p=128), W=[tl], ds=tl.ds)
            c.dma(km[s][:, :], D["km"][r, :], W=[km[s]], ds=km[s].ds)
            c.dma(Bm[s][:, :], D["Bm"][r, :], W=[Bm[s]], ds=Bm[s].ds)
            c.dma(vv[s][:, :], D["vs"][r, :], W=[vv[s]], ds=vv[s].ds)

        def loadB(it):
            t = order[it]
            s2 = it % 2
            r = slice(t * 128, (t + 1) * 128)
            c.dma(ybl[s2][:, :], D["yb"][r, :], W=[ybl[s2]], ds=ybl[s2].ds)
            c.dma(obl[s2][:, :], D["ob"][r, :], W=[obl[s2]], ds=obl[s2].ds)

        def bc(ap, n, m):
            return ap.unsqueeze(2).to_broadcast([128, n, m])

        BY = [banks[0], banks[1]]
        BSEG = [banks[2], banks[3]]
        BY2, BS = banks[4], banks[5]
        BM = [banks[6], banks[7]]

        def stageA(it):
            s = it % NS
            u = it % 2
            xs3 = xs[s][:, :].rearrange("p (h q) -> p h q", h=32)
            xd3 = xd[u][:, :].rearrange("p (h q) -> p h q", h=32)
            xdd3 = xdd[u][:, :].rearrange("p (h q) -> p h q", h=32)

            def rpart(h0, h1):
                for h in range(h0, h1):
                    c.act(Rt[u][:, h, :], MCb[:, :], AF.Identity, R=[MCb, a32[u]], W=[Rt[u]], scale=a32[u][:, h:h + 1])

            def p0():
                c.tt("dve", a32[u][:, :], dt[s][:, :], negA[:, :], ALU.mult, R=[dt[s], negA], W=[a32[u]])
                c.cp("dve", abf[u][:, :], a32[u][:, :], R=[a32[u]], W=[abf[u]])
                b1 = BM[0]
                c.mm(b1[:, 0:32], MCb[:, :], abf[u][:, :], True, True, R=[MCb, abf[u]], W=[b1])
                c.mm(b1[:, 32:64], MA[:, :], abf[u][:, :], True, True, R=[MA, abf[u]], W=[b1])
                c.mm(b1[:, 64:96], ones[:, :], abf[u][:, :], True, True, R=[ones, abf[u]], W=[b1], inc=True)
                c.act(Eall[u][:, :], b1[:, 0:96], AF.Exp, R=[b1], W=[Eall[u]])
                rpart(0, 8)

            def p1():
                rpart(8, 16)
                rpart(16, 24)

            def p2():
                rpart(24, 32)
                bcb = BM[1]
                for g in range(4):
                    c.mm(bcb[:, g * 128:(g + 1) * 128], BT[s][:, g, :], CT[s][:, g, :], True, True, R=[BT[s], CT[s]], W=[bcb], inc=(g == 3))
                c.tt("dve", CBm[u][:, :, :], bcb[:, :].rearrange("p (g i) -> p g i", g=4), MC[:, :].unsqueeze(1).to_broadcast([128, 4, 128]), ALU.mult,
                     R=[bcb, MC], W=[CBm[u]])

            def p3():
                for h0 in (0, 16):
                    c.tt("dve", xd3[:, h0:h0 + 16, :], xs3[:, h0:h0 + 16, :], bc(dt[s][:, h0:h0 + 16], 16, 64), ALU.mult, R=[xs[s], dt[s]], W=[xd[u]])

            def p4():
                for h0 in (0, 16):
                    c.tt("dve", xdd3[:, h0:h0 + 16, :], xd3[:, h0:h0 + 16, :], bc(Eall[u][:, 32 + h0:32 + h0 + 16], 16, 64), ALU.mult,
                         R=[xd[u], Eall[u]], W=[xdd[u]])

            def p5():
                bb = BM[0]
                for hd in range(4):
                    c.mm(bb[:, hd * 128:(hd + 1) * 128], la[s][:, hd * 128:(hd + 1) * 128], MCs[:, :], True, True, R=[la[s], MCs], W=[bb], inc=(hd == 3))
                br = BM[1]
                c.mm(br[:, :], MAs[:, :], la[s][:, :], True, True, R=[MAs, la[s]], W=[br], inc=True)
                c.act(Ep[u][:, :], bb[:, :], AF.Exp, R=[bb], W=[Ep[u]])
                c.act(Em[u][:, :], bb[:, :], AF.Exp, R=[bb], W=[Em[u]], scale=-1.0)
                c.act(gdec[u][:, :].rearrange("p (a b) -> p a b", b=1), bb[:, :].rearrange("p (a b) -> p a b", a=4)[:, :, last:last + 1], AF.Exp,
                      R=[bb], W=[gdec[u]])
                c.act(Er[u][:, :], br[:, :], AF.Exp, R=[br], W=[Er[u]])

            def p6():
                c.tt("dve", qin[u][:, :, :], qT[s][:, :, :], Ep[u][:, :].rearrange("p (a b) -> p a b", a=4), ALU.mult, R=[qT[s], Ep[u]], W=[qin[u]])
                c.tt("dve", kin[u][:, :, :], kT[s][:, :, :], Em[u][:, :].rearrange("p (a b) -> p a b", a=4), ALU.mult, R=[kT[s], Em[u]], W=[kin[u]])
                c.tt("dve", kdec[u][:, :], km[s][:, :], Er[u][:, :], ALU.mult, R=[km[s], Er[u]], W=[kdec[u]])

            def p7():
                ba = BM[0]
                for hd in range(4):
                    c.mm(ba[:, hd * 128:(hd + 1) * 128], kin[u][:, hd, :], qin[u][:, hd, :], True, True, R=[kin[u], qin[u]], W=[ba], inc=(hd == 3))
                c.tt("dve", attm[u][:, :, :], ba[:, :].rearrange("p (a b) -> p a b", a=4), MC[:, :].unsqueeze(1).to_broadcast([128, 4, 128]), ALU.mult,
                     R=[ba, MC], W=[attm[u]])

            return [p0, p1, p2, p3, p4, p5, p6, p7]

        def seg_mm(u, sg):
            bk = BSEG[sg % 2]
            c.mm(bk[:, :], MA[:, :], Rt[u][:, sg * 4:(sg + 1) * 4, :].rearrange("p a b -> p (a b)"), True, True, R=[MA, Rt[u]], W=[bk], inc=True)
            e = Eg[sg % 3]
            c.act(e[:, :], bk[:, :], AF.Exp, R=[bk], W=[e])

        def stageB(it, fill):
            t = order[it]
            s = it % NS
            u = it % 2
            s2 = it % 2
            bd = None
            if fwd and t % TPS == 0 and t != 0:
                bd = t // TPS
            if (not fwd) and (t + 1) % TPS == 0 and t != NT - 1:
                bd = (t + 1) // TPS
            if bd is not None:
                c.ts("dve", h32[:, :], h32[:, :], flags[:, bd:bd + 1], None, ALU.mult, None, R=[h32, flags], W=[h32])
                c.cp("act", hbf[:, :], h32[:, :], R=[h32], W=[hbf])
                c.ts("dve", gh32[:, :], gh32[:, :], flags[:, bd:bd + 1], None, ALU.mult, None, R=[gh32, flags], W=[gh32])
                c.cp("act", ghbf[:, :], gh32[:, :], R=[gh32], W=[ghbf])
            yo = yn[s2] if fwd else ybs[s2]

            def tail(g):
                by = BY[g % 2]
                c.mm(BY2[:, :], CT[s][:, g, :], hbf[:, g * 512:(g + 1) * 512], True, True, R=[CT[s], hbf], W=[BY2], inc=True)
                c.mm(BS[:, :], Bm[s][:, g * 128:(g + 1) * 128], xdd[u][:, g * 512:(g + 1) * 512], True, True, R=[Bm[s], xdd[u]], W=[BS], inc=True)
                ty = ty16[g % 2]
                c.tt("dve", ty[:, :].rearrange("p (h q) -> p h q", h=8), BY2[:, :].rearrange("p (h q) -> p h q", h=8),
                     bc(Eall[u][:, g * 8:(g + 1) * 8], 8, 64), ALU.mult, R=[BY2, Eall[u]], W=[ty])
                c.mm(by[:, :], ident[:, :], ty[:, :], False, True, R=[ident, ty], W=[by], inc=True)
                c.cp("act", yo[:, g * 512:(g + 1) * 512], by[:, :], R=[by], W=[yo])
                tsb = tmps[g % 2]
                c.tt("pool", tsb[:, :].rearrange("p (h q) -> p h q", h=8), h32[:, g * 512:(g + 1) * 512].rearrange("p (h q) -> p h q", h=8),
                     bc(Eall[u][:, 64 + g * 8:64 + (g + 1) * 8], 8, 64), ALU.mult, R=[h32, Eall[u]], W=[tsb])
                c.tt("dve", h32[:, g * 512:(g + 1) * 512], tsb[:, :], BS[:, :], ALU.add, R=[tsb, BS], W=[h32])
                c.cp("act", hbf[:, g * 512:(g + 1) * 512], h32[:, g * 512:(g + 1) * 512], R=[h32], W=[hbf])

            seg_mm(u, 0)
            seg_mm(u, 1)
            for g in range(4):
                by = BY[g % 2]
                for sg in (2 * g, 2 * g + 1):
                    w = WT[sg % 3]
                    c.tt("dve", w[:, :, :], Eg[sg % 3][:, :].rearrange("p (a b) -> p a b", a=4), CBm[u][:, g:g + 1, :].to_broadcast([128, 4, 128]), ALU.mult,
                         R=[Eg[sg % 3], CBm[u]], W=[w])
                    if sg % 2 == 0:
                        if fwd:
                            c.mm(by[:, :], ident[:, :], ybl[s2][:, g * 512:(g + 1) * 512], True, False, R=[ident, ybl[s2]], W=[by])
                        else:
                            c.mm(by[:, :], zer[:, :], xd[u][:, g * 512:(g + 1) * 512], True, False, R=[zer, xd[u]], W=[by])
                    for e in range(4):
                        h = sg * 4 + e
                        o_ap = by[:, (h % 8) * 64:(h % 8) * 64 + 64]
                        if fwd:
                            c.mm(o_ap, w[:, e, :], xd[u][:, h * 64:(h + 1) * 64], False, False, R=[w, xd[u]], W=[by])
                            c.mm(o_ap, Dg[:, h, :], xs[s][:, h * 64:(h + 1) * 64], False, False, R=[Dg, xs[s]], W=[by], inc=(e == 3))
                        else:
                            c.mm(o_ap, w[:, e, :], xd[u][:, h * 64:(h + 1) * 64], False, False, R=[w, xd[u]], W=[by], inc=(e == 3))
                    if sg + 2 < 8:
                        seg_mm(u, sg + 2)
                    if fill:
                        fill[sg]()
                if g >= 1:
                    tail(g - 1)
            tail(3)
            c.dma(D["yn" if fwd else "yb"][t * 128:(t + 1) * 128, :], yo[:, :], R=[yo], ds=yo.ds)
            bo = [BM[0], BM[1]]
            for hd in range(4):
                o_ap = bo[hd // 2][:, (hd % 2) * 256:(hd % 2) * 256 + 256]
                c.mm(o_ap, attm[u][:, hd, :], vv[s][:, hd * 256:(hd + 1) * 256], True, False, R=[attm[u], vv[s]], W=[bo[hd // 2]])
                c.mm(o_ap, qin[u][:, hd, :], ghbf[:, hd * 256:(hd + 1) * 256], False, True, R=[qin[u], ghbf], W=[bo[hd // 2]], inc=True)
            bg = [BY2, BS]
            for hd in range(4):
                c.mm(bg[hd // 2][:, (hd % 2) * 256:(hd % 2) * 256 + 256], kdec[u][:, hd * 128:(hd + 1) * 128], vv[s][:, hd * 256:(hd + 1) * 256], True, True,
                     R=[kdec[u], vv[s]], W=[bg[hd // 2]], inc=True)
            for hd in range(4):
                c.stt("dve", gh32[:, hd * 256:(hd + 1) * 256], gh32[:, hd * 256:(hd + 1) * 256], gdec[u][:, hd:hd + 1],
                      bg[hd // 2][:, (hd % 2) * 256:(hd % 2) * 256 + 256], ALU.mult, ALU.add, R=[gh32, gdec[u], bg[hd // 2]], W=[gh32])
            if fwd:
                for hf in range(2):
                    c.tt("dve", on[s2][:, hf * 512:(hf + 1) * 512], bo[hf][:, :], obl[s2][:, hf * 512:(hf + 1) * 512], ALU.add, R=[bo[hf], obl[s2]], W=[on[s2]])
                c.dma(D["on"][t * 128:(t + 1) * 128, :], on[s2][:, :], R=[on[s2]], ds=on[s2].ds)
            else:
                for hf in range(2):
                    c.cp("act", obs[s2][:, hf * 512:(hf + 1) * 512], bo[hf][:, :], R=[bo[hf]], W=[obs[s2]])
                c.dma(D["ob"][t * 128:(t + 1) * 128, :], obs[s2][:, :], R=[obs[s2]], ds=obs[s2].ds)
            c.cp("act", ghbf[:, :], gh32[:, :], R=[gh32], W=[ghbf])

        loadA(0)
        if NT > 1:
            loadA(1)
        if fwd:
            loadB(0)
        for f in stageA(0):
            f()
        for it in range(NT):
            if it + 2 < NT:
                loadA(it + 2)
            if fwd and it + 1 < NT:
                loadB(it + 1)
            stageB(it, stageA(it + 1) if it + 1 < NT else None)
    c.end_phase()


def load_w_res(c, es, D, name, rows, cols):
    kc = rows // 128
    t = c.tile(es, "r_" + name, [128, kc, cols], BF16, dma=True)
    step = max(1, 8192 // cols)
    for k0 in range(0, kc, step):
        k1 = min(kc, k0 + step)
        c.dma(t[:, k0:k1, :], D[name + "_b"][k0 * 128:k1 * 128, :].rearrange("(kc p) c -> p kc c", p=128), W=[t], ds=t.ds)
    return t


def phase_d1(c, D, cfg):
    NT = cfg.NT
    with ExitStack() as es:
        S = lambda n, sh, dt, dma=False: c.tile(es, n, sh, dt, dma)
        ident = load_cst(c, es, D, "ident", BF16)
        n2w = load_cst(c, es, D, "n2w")
        snw = load_cst(c, es, D, "snw")
        gnw = load_cst(c, es, D, "gnw")
        wso = load_w_res(c, es, D, "w_ssd_out", 2048, 1024)
        wgo = load_w_res(c, es, D, "w_gla_out", 1024, 1024)
        wo = load_w_res(c, es, D, "w_o", 1024, 1024)
        yn = [S("yn%d" % i, [128, 2048], BF16, True) for i in range(2)]
        on = [S("on%d" % i, [128, 1024], BF16, True) for i in range(2)]
        gt = [S("gt%d" % i, [128, 2048], BF16, True) for i in range(3)]
        xt = [S("xt%d" % i, [128, 1024], F32, True) for i in range(3)]
        zs = [S("zs%d" % i, [128, 2048], BF16, True) for i in range(2)]
        gs = [S("gs%d" % i, [128, 1024], BF16, True) for i in range(2)]
        y32 = S("y32", [128, 2048], F32)
        o32 = S("o32", [128, 1024], F32)
        ynb2 = [S("ynb%d" % i, [128, 2048], BF16) for i in range(2)]
        onb2 = [S("onb%d" % i, [128, 1024], BF16) for i in range(2)]
        ssq1 = S("ssq1", [128, 8], F32)
        rs1 = S("rs1", [128, 8], F32)
        ynT = S("ynT", [128, 16, 128], BF16)
        onT = S("onT", [128, 8, 128], BF16)
        m1 = S("m1", [128, 1024], F32)
        m2 = S("m2", [128, 1024], F32)
        mx = [S("mx%d" % i, [128, 1024], BF16) for i in range(2)]
        mT = S("mT", [128, 8, 128], BF16)
        x1 = [S("x1_%d" % i, [128, 1024], F32, True) for i in range(2)]
        h2 = S("h2", [128, 1024], BF16)
        h2T = [S("h2T%d" % i, [128, 8, 128], BF16, True) for i in range(2)]
        junk = S("junk", [128, 1024], BF16)
        ssq = S("ssq", [128, 2], F32)
        rs = S("rs", [128, 2], F32)
        banks = c.banks
        bk_i = [0]

        def nbank():
            b = banks[bk_i[0] % 8]
            bk_i[0] += 1
            return b

        def load0(t):
            s = t % 2
            r = slice(t * 128, (t + 1) * 128)
            c.dma(yn[s][:, :], D["yn"][r, :], W=[yn[s]], ds=yn[s].ds)
            c.dma(zs[s][:, :], D["zs"][r, :], W=[zs[s]], ds=zs[s].ds)
            c.dma(on[s][:, :], D["on"][r, :], W=[on[s]], ds=on[s].ds)
            c.dma(gs[s][:, :], D["gs"][r, :], W=[gs[s]], ds=gs[s].ds)

        def load12(t):
            s = t % 3
            r = slice(t * 128, (t + 1) * 128)
            c.dma(gt[s][:, :], D["gts"][r, :], W=[gt[s]], ds=gt[s].ds)
            c.dma(xt[s][:, :], D["xm"][r, :], W=[xt[s]], ds=xt[s].ds)

        def transp(src, ncol, dst, engs):
            for b0 in range(0, ncol, 8):
                bk = nbank()
                pv = bk[:, :].bitcast(BF16).rearrange("p (a b) -> p a b", a=8)
                for k in range(8):
                    c.tr(pv[:, k, :], src[:, (b0 + k) * 128:(b0 + k + 1) * 128], ident[:, :], R=[src, ident], W=[bk], inc=(k == 7))
                c.cp(engs[(b0 // 8) % len(engs)], dst[:, b0:b0 + 8, :], pv[:, :, :], R=[bk], W=[dst])

        def stage0(t):
            s = t % 2
            ynb = ynb2[t % 2]
            onb = onb2[t % 2]
            c.tt("dve", y32[:, :], yn[s][:, :], zs[s][:, :], ALU.mult, R=[yn[s], zs[s]], W=[y32])
            for g in range(4):
                c.act(junk[:, 0:512], y32[:, g * 512:(g + 1) * 512], AF.Square, R=[y32], W=[junk, ssq1], accum_out=ssq1[:, g:g + 1])
            c.act(rs1[:, 0:4], ssq1[:, 0:4], AF.Ln, R=[ssq1], W=[rs1], scale=1.0 / 512, bias=EPS)
            c.act(rs1[:, 0:4], rs1[:, 0:4], AF.Exp, R=[rs1], W=[rs1], scale=-0.5)
            for g in range(4):
                c.stt("dve", ynb[:, g * 512:(g + 1) * 512], y32[:, g * 512:(g + 1) * 512], rs1[:, g:g + 1], snw[:, g * 512:(g + 1) * 512],
                      ALU.mult, ALU.mult, R=[y32, rs1, snw], W=[ynb])
            for hd in range(4):
                c.act(junk[:, 0:256], on[s][:, hd * 256:(hd + 1) * 256], AF.Square, R=[on[s]], W=[junk, ssq1], accum_out=ssq1[:, 4 + hd:5 + hd])
            c.act(rs1[:, 4:8], ssq1[:, 4:8], AF.Ln, R=[ssq1], W=[rs1], scale=1.0 / 256, bias=EPS)
            c.act(rs1[:, 4:8], rs1[:, 4:8], AF.Exp, R=[rs1], W=[rs1], scale=-0.5)
            for hd in range(4):
                c.stt("dve", o32[:, hd * 256:(hd + 1) * 256], on[s][:, hd * 256:(hd + 1) * 256], rs1[:, 4 + hd:5 + hd], gnw[:, hd * 256:(hd + 1) * 256],
                      ALU.mult, ALU.mult, R=[on[s], rs1, gnw], W=[o32])
            c.tt("dve", onb[:, :], o32[:, :], gs[s][:, :], ALU.mult, R=[o32, gs[s]], W=[onb])

        def stage1a(t):
            transp(ynb2[t % 2], 16, ynT, ["act", "dve"])
            transp(onb2[t % 2], 8, onT, ["act"])

        def stage1b(t):
            s = t % 3
            for hf in range(2):
                ba = nbank()
                for kc in range(16):
                    c.mm(ba[:, :], ynT[:, kc, :], wso[:, kc, hf * 512:(hf + 1) * 512], kc == 0, kc == 15, R=[ynT, wso], W=[ba], inc=(kc == 15))
                bb = nbank()
                for kc in range(8):
                    c.mm(bb[:, :], onT[:, kc, :], wgo[:, kc, hf * 512:(hf + 1) * 512], kc == 0, kc == 7, R=[onT, wgo], W=[bb], inc=(kc == 7))
                cs = slice(hf * 512, (hf + 1) * 512)
                c.stt("dve", m1[:, cs], gt[s][:, hf * 512:(hf + 1) * 512], 1.0, ba[:, :], ALU.add, ALU.mult, R=[gt[s], ba], W=[m1])
                c.stt("dve", m2[:, cs], gt[s][:, 1024 + hf * 512:1024 + (hf + 1) * 512], 1.0, bb[:, :], ALU.add, ALU.mult, R=[gt[s], bb], W=[m2])
                c.tt("pool", mx[t % 2][:, cs], m1[:, cs], m2[:, cs], ALU.add, R=[m1, m2], W=[mx[t % 2]])

        def stage2a(t):
            s = t % 3
            transp(mx[t % 2], 8, mT, ["act"])
            xo = x1[t % 2]
            for hf in range(2):
                bo = nbank()
                for kc in range(8):
                    c.mm(bo[:, :], mT[:, kc, :], wo[:, kc, hf * 512:(hf + 1) * 512], kc == 0, kc == 7, R=[mT, wo], W=[bo], inc=(kc == 7))
                c.stt("dve", xo[:, hf * 512:(hf + 1) * 512], bo[:, :], 0.5, xt[s][:, hf * 512:(hf + 1) * 512], ALU.mult, ALU.add, R=[bo, xt[s]], W=[xo])
            c.dma(D["x1"][t * 128:(t + 1) * 128, :], xo[:, :], R=[xo], ds=xo.ds)
            c.act(junk[:, :], xo[:, :], AF.Square, R=[xo], W=[junk, ssq], accum_out=ssq[:, 0:1])
            c.act(rs[:, 0:1], ssq[:, 0:1], AF.Ln, R=[ssq], W=[rs], scale=1.0 / 1024, bias=EPS)
            c.act(rs[:, 0:1], rs[:, 0:1], AF.Exp, R=[rs], W=[rs], scale=-0.5)
            c.stt("dve", h2[:, :], xo[:, :], rs[:, 0:1], n2w[:, :], ALU.mult, ALU.mult, R=[xo, rs, n2w], W=[h2])

        def stage2b(t):
            transp(h2, 8, h2T[t % 2], ["act"])
            c.dma(D["h2T"][:, t * 128:(t + 1) * 128].rearrange("(kc p) t -> p kc t", p=128), h2T[t % 2][:, :, :], R=[h2T[t % 2]], ds=h2T[t % 2].ds)

        load0(0)
        if NT > 1:
            load0(1)
        load12(0)
        if NT > 1:
            load12(1)
        stage0(0)
        if NT > 2:
            load0(2)
        if NT > 1:
            stage0(1)
        stage1a(0)
        stage1b(0)
        for t in range(NT):
            if t + 3 < NT:
                load0(t + 3)
            if t + 2 < NT:
                load12(t + 2)
            if t + 1 < NT:
                stage1a(t + 1)
            stage2a(t)
            if t + 2 < NT:
                stage0(t + 2)
            if t + 1 < NT:
                stage1b(t + 1)
            stage2b(t)
    c.end_phase()


def phase_ffn(c, D, cfg):
    T = cfg.T
    NBK = T // 256
    SEG = cfg.SEG
    with ExitStack() as es:
        S = lambda n, sh, dt, dma=False: c.tile(es, n, sh, dt, dma)
        fcw = load_cst(c, es, D, "fcw")
        fcb = load_cst(c, es, D, "fcb")
        fnw = load_cst(c, es, D, "fnw")
        flags = load_cst(c, es, D, "flags")
        wup = load_w_res(c, es, D, "w_ffn_up", 1024, 2 * D_FF)
        wdn = load_w_res(c, es, D, "w_ffn_down", D_FF, 1024)
        hT = [S("hT%d" % i, [128, 8, 258], BF16, True) for i in range(2)]
        x1 = S("x1", [128, 2, 1024], F32, True)
        actT = [S("actT%d" % i, [128, 22, 256], BF16) for i in range(2)]
        ag = [S("ag%d" % i, [128, 256], F32) for i in range(2)]
        av = [S("av%d" % i, [128, 256], F32) for i in range(2)]
        sg = [S("sg%d" % i, [128, 256], F32) for i in range(2)]
        x2 = [S("x2_%d" % i, [128, 1024], F32, True) for i in range(2)]
        junk = S("junk", [128, 1024], BF16)
        ssq = S("ssq", [128, 2], F32)
        rs = S("rs", [128, 2], F32)
        banks = c.banks
        bk_i = [0]

        def nbank():
            b = banks[bk_i[0] % 8]
            bk_i[0] += 1
            return b

        def load(kb):
            s = kb % 2
            t0 = kb * 256
            lo = 1 if kb == 0 else 0
            hi = 257 if kb == NBK - 1 else 258
            if kb == 0:
                c.op("pool", lambda E: E.memset(hT[s][:, :, 0:1], 0.0), W=[hT[s]])
            if kb == NBK - 1:
                c.op("pool", lambda E: E.memset(hT[s][:, :, 257:258], 0.0), W=[hT[s]])
            c.dma(hT[s][:, :, lo:hi], D["h2T"][:, t0 - 1 + lo:t0 - 1 + hi].rearrange("(kc p) t -> p kc t", p=128), W=[hT[s]], ds=hT[s].ds)
            if t0 % SEG == 0 and t0 != 0:
                bd = t0 // SEG
                c.ts("dve", hT[s][:, :, 0:1], hT[s][:, :, 0:1], flags[:, bd:bd + 1], None, ALU.mult, None, R=[hT[s], flags], W=[hT[s]])
            if (t0 + 256) % SEG == 0 and t0 + 256 != T:
                bd = (t0 + 256) // SEG
                c.ts("dve", hT[s][:, :, 257:258], hT[s][:, :, 257:258], flags[:, bd:bd + 1], None, ALU.mult, None, R=[hT[s], flags], W=[hT[s]])

        def conv(bk, ch, acc):
            c.act(acc[:, :], bk[:, 0:256], AF.Identity, R=[bk, fcw, fcb], W=[acc], bias=fcb[:, ch:ch + 1], scale=fcw[:, ch * 3:ch * 3 + 1])
            for k in (1, 2):
                c.stt("dve", acc[:, :], bk[:, k:k + 256], fcw[:, ch * 3 + k:ch * 3 + k + 1], acc[:, :], ALU.mult, ALU.add, R=[bk, fcw, acc], W=[acc])

        def up(kb):
            s = kb % 2
            aT = actT[kb % 2]
            for cc in range(22):
                bg = nbank()
                for kc in range(8):
                    c.mm(bg[:, 0:258], wup[:, kc, cc * 128:(cc + 1) * 128], hT[s][:, kc, :], kc == 0, kc == 7, R=[wup, hT[s]], W=[bg], inc=(kc == 7))
                bv = nbank()
                for kc in range(8):
                    c.mm(bv[:, 0:258], wup[:, kc, (cc + 22) * 128:(cc + 23) * 128], hT[s][:, kc, :], kc == 0, kc == 7, R=[wup, hT[s]], W=[bv], inc=(kc == 7))
                conv(bg, cc, ag[cc % 2])
                conv(bv, cc + 22, av[cc % 2])
                c.act(sg[cc % 2][:, :], ag[cc % 2][:, :], AF.Silu, R=[ag[cc % 2]], W=[sg[cc % 2]])
                c.tt("pool", aT[:, cc, :], sg[cc % 2][:, :], av[cc % 2][:, :], ALU.mult, R=[sg[cc % 2], av[cc % 2]], W=[aT])

        def down(kb):
            aT = actT[kb % 2]
            c.dma(x1[:, :, :], D["x1"][kb * 256:(kb + 1) * 256, :].rearrange("(j p) d -> p j d", p=128), W=[x1], ds=x1.ds)
            for j in range(2):
                o = x2[j]
                for hf in range(2):
                    bo = nbank()
                    for cc in range(22):
                        c.mm(bo[:, :], aT[:, cc, j * 128:(j + 1) * 128], wdn[:, cc, hf * 512:(hf + 1) * 512], cc == 0, cc == 21, R=[aT, wdn], W=[bo], inc=(cc == 21))
                    c.tt("dve", o[:, hf * 512:(hf + 1) * 512], bo[:, :], x1[:, j, hf * 512:(hf + 1) * 512], ALU.add, R=[bo, x1], W=[o])
                c.act(junk[:, :], o[:, :], AF.Square, R=[o], W=[junk, ssq], accum_out=ssq[:, 0:1])
                c.act(rs[:, 0:1], ssq[:, 0:1], AF.Ln, R=[ssq], W=[rs], scale=1.0 / 1024, bias=EPS)
                c.act(rs[:, 0:1], rs[:, 0:1], AF.Exp, R=[rs], W=[rs], scale=-0.5)
                c.stt("dve", o[:, :], o[:, :], rs[:, 0:1], fnw[:, :], ALU.mult, ALU.mult, R=[o, rs, fnw], W=[o])
                c.dma(D["y"][kb * 256 + j * 128:kb * 256 + (j + 1) * 128, :], o[:, :], R=[o], ds=o.ds)

        load(0)
        if NBK > 1:
            load(1)
        up(0)
        for kb in range(NBK):
            if kb + 1 < NBK:
                up(kb + 1)
            if kb + 2 < NBK:
                load(kb + 2)
            down(kb)
    c.end_phase()


def plan_cores(nb_prompt=16, nb_sample=2, nseg=4, seg=2048):
    cores = []
    for s in range(nb_sample):
        cores.append(([("s", s, i * seg) for i in range(nseg)], [0, 1, 1, 1, 0]))
    rest = [("p", i, 0) for i in range(nb_prompt)]
    ncore = 8 - nb_sample
    per = [len(rest) // ncore + (1 if i < len(rest) % ncore else 0) for i in range(ncore)]
    k = 0
    for i in range(ncore):
        segs = rest[k:k + per[i]]
        k += per[i]
        segs = segs + [None] * (nseg - len(segs))
        cores.append((segs, [0, 0, 0, 0, 0]))
    return cores


def core_inputs(inp, segs, flags, cfg):
    SEG, NSEG, T, NB = cfg.SEG, cfg.NSEG, cfg.T, cfg.NB
    xm = np.zeros((T, 1024), np.float32)
    for i, sg in enumerate(segs):
        if sg is None:
            continue
        src = inp["x_sample"] if sg[0] == "s" else inp["x_prompt"]
        xm[i * SEG:(i + 1) * SEG] = src[sg[1], sg[2]:sg[2] + SEG]
    xh = np.zeros((NB, 4, 1024), np.float32)
    for b in range(NB):
        t0 = b * 512
        for r, t in enumerate((t0 - 2, t0 - 1, t0 + 512, t0 + 513)):
            if t < 0 or t >= T:
                continue
            sb = t // SEG
            so = t0 // SEG
            if sb != so:
                bd = max(sb, so)
                if flags[bd] == 0:
                    continue
            xh[b, r] = xm[t]
    m = {"xm": xm, "xh": xh.reshape(NB * 4, 1024), "cst": build_consts(inp, list(flags) + [0, 0, 0])}
    for n, r, cc in WEIGHTS:
        m[n] = np.ascontiguousarray(inp[n][0])
    return m


_CACHE = {}


def kernel(**inputs):
    inp = {k: np.asarray(v) for k, v in inputs.items()}
    cfg = Cfg()
    if "nc" not in _CACHE:
        _CACHE["nc"] = build(cfg)[0]
    nc = _CACHE["nc"]
    cores = plan_cores()
    in_maps = [core_inputs(inp, segs, flags, cfg) for segs, flags in cores]
    res = run_bass_kernel_spmd(nc, in_maps, core_ids=list(range(8)))
    yp = np.zeros((16, 2048, 1024), np.float32)
    ys = np.zeros((2, 8192, 1024), np.float32)
    for ci, (segs, flags) in enumerate(cores):
        y = res.results[ci]["y"]
        for i, sg in enumerate(segs):
            if sg is None:
                continue
            blk = y[i * cfg.SEG:(i + 1) * cfg.SEG]
            if sg[0] == "s":
                ys[sg[1], sg[2]:sg[2] + cfg.SEG] = blk
            else:
                yp[sg[1]] = blk
    return (yp, ys)
```

```python
import numpy as np
from contextlib import ExitStack
import concourse.bass as bass
import concourse.mybir as mybir
from concourse.bass_utils import run_bass_kernel_spmd

F32 = mybir.dt.float32
BF16 = mybir.dt.bfloat16
AF = mybir.ActivationFunctionType
ALU = mybir.AluOpType

D_MODEL = 1024
D_IN = 10336
D_FF = 2816
EPS = 1e-6
O_Z, O_XBC, O_DT, O_Q, O_K, O_V, O_G, O_GK, O_GT = 0, 2048, 5120, 5184, 5696, 6208, 7232, 8256, 8288


class Tok:
    __slots__ = ("w", "r")

    def __init__(self):
        self.w = None
        self.r = {}


class DSem:
    def __init__(self, key, h):
        self.key = key
        self.h = h
        self.count = 0


class TL:
    def __init__(self, t, ds=None):
        self.t = t
        self.tok = Tok()
        self.ds = ds

    def __getitem__(self, k):
        return self.t[k]


class Ctx:
    def __init__(self, nc):
        self.nc = nc
        self.eng = {"pe": nc.tensor, "dve": nc.vector, "act": nc.scalar, "pool": nc.gpsimd, "sp": nc.sync}
        self.esem = {e: nc.alloc_semaphore(name="e_" + e) for e in ("pe", "dve", "act", "pool")}
        self.ecnt = {e: 0 for e in self.esem}
        self.seen = {e: {} for e in self.eng}
        self.dsems = {}
        self.free_ds = []
        self.phase_ds = []
        self.nds = 0
        self.ninst = 0

    def new_ds(self):
        if self.free_ds:
            d = self.free_ds.pop()
        else:
            key = "d%d" % self.nds
            self.nds += 1
            d = DSem(key, self.nc.alloc_semaphore(name=key))
            self.dsems[key] = d
        self.phase_ds.append(d)
        return d

    def end_phase(self):
        self.barrier()
        self.free_ds.extend(self.phase_ds)
        self.phase_ds = []

    def tile(self, es, name, shape, dt, dma=False):
        self.ntile = getattr(self, "ntile", 0) + 1
        t = es.enter_context(self.nc.sbuf_tensor("%s_%d" % (name, self.ntile), shape, dt))
        return TL(t, self.new_ds() if dma else None)

    def semh(self, k):
        return self.esem[k] if k in self.esem else self.dsems[k].h

    def cur(self, k):
        return self.ecnt[k] if k in self.ecnt else self.dsems[k].count

    def _waits(self, e, R, W):
        need = {}

        def add(ev, kind):
            if ev is None:
                return
            k, v = ev
            if k == e and (e == "pe" or kind != "raw"):
                return
            if k in self.dsems:
                v = self.dsems[k].count
            if need.get(k, 0) < v:
                need[k] = v

        for t in R:
            add(t.tok.w, "raw")
        for t in W:
            add(t.tok.w, "waw")
            for k, v in t.tok.r.items():
                add((k, v), "war")
        sn = self.seen[e]
        for k, v in need.items():
            if sn.get(k, 0) < v:
                self.eng[e].wait_ge(self.semh(k), v)
                sn[k] = v
                self.ninst += 1

    def op(self, e, fn, R=(), W=(), inc=True):
        self._waits(e, R, W)
        ins = fn(self.eng[e])
        self.ninst += 1
        if inc:
            self.ecnt[e] += 1
            ins.then_inc(self.esem[e], 1)
            v = self.ecnt[e]
        else:
            v = self.ecnt[e] + 1
        for t in R:
            if t.tok.r.get(e, 0) < v:
                t.tok.r[e] = v
        for t in W:
            t.tok.w = (e, v)
            t.tok.r = {}
        return ins

    def dma(self, out, in_, R=(), W=(), ds=None, q="sp", **kw):
        self._waits(q, R, W)
        ins = self.eng[q].dma_start(out=out, in_=in_, **kw)
        ins.then_inc(ds.h, 16)
        ds.count += 16
        self.ninst += 1
        for t in R:
            t.tok.r[ds.key] = ds.count
        for t in W:
            t.tok.w = (ds.key, ds.count)
            t.tok.r = {}
        return ins

    def barrier(self):
        keys = list(self.esem.keys()) + [k for k, d in self.dsems.items() if d.count > 0]
        for e in self.eng:
            sn = self.seen[e]
            for k in keys:
                v = self.cur(k)
                if k == e or v == 0:
                    continue
                if sn.get(k, 0) < v:
                    self.eng[e].wait_ge(self.semh(k), v)
                    sn[k] = v
                    self.ninst += 1

    def mm(self, out, lhsT, rhs, start, stop, R, W, inc=False):
        return self.op("pe", lambda E: E.matmul(out, lhsT=lhsT, rhs=rhs, start=start, stop=stop), R, W, inc)

    def tr(self, out, in_, ident, R, W, inc=False):
        return self.op("pe", lambda E: E.transpose(out, in_, ident), R, W, inc)

    def act(self, out, in_, func, R, W, bias=None, scale=None, accum_out=None):
        kw = {}
        if bias is not None:
            kw["bias"] = bias
        if scale is not None:
            kw["scale"] = scale
        if accum_out is not None:
            kw["accum_out"] = accum_out
        return self.op("act", lambda E: E.activation(out=out, in_=in_, func=func, **kw), R, W)

    def tt(self, e, out, in0, in1, op, R, W):
        return self.op(e, lambda E: E.tensor_tensor(out=out, in0=in0, in1=in1, op=op), R, W)

    def ts(self, e, out, in0, s1, s2, op0, op1, R, W):
        if s2 is None:
            return self.op(e, lambda E: E.tensor_scalar(out=out, in0=in0, scalar1=s1, scalar2=None, op0=op0), R, W)
        return self.op(e, lambda E: E.tensor_scalar(out=out, in0=in0, scalar1=s1, scalar2=s2, op0=op0, op1=op1), R, W)

    def stt(self, e, out, in0, scalar, in1, op0, op1, R, W):
        return self.op(e, lambda E: E.scalar_tensor_tensor(out=out, in0=in0, scalar=scalar, in1=in1, op0=op0, op1=op1), R, W)

    def cp(self, e, out, in_, R, W):
        if e == "act":
            return self.op(e, lambda E: E.copy(out=out, in_=in_), R, W)
        return self.op(e, lambda E: E.tensor_copy(out=out, in_=in_), R, W)


CST_ITEMS = [
    ("ident", 128), ("Af", 128), ("Ab", 128), ("Cf", 128), ("Cb", 128), ("Asf", 128), ("Asb", 128),
    ("Csf", 128), ("Csb", 128), ("Bf", 64), ("Bb", 64), ("ch0", 128), ("ch1", 128),
    ("n1w", 1024), ("n2w", 1024), ("fnw", 1024), ("snw", 2048), ("gnw", 1024), ("dtb", 64), ("alog", 64),
    ("dD", 32), ("cw", 120), ("cb", 24), ("fcw", 132), ("fcb", 44), ("w2a", 1024), ("flags", 8),
]
CST_OFF = {}
_o = 0
for _n, _w in CST_ITEMS:
    CST_OFF[_n] = (_o, _w)
    _o += _w
CST_W = _o


def build_consts(inp, flags):
    c = np.zeros((128, CST_W), np.float32)

    def put(name, arr):
        o, w = CST_OFF[name]
        c[:, o:o + w] = arr

    k = np.arange(128)
    same = np.ones((128, 128), bool)
    Af = ((k[:, None] > k[None, :]) & same).astype(np.float32)
    Ab = ((k[:, None] < k[None, :]) & same).astype(np.float32)
    Cf = ((k[:, None] <= k[None, :]) & same).astype(np.float32)
    Cb = ((k[:, None] >= k[None, :]) & same).astype(np.float32)
    put("ident", np.eye(128, dtype=np.float32))
    put("Af", Af); put("Ab", Ab); put("Cf", Cf); put("Cb", Cb)
    put("Asf", Af * (-1.0 / 16)); put("Asb", Ab * (-1.0 / 16)); put("Csf", Cf * (-1.0 / 16)); put("Csb", Cb * (-1.0 / 16))
    il = np.arange(64)
    put("Bf", ((k[:, None] % 64) <= il[None, :]).astype(np.float32))
    put("Bb", ((k[:, None] % 64) >= il[None, :]).astype(np.float32))
    put("ch0", np.ones((128, 128), np.float32))
    put("ch1", np.repeat((k >= 64).astype(np.float32)[:, None], 128, 1))
    rep = lambda v: np.broadcast_to(np.asarray(v, np.float32).reshape(1, -1), (128, np.asarray(v).size))
    put("n1w", rep(inp["norm1_w"][0])); put("n2w", rep(inp["norm2_w"][0])); put("fnw", rep(inp["final_norm_w"]))
    put("snw", rep(inp["ssd_norm_w"][0])); put("gnw", rep(inp["gla_norm_w"][0]))
    put("dtb", rep(inp["ssd_dt_bias"][0])); put("alog", rep(inp["ssd_a_log"][0])); put("dD", rep(inp["ssd_d"][0]))
    cw = inp["ssd_conv_w"][0]
    put("cw", cw.reshape(5, 24, 128).transpose(2, 1, 0).reshape(128, 120))
    put("cb", inp["ssd_conv_b"][0].reshape(24, 128).T)
    fw = inp["ffn_conv_w"][0]
    put("fcw", fw.reshape(3, 44, 128).transpose(2, 1, 0).reshape(128, 132))
    put("fcb", inp["ffn_conv_b"][0].reshape(44, 128).T)
    w2a = np.zeros((128, 2, 512), np.float32)
    w2a[0:16] = inp["gla_gate_w2"][0].transpose(1, 0, 2)
    w2a[32] = inp["gla_gate_b"][0]
    put("w2a", w2a.reshape(128, 1024))
    put("flags", np.broadcast_to(np.asarray(flags, np.float32).reshape(1, 8), (128, 8)))
    return c


class Cfg:
    def __init__(self, SEG=2048, NSEG=4, debug=False, phases=(0, 1, 2, 3, 4, 5)):
        self.SEG, self.NSEG = SEG, NSEG
        self.T = SEG * NSEG
        self.NB = self.T // 512
        self.NT = self.T // 128
        self.debug = debug
        self.phases = phases


WEIGHTS = [("w_in", 1024, D_IN), ("w_ssd_out", 2048, 1024), ("w_gla_out", 1024, 1024), ("w_o", 1024, 1024),
           ("w_ffn_up", 1024, 2 * D_FF), ("w_ffn_down", D_FF, 1024)]


def build(cfg):
    nc = bass.Bass("TRN2", target_bir_lowering=False)
    T, NB = cfg.T, cfg.NB
    D = {}
    D["xm"] = nc.dram_tensor("xm", [T, 1024], F32, kind="ExternalInput").ap()
    D["xh"] = nc.dram_tensor("xh", [NB * 4, 1024], F32, kind="ExternalInput").ap()
    D["cst"] = nc.dram_tensor("cst", [128, CST_W], F32, kind="ExternalInput").ap()
    for n, r, cc in WEIGHTS:
        D[n] = nc.dram_tensor(n, [r, cc], F32, kind="ExternalInput").ap()
        D[n + "_b"] = nc.dram_tensor(n + "_b", [r, cc], BF16, kind="Internal").ap()
    D["y"] = nc.dram_tensor("y", [T, 1024], F32, kind="ExternalOutput").ap()
    sk = "ExternalOutput" if cfg.debug else "Internal"

    def scr(name, shape, dt):
        D[name] = nc.dram_tensor(name, shape, dt, kind=sk).ap()

    scr("zs", [T, 2048], BF16); scr("gts", [T, 2048], BF16); scr("vs", [T, 1024], BF16); scr("gs", [T, 1024], BF16)
    scr("xs", [T, 2048], BF16); scr("Bm", [T, 512], BF16); scr("km", [T, 512], BF16)
    scr("BT", [NB, 512, 512], BF16); scr("CT", [NB, 512, 512], BF16); scr("qT", [NB, 512, 512], BF16); scr("kT", [NB, 512, 512], BF16)
    scr("la", [T, 1024], BF16); scr("dts", [T, 64], F32)
    scr("yb", [T, 2048], BF16); scr("ob", [T, 1024], BF16)
    scr("yn", [T, 2048], BF16); scr("on", [T, 1024], BF16)
    scr("x1", [T, 1024], F32); scr("h2T", [1024, T], BF16)

    c = Ctx(nc)
    c.banks = [TL(nc.alloc_psum_tensor("bank%d" % i, [128, 512], F32)) for i in range(8)]
    if 0 in cfg.phases:
        phase0(c, D, cfg)
    if 1 in cfg.phases:
        phase1(c, D, cfg)
    if 2 in cfg.phases:
        scan_phase(c, D, cfg, "b")
    if 3 in cfg.phases:
        scan_phase(c, D, cfg, "f")
    if 4 in cfg.phases:
        phase_d1(c, D, cfg)
    if 5 in cfg.phases:
        phase_ffn(c, D, cfg)
    c.barrier()
    return nc, c


def cst_ap(D, name, rows=128):
    o, w = CST_OFF[name]
    return D["cst"][0:rows, o:o + w]


def load_cst(c, es, D, name, dt=F32, rows=128):
    o, w = CST_OFF[name]
    t32 = c.tile(es, "c_" + name, [128, w], F32, dma=True)
    c.dma(t32[0:rows, :], cst_ap(D, name, rows), W=[t32], ds=t32.ds)
    if dt == F32:
        return t32
    tb = c.tile(es, "cb_" + name, [128, w], BF16)
    c.cp("pool", tb[0:rows, :], t32[0:rows, :], R=[t32], W=[tb])
    return tb


def phase0(c, D, cfg):
    ds_in = c.new_ds()
    ds = c.new_ds()
    for n, r, cc in WEIGHTS:
        for r0 in range(0, r, 128):
            c.dma(D[n + "_b"][r0:r0 + 128, :], D[n][r0:r0 + 128, :], ds=(ds_in if n == "w_in" else ds), q="pool")
    c.ds_win = ds_in


def phase1(c, D, cfg):
    nc = c.nc
    NB = cfg.NB
    with ExitStack() as es:
        S = lambda n, sh, dt, dma=False: c.tile(es, n, sh, dt, dma)
        identf = load_cst(c, es, D, "ident")
        ident = S("identb", [128, 128], BF16)
        c.cp("pool", ident[:, :], identf[:, :], R=[identf], W=[ident])
        n1w = load_cst(c, es, D, "n1w")
        cw = load_cst(c, es, D, "cw")
        dg = S("dg", [128, 120, 128], BF16)
        for idx in range(120):
            c.ts("dve", dg[:, idx, :], identf[:, :], cw[:, idx:idx + 1], None, ALU.mult, None, R=[identf, cw], W=[dg])
        xin = [S("xin%d" % i, [128, 2, 260], BF16) for i in range(2)]
        cb = load_cst(c, es, D, "cb")
        dtb = load_cst(c, es, D, "dtb")
        w2a = load_cst(c, es, D, "w2a", BF16, rows=64)
        xt = [S("xt%d" % i, [128, 4, 1024], F32, True) for i in range(2)]
        xh = [S("xh%d" % i, [4, 1024], F32, True) for i in range(2)]
        hb = [S("hb%d" % i, [128, 4, 1024], BF16) for i in range(2)]
        hbh = [S("hbh%d" % i, [4, 1024], BF16) for i in range(2)]
        hT = [S("hT%d" % i, [128, 8, 516], BF16) for i in range(2)]
        wt = [S("wt%d" % i, [128, 8, 512], BF16, True) for i in range(3)]
        st = [S("st%d" % i, [128, 4, 512], BF16, True) for i in range(3)]
        xo = [S("xo%d" % i, [128, 4, 512], BF16, True) for i in range(2)]
        acc = [S("acc%d" % i, [128, 2, 256], F32) for i in range(3)]
        junk = S("junk", [128, 1024], BF16)
        ss = [S("ss%d" % i, [128, 8], F32) for i in range(2)]
        rstd = [S("rstd%d" % i, [128, 8], F32) for i in range(2)]
        gkT = [S("gkT%d" % z, [64, 512], BF16) for z in range(2)]
        lat = [S("lat%d" % i, [128, 1024], BF16, True) for i in range(2)]
        et = [S("et%d" % i, [128, 1024], F32) for i in range(2)]
        dtt = S("dtt", [128, 4, 64], F32, True)
        dte = S("dte", [128, 4, 64], F32)
        banks = c.banks
        st_i = [0]
        wt_i = [0]
        bk_i = [0]

        def nbank():
            b = banks[bk_i[0] % 8]
            bk_i[0] += 1
            return b

        for z in range(2):
            c.op("pool", lambda E, z=z: E.memset(gkT[z][0:64, :], 0.0), W=[gkT[z]])
            c.op("pool", lambda E, z=z: E.memset(gkT[z][32:33, :], 1.0), W=[gkT[z]])

        def load_x(b):
            s = b % 2
            c.dma(xt[s][:, :, :], D["xm"][b * 512:(b + 1) * 512, :].rearrange("(j p) d -> p j d", p=128), W=[xt[s]], ds=xt[s].ds)
            c.dma(xh[s][0:4, :], D["xh"][b * 4:(b + 1) * 4, :], W=[xh[s]], ds=xh[s].ds)

        def prepA(b):
            s = b % 2
            for j in range(4):
                c.act(junk[:, :], xt[s][:, j, :], AF.Square, R=[xt[s]], W=[junk, ss[s]], accum_out=ss[s][:, j:j + 1])
            c.act(junk[0:4, :], xh[s][0:4, :], AF.Square, R=[xh[s]], W=[junk, ss[s]], accum_out=ss[s][0:4, 4:5])
            c.act(rstd[s][:, 0:5], ss[s][:, 0:5], AF.Ln, R=[ss[s]], W=[rstd[s]], scale=1.0 / 1024, bias=EPS)
            c.act(rstd[s][:, 0:5], rstd[s][:, 0:5], AF.Exp, R=[rstd[s]], W=[rstd[s]], scale=-0.5)
            for j in range(4):
                c.stt("dve", hb[s][:, j, :], xt[s][:, j, :], rstd[s][:, j:j + 1], n1w[:, :], ALU.mult, ALU.mult,
                      R=[xt[s], rstd[s], n1w], W=[hb[s]])
            c.stt("dve", hbh[s][0:4, :], xh[s][0:4, :], rstd[s][0:4, 4:5], n1w[0:4, :], ALU.mult, ALU.mult,
                  R=[xh[s], rstd[s], n1w], W=[hbh[s]])

        def prepB(b):
            s = b % 2
            for j in range(4):
                bk = nbank()
                pv = bk[:, :].bitcast(BF16).rearrange("p (a b) -> p a b", a=8)
                for kc in range(8):
                    c.tr(pv[:, kc, :], hb[s][:, j, kc * 128:(kc + 1) * 128], ident[:, :], R=[hb[s], ident], W=[bk], inc=(kc == 7))
                c.cp("dve" if j % 2 == 0 else "act", hT[s][:, :, 2 + 128 * j:2 + 128 * (j + 1)], pv[:, :, :], R=[bk], W=[hT[s]])
            bk = nbank()
            pv = bk[:, :].bitcast(BF16).rearrange("p (a b) -> p a b", a=8)
            for kc in range(8):
                c.tr(pv[:, kc, 0:4], hbh[s][0:4, kc * 128:(kc + 1) * 128], ident[0:4, 0:4], R=[hbh[s], ident], W=[bk], inc=(kc == 7))
            c.cp("dve", hT[s][:, :, 0:2], pv[:, :, 0:2], R=[bk], W=[hT[s]])
            c.cp("dve", hT[s][:, :, 514:516], pv[:, :, 2:4], R=[bk], W=[hT[s]])

        def load_w(c0, w):
            t = wt[wt_i[0] % 3]
            wt_i[0] += 1
            c.dma(t[:, :, 0:w], D["w_in_b"][:, c0:c0 + w].rearrange("(kc p) c -> p kc c", p=128), W=[t], ds=t.ds)
            return t

        def tok_group(t, b, c0, w, func, dst, dcol, scale=None):
            s = b % 2
            so = st[st_i[0] % 3]
            st_i[0] += 1
            for j in range(4):
                bk = nbank()
                for kc in range(8):
                    c.mm(bk[:, 0:w], hT[s][:, kc, 2 + 128 * j:2 + 128 * (j + 1)], t[:, kc, 0:w], kc == 0, kc == 7,
                         R=[hT[s], t], W=[bk], inc=(kc == 7))
                if func is None:
                    c.cp("dve", so[:, j, 0:w], bk[:, 0:w], R=[bk], W=[so])
                else:
                    c.act(so[:, j, 0:w], bk[:, 0:w], func, R=[bk], W=[so], scale=scale)
            c.dma(dst[b * 512:(b + 1) * 512, dcol:dcol + w].rearrange("(j p) c -> p j c", p=128), so[:, :, 0:w], R=[so], ds=so.ds)

        def fm_store(b, src, dst):
            c.dma(dst[b, :, :].rearrange("(cc p) t -> p cc t", p=128), src[:, :, :], R=[src], ds=src.ds)

        def tm_from_fm(b, src, dst, dcol):
            so = st[st_i[0] % 3]
            st_i[0] += 1
            for j in range(4):
                bk = nbank()
                pv = bk[:, :].bitcast(BF16).rearrange("p (a b) -> p a b", a=8)
                for cc in range(4):
                    c.tr(pv[:, cc, :], src[:, cc, j * 128:(j + 1) * 128], ident[:, :], R=[src, ident], W=[bk], inc=(cc == 3))
                c.cp("dve" if j % 2 == 0 else "act", so[:, j, :].rearrange("p (a b) -> p a b", a=4), pv[:, 0:4, :], R=[bk], W=[so])
            c.dma(dst[b * 512:(b + 1) * 512, dcol:dcol + 512].rearrange("(j p) c -> p j c", p=128), so[:, :, :], R=[so], ds=so.ds)

        def xbc_group(t, b, g):
            s = b % 2
            o = xo[g % 2]
            def proj(cc):
                ch = g * 4 + cc
                bks = [nbank(), nbank()]
                for hf in range(2):
                    for kc in range(8):
                        c.mm(bks[hf][:, 0:260], t[:, kc, cc * 128:(cc + 1) * 128], hT[s][:, kc, hf * 256:hf * 256 + 260], kc == 0, kc == 7,
                             R=[hT[s], t], W=[bks[hf]], inc=(kc == 7))
                xi = xin[ch % 2]
                c.cp("act", xi[:, 0, :], bks[0][:, 0:260], R=[bks[0]], W=[xi])
                c.cp("dve", xi[:, 1, :], bks[1][:, 0:260], R=[bks[1]], W=[xi])

            def conv(cc):
                ch = g * 4 + cc
                xi = xin[ch % 2]
                bk2 = nbank()
                for hf in range(2):
                    for k in range(5):
                        c.mm(bk2[:, hf * 256:(hf + 1) * 256], dg[:, ch * 5 + k, :], xi[:, hf, k:k + 256], k == 0, k == 4,
                             R=[dg, xi], W=[bk2], inc=(k == 4 and hf == 1))
                c.act(o[:, cc, :], bk2[:, :], AF.Silu, R=[bk2, cb], W=[o], bias=cb[:, ch:ch + 1])

            proj(0)
            for cc in range(4):
                if cc + 1 < 4:
                    proj(cc + 1)
                conv(cc)
            if g < 4:
                tm_from_fm(b, o, D["xs"], g * 512)
            elif g == 4:
                fm_store(b, o, D["BT"])
                tm_from_fm(b, o, D["Bm"], 0)
            else:
                fm_store(b, o, D["CT"])

        def fm_group(t, b, c0, dst, scale):
            s = b % 2
            o = xo[0] if dst is D["qT"] else xo[1]
            for cc in range(4):
                bk = nbank()
                for kc in range(8):
                    c.mm(bk[:, :], t[:, kc, cc * 128:(cc + 1) * 128], hT[s][:, kc, 2:514], kc == 0, kc == 7, R=[hT[s], t], W=[bk], inc=(kc == 7))
                c.act(o[:, cc, :], bk[:, :], AF.Copy, R=[bk], W=[o], scale=scale)
            fm_store(b, o, dst)
            if dst is D["kT"]:
                tm_from_fm(b, o, D["km"], 0)

        def gk_group(t, b):
            s = b % 2
            for z in range(2):
                bk = nbank()
                for kc in range(8):
                    c.mm(bk[0:16, :], t[:, kc, z * 16:(z + 1) * 16], hT[s][:, kc, 2:514], kc == 0, kc == 7, R=[hT[s], t], W=[bk], inc=(kc == 7))
                c.cp("dve", gkT[z][0:16, :], bk[0:16, :], R=[bk], W=[gkT[z]])
            for j in range(4):
                l = lat[j % 2]
                e = et[j % 2]
                for z in range(2):
                    bk = nbank()
                    c.mm(bk[:, :], gkT[z][0:33, j * 128:(j + 1) * 128], w2a[0:33, z * 512:(z + 1) * 512], True, True, R=[gkT[z], w2a], W=[bk], inc=True)
                    c.act(e[:, z * 512:(z + 1) * 512], bk[:, :], AF.Exp, R=[bk], W=[e], scale=-1.0)
                c.act(l[:, :], e[:, :], AF.Ln, R=[e], W=[l], bias=1.0)
                c.dma(D["la"][b * 512 + j * 128:b * 512 + (j + 1) * 128, :], l[:, :], R=[l], ds=l.ds)

        def dt_group(t, b):
            s = b % 2
            for j in range(4):
                bk = nbank()
                for kc in range(8):
                    c.mm(bk[:, 0:64], hT[s][:, kc, 2 + 128 * j:2 + 128 * (j + 1)], t[:, kc, 0:64], kc == 0, kc == 7, R=[hT[s], t], W=[bk], inc=(kc == 7))
                c.tt("dve", dte[:, j, :], bk[:, 0:64], dtb[:, :], ALU.add, R=[bk, dtb], W=[dte])
            c.act(dte[:, :, :], dte[:, :, :], AF.Exp, R=[dte], W=[dte])
            c.act(dtt[:, :, :], dte[:, :, :], AF.Ln, R=[dte], W=[dtt], bias=1.0)
            c.dma(D["dts"][b * 512:(b + 1) * 512, :].rearrange("(j p) c -> p j c", p=128), dtt[:, :, :], R=[dtt], ds=dtt.ds)

        seq = []
        for b in range(NB):
            if b + 1 < NB:
                seq.append((None, lambda t, b=b: load_x(b + 1)))
            for g in range(4):
                seq.append(((O_Z + g * 512, 512), lambda t, b=b, g=g: tok_group(t, b, O_Z + g * 512, 512, AF.Silu, D["zs"], g * 512)))
            if b + 1 < NB:
                seq.append((None, lambda t, b=b: prepA(b + 1)))
            seq.append(((O_GK, 32), lambda t, b=b: gk_group(t, b)))
            seq.append(((O_DT, 64), lambda t, b=b: dt_group(t, b)))
            if b + 1 < NB:
                seq.append((None, lambda t, b=b: prepB(b + 1)))
            for g in range(2):
                seq.append(((O_G + g * 512, 512), lambda t, b=b, g=g: tok_group(t, b, O_G + g * 512, 512, AF.Silu, D["gs"], g * 512)))
            for g in range(6):
                seq.append(((O_XBC + g * 512, 512), lambda t, b=b, g=g: xbc_group(t, b, g)))
            for g in range(4):
                seq.append(((O_GT + g * 512, 512), lambda t, b=b, g=g: tok_group(t, b, O_GT + g * 512, 512, AF.Tanh, D["gts"], g * 512, scale=0.5)))
            for g in range(2):
                seq.append(((O_V + g * 512, 512), lambda t, b=b, g=g: tok_group(t, b, O_V + g * 512, 512, None, D["vs"], g * 512)))
            seq.append(((O_Q, 512), lambda t, b=b: fm_group(t, b, O_Q, D["qT"], 128.0 ** -0.5)))
            seq.append(((O_K, 512), lambda t, b=b: fm_group(t, b, O_K, D["kT"], 1.0)))
        loaded = {}

        def ensure(i):
            if i < len(seq) and seq[i][0] is not None and i not in loaded:
                loaded[i] = load_w(*seq[i][0])

        load_x(0)
        if getattr(c, "ds_win", None) is not None:
            c.eng["sp"].wait_ge(c.ds_win.h, c.ds_win.count)
            c.seen["sp"][c.ds_win.key] = c.ds_win.count
        prepA(0)
        prepB(0)
        for i, (wl, fn) in enumerate(seq):
            ensure(i)
            ensure(i + 1)
            ensure(i + 2)
            fn(loaded.pop(i) if wl is not None else None)
    c.end_phase()


def scan_phase(c, D, cfg, d):
    NT = cfg.NT
    TPS = cfg.SEG // 128
    fwd = d == "f"
    z = 0 if fwd else 1
    with ExitStack() as es:
        S = lambda n, sh, dt, dma=False: c.tile(es, n, sh, dt, dma)
        MA = load_cst(c, es, D, "A" + d, BF16)
        MAs = load_cst(c, es, D, "As" + d, BF16)
        MC = load_cst(c, es, D, "C" + d, F32)
        MCs = load_cst(c, es, D, "Cs" + d, BF16)
        ones = load_cst(c, es, D, "ch0", BF16)
        alog = load_cst(c, es, D, "alog")
        flags = load_cst(c, es, D, "flags")
        MCb = S("MCb", [128, 128], BF16)
        c.cp("pool", MCb[:, :], MC[:, :], R=[MC], W=[MCb])
        negA = S("negA", [128, 32], F32)
        c.act(negA[:, :], alog[:, z * 32:(z + 1) * 32], AF.Exp, R=[alog], W=[negA])
        c.ts("dve", negA[:, :], negA[:, :], -1.0, None, ALU.mult, None, R=[negA], W=[negA])
        identf = load_cst(c, es, D, "ident")
        ident = S("identb", [128, 128], BF16)
        c.cp("pool", ident[:, :], identf[:, :], R=[identf], W=[ident])
        if fwd:
            dD = load_cst(c, es, D, "dD")
            Dg = S("Dg", [128, 32, 128], BF16)
            for h in range(32):
                c.ts("dve", Dg[:, h, :], identf[:, :], dD[:, h:h + 1], None, ALU.mult, None, R=[identf, dD], W=[Dg])
        NS = 3
        xs = [S("xs%d" % i, [128, 2048], BF16, True) for i in range(NS)]
        Bm = [S("Bm%d" % i, [128, 512], BF16, True) for i in range(NS)]
        km = [S("km%d" % i, [128, 512], BF16, True) for i in range(NS)]
        vv = [S("vv%d" % i, [128, 1024], BF16, True) for i in range(NS)]
        la = [S("la%d" % i, [128, 512], BF16, True) for i in range(NS)]
        dt = [S("dt%d" % i, [128, 32], F32, True) for i in range(NS)]
        BT = [S("BT%d" % i, [128, 4, 128], BF16, True) for i in range(NS)]
        CT = [S("CT%d" % i, [128, 4, 128], BF16, True) for i in range(NS)]
        qT = [S("qT%d" % i, [128, 4, 128], BF16, True) for i in range(NS)]
        kT = [S("kT%d" % i, [128, 4, 128], BF16, True) for i in range(NS)]
        if fwd:
            ybl = [S("ybl%d" % i, [128, 2048], BF16, True) for i in range(2)]
            obl = [S("obl%d" % i, [128, 1024], BF16, True) for i in range(2)]
            yn = [S("yn%d" % i, [128, 2048], BF16, True) for i in range(2)]
            on = [S("on%d" % i, [128, 1024], BF16, True) for i in range(2)]
        else:
            ybs = [S("ybs%d" % i, [128, 2048], BF16, True) for i in range(2)]
            obs = [S("obs%d" % i, [128, 1024], BF16, True) for i in range(2)]
        abf = [S("abf%d" % i, [128, 32], BF16) for i in range(2)]
        a32 = [S("a32_%d" % i, [128, 32], F32) for i in range(2)]
        ty16 = [S("ty16_%d" % i, [128, 512], BF16) for i in range(2)]
        zer = S("zer", [128, 128], BF16)
        c.op("pool", lambda E: E.memset(zer[:, :], 0.0), W=[zer])
        Eall = [S("Eall%d" % i, [128, 96], F32) for i in range(2)]
        Rt = [S("Rt%d" % i, [128, 32, 128], BF16) for i in range(2)]
        xd = [S("xd%d" % i, [128, 2048], BF16) for i in range(2)]
        xdd = [S("xdd%d" % i, [128, 2048], BF16) for i in range(2)]
        CBm = [S("CBm%d" % i, [128, 4, 128], BF16) for i in range(2)]
        Ep = [S("Ep%d" % i, [128, 512], BF16) for i in range(2)]
        Em = [S("Em%d" % i, [128, 512], BF16) for i in range(2)]
        Er = [S("Er%d" % i, [128, 512], BF16) for i in range(2)]
        gdec = [S("gdec%d" % i, [128, 4], F32) for i in range(2)]
        qin = [S("qin%d" % i, [128, 4, 128], BF16) for i in range(2)]
        kin = [S("kin%d" % i, [128, 4, 128], BF16) for i in range(2)]
        kdec = [S("kdec%d" % i, [128, 512], BF16) for i in range(2)]
        attm = [S("attm%d" % i, [128, 4, 128], BF16) for i in range(2)]
        Eg = [S("Eg%d" % i, [128, 512], BF16) for i in range(3)]
        WT = [S("WT%d" % i, [128, 4, 128], BF16) for i in range(3)]
        tmpy = [S("tmpy%d" % i, [128, 512], F32) for i in range(2)]
        tmps = [S("tmps%d" % i, [128, 512], F32) for i in range(2)]
        h32 = S("h32", [128, 2048], F32)
        hbf = S("hbf", [128, 2048], BF16)
        gh32 = S("gh32", [128, 1024], F32)
        ghbf = S("ghbf", [128, 1024], BF16)
        for tl in (h32, hbf, gh32, ghbf):
            c.op("pool", lambda E, tl=tl: E.memset(tl[:, :], 0.0), W=[tl])
        banks = c.banks
        bk_i = [0]

        def nbank():
            b = banks[bk_i[0] % 8]
            bk_i[0] += 1
            return b

        order = list(range(NT)) if fwd else list(range(NT - 1, -1, -1))
        last = 127 if fwd else 0

        def loadA(it):
            t = order[it]
            s = it % NS
            r = slice(t * 128, (t + 1) * 128)
            c.dma(dt[s][:, :], D["dts"][r, z * 32:(z + 1) * 32], W=[dt[s]], ds=dt[s].ds)
            c.dma(xs[s][:, :], D["xs"][r, :], W=[xs[s]], ds=xs[s].ds)
            c.dma(la[s][:, :], D["la"][r, z * 512:(z + 1) * 512], W=[la[s]], ds=la[s].ds)
            b, j = t // 4, t % 4
            for nm, tl in (("BT", BT[s]), ("CT", CT[s]), ("qT", qT[s]), ("kT", kT[s])):
                c.dma(tl[:, :, :], D[nm][b, :, j * 128:(j + 1) * 128].rearrange("(cc p) t -> p cc t", p=128), W=[tl], ds=tl.ds)
            c.dma(km[s][:, :], D["km"][r, :], W=[km[s]], ds=km[s].ds)
            c.dma(Bm[s][:, :], D["Bm"][r, :], W=[Bm[s]], ds=Bm[s].ds)
            c.dma(vv[s][:, :], D["vs"][r, :], W=[vv[s]], ds=vv[s].ds)

        def loadB(it):
            t = order[it]
            s2 = it % 2
            r = slice(t * 128, (t + 1) * 128)
            c.dma(ybl[s2][:, :], D["yb"][r, :], W=[ybl[s2]], ds=ybl[s2].ds)
            c.dma(obl[s2][:, :], D["ob"][r, :], W=[obl[s2]], ds=obl[s2].ds)

        def bc(ap, n, m):
            return ap.unsqueeze(2).to_broadcast([128, n, m])

        BY = [banks[0], banks[1]]
        BSEG = [banks[2], banks[3]]
        BY2, BS = banks[4], banks[5]
        BM = [banks[6], banks[7]]

        def stageA(it):
            s = it % NS
            u = it % 2
            xs3 = xs[s][:, :].rearrange("p (h q) -> p h q", h=32)
            xd3 = xd[u][:, :].rearrange("p (h q) -> p h q", h=32)
            xdd3 = xdd[u][:, :].rearrange("p (h q) -> p h q", h=32)

            def rpart(h0, h1):
                for h in range(h0, h1):
                    c.act(Rt[u][:, h, :], MCb[:, :], AF.Identity, R=[MCb, a32[u]], W=[Rt[u]], scale=a32[u][:, h:h + 1])

            def p0():
                c.tt("dve", a32[u][:, :], dt[s][:, :], negA[:, :], ALU.mult, R=[dt[s], negA], W=[a32[u]])
                c.cp("dve", abf[u][:, :], a32[u][:, :], R=[a32[u]], W=[abf[u]])
                b1 = BM[0]
                c.mm(b1[:, 0:32], MCb[:, :], abf[u][:, :], True, True, R=[MCb, abf[u]], W=[b1])
                c.mm(b1[:, 32:64], MA[:, :], abf[u][:, :], True, True, R=[MA, abf[u]], W=[b1])
                c.mm(b1[:, 64:96], ones[:, :], abf[u][:, :], True, True, R=[ones, abf[u]], W=[b1], inc=True)
                c.act(Eall[u][:, :], b1[:, 0:96], AF.Exp, R=[b1], W=[Eall[u]])
                rpart(0, 8)

            def p1():
                rpart(8, 16)
                rpart(16, 24)

            def p2():
                rpart(24, 32)
                bcb = BM[1]
                for g in range(4):
                    c.mm(bcb[:, g * 128:(g + 1) * 128], BT[s][:, g, :], CT[s][:, g, :], True, True, R=[BT[s], CT[s]], W=[bcb], inc=(g == 3))
                c.tt("dve", CBm[u][:, :, :], bcb[:, :].rearrange("p (g i) -> p g i", g=4), MC[:, :].unsqueeze(1).to_broadcast([128, 4, 128]), ALU.mult,
                     R=[bcb, MC], W=[CBm[u]])

            def p3():
                for h0 in (0, 16):
                    c.tt("dve", xd3[:, h0:h0 + 16, :], xs3[:, h0:h0 + 16, :], bc(dt[s][:, h0:h0 + 16], 16, 64), ALU.mult, R=[xs[s], dt[s]], W=[xd[u]])

            def p4():
                for h0 in (0, 16):
                    c.tt("dve", xdd3[:, h0:h0 + 16, :], xd3[:, h0:h0 + 16, :], bc(Eall[u][:, 32 + h0:32 + h0 + 16], 16, 64), ALU.mult,
                         R=[xd[u], Eall[u]], W=[xdd[u]])

            def p5():
                bb = BM[0]
                for hd in range(4):
                    c.mm(bb[:, hd * 128:(hd + 1) * 128], la[s][:, hd * 128:(hd + 1) * 128], MCs[:, :], True, True, R=[la[s], MCs], W=[bb], inc=(hd == 3))
                br = BM[1]
                c.mm(br[:, :], MAs[:, :], la[s][:, :], True, True, R=[MAs, la[s]], W=[br], inc=True)
                c.act(Ep[u][:, :], bb[:, :], AF.Exp, R=[bb], W=[Ep[u]])
                c.act(Em[u][:, :], bb[:, :], AF.Exp, R=[bb], W=[Em[u]], scale=-1.0)
                c.act(gdec[u][:, :].rearrange("p (a b) -> p a b", b=1), bb[:, :].rearrange("p (a b) -> p a b", a=4)[:, :, last:last + 1], AF.Exp,
                      R=[bb], W=[gdec[u]])
                c.act(Er[u][:, :], br[:, :], AF.Exp, R=[br], W=[Er[u]])

            def p6():
                c.tt("dve", qin[u][:, :, :], qT[s][:, :, :], Ep[u][:, :].rearrange("p (a b) -> p a b", a=4), ALU.mult, R=[qT[s], Ep[u]], W=[qin[u]])
                c.tt("dve", kin[u][:, :, :], kT[s][:, :, :], Em[u][:, :].rearrange("p (a b) -> p a b", a=4), ALU.mult, R=[kT[s], Em[u]], W=[kin[u]])
                c.tt("dve", kdec[u][:, :], km[s][:, :], Er[u][:, :], ALU.mult, R=[km[s], Er[u]], W=[kdec[u]])

            def p7():
                ba = BM[0]
                for hd in range(4):
                    c.mm(ba[:, hd * 128:(hd + 1) * 128], kin[u][:, hd, :], qin[u][:, hd, :], True, True, R=[kin[u], qin[u]], W=[ba], inc=(hd == 3))
                c.tt("dve", attm[u][:, :, :], ba[:, :].rearrange("p (a b) -> p a b", a=4), MC[:, :].unsqueeze(1).to_broadcast([128, 4, 128]), ALU.mult,
                     R=[ba, MC], W=[attm[u]])

            return [p0, p1, p2, p3, p4, p5, p6, p7]

        def seg_mm(u, sg):
            bk = BSEG[sg % 2]
            c.mm(bk[:, :], MA[:, :], Rt[u][:, sg * 4:(sg + 1) * 4, :].rearrange("p a b -> p (a b)"), True, True, R=[MA, Rt[u]], W=[bk], inc=True)
            e = Eg[sg % 3]
            c.act(e[:, :], bk[:, :], AF.Exp, R=[bk], W=[e])

        def stageB(it, fill):
            t = order[it]
            s = it % NS
            u = it % 2
            s2 = it % 2
            bd = None
            if fwd and t % TPS == 0 and t != 0:
                bd = t // TPS
            if (not fwd) and (t + 1) % TPS == 0 and t != NT - 1:
                bd = (t + 1) // TPS
            if bd is not None:
                c.ts("dve", h32[:, :], h32[:, :], flags[:, bd:bd + 1], None, ALU.mult, None, R=[h32, flags], W=[h32])
                c.cp("act", hbf[:, :], h32[:, :], R=[h32], W=[hbf])
                c.ts("dve", gh32[:, :], gh32[:, :], flags[:, bd:bd + 1], None, ALU.mult, None, R=[gh32, flags], W=[gh32])
                c.cp("act", ghbf[:, :], gh32[:, :], R=[gh32], W=[ghbf])
            yo = yn[s2] if fwd else ybs[s2]

            def tail(g):
                by = BY[g % 2]
                c.mm(BY2[:, :], CT[s][:, g, :], hbf[:, g * 512:(g + 1) * 512], True, True, R=[CT[s], hbf], W=[BY2], inc=True)
                c.mm(BS[:, :], Bm[s][:, g * 128:(g + 1) * 128], xdd[u][:, g * 512:(g + 1) * 512], True, True, R=[Bm[s], xdd[u]], W=[BS], inc=True)
                ty = ty16[g % 2]
                c.tt("dve", ty[:, :].rearrange("p (h q) -> p h q", h=8), BY2[:, :].rearrange("p (h q) -> p h q", h=8),
                     bc(Eall[u][:, g * 8:(g + 1) * 8], 8, 64), ALU.mult, R=[BY2, Eall[u]], W=[ty])
                c.mm(by[:, :], ident[:, :], ty[:, :], False, True, R=[ident, ty], W=[by], inc=True)
                c.cp("act", yo[:, g * 512:(g + 1) * 512], by[:, :], R=[by], W=[yo])
                tsb = tmps[g % 2]
                c.tt("pool", tsb[:, :].rearrange("p (h q) -> p h q", h=8), h32[:, g * 512:(g + 1) * 512].rearrange("p (h q) -> p h q", h=8),
                     bc(Eall[u][:, 64 + g * 8:64 + (g + 1) * 8], 8, 64), ALU.mult, R=[h32, Eall[u]], W=[tsb])
                c.tt("dve", h32[:, g * 512:(g + 1) * 512], tsb[:, :], BS[:, :], ALU.add, R=[tsb, BS], W=[h32])
                c.cp("act", hbf[:, g * 512:(g + 1) * 512], h32[:, g * 512:(g + 1) * 512], R=[h32], W=[hbf])

            seg_mm(u, 0)
            seg_mm(u, 1)
            for g in range(4):
                by = BY[g % 2]
                for sg in (2 * g, 2 * g + 1):
                    w = WT[sg % 3]
                    c.tt("dve", w[:, :, :], Eg[sg % 3][:, :].rearrange("p (a b) -> p a b", a=4), CBm[u][:, g:g + 1, :].to_broadcast([128, 4, 128]), ALU.mult,
                         R=[Eg[sg % 3], CBm[u]], W=[w])
                    if sg % 2 == 0:
                        if fwd:
                            c.mm(by[:, :], ident[:, :], ybl[s2][:, g * 512:(g + 1) * 512], True, False, R=[ident, ybl[s2]], W=[by])
                        else:
                            c.mm(by[:, :], zer[:, :], xd[u][:, g * 512:(g + 1) * 512], True, False, R=[zer, xd[u]], W=[by])
                    for e in range(4):
                        h = sg * 4 + e
                        o_ap = by[:, (h % 8) * 64:(h % 8) * 64 + 64]
                        if fwd:
                            c.mm(o_ap, w[:, e, :], xd[u][:, h * 64:(h + 1) * 64], False, False, R=[w, xd[u]], W=[by])
                            c.mm(o_ap, Dg[:, h, :], xs[s][:, h * 64:(h + 1) * 64], False, False, R=[Dg, xs[s]], W=[by], inc=(e == 3))
                        else:
                            c.mm(o_ap, w[:, e, :], xd[u][:, h * 64:(h + 1) * 64], False, False, R=[w, xd[u]], W=[by], inc=(e == 3))
                    if sg + 2 < 8:
                        seg_mm(u, sg + 2)
                    if fill:
                        fill[sg]()
                if g >= 1:
                    tail(g - 1)
            tail(3)
            c.dma(D["yn" if fwd else "yb"][t * 128:(t + 1) * 128, :], yo[:, :], R=[yo], ds=yo.ds)
            bo = [BM[0], BM[1]]
            for hd in range(4):
                o_ap = bo[hd // 2][:, (hd % 2) * 256:(hd % 2) * 256 + 256]
                c.mm(o_ap, attm[u][:, hd, :], vv[s][:, hd * 256:(hd + 1) * 256], True, False, R=[attm[u], vv[s]], W=[bo[hd // 2]])
                c.mm(o_ap, qin[u][:, hd, :], ghbf[:, hd * 256:(hd + 1) * 256], False, True, R=[qin[u], ghbf], W=[bo[hd // 2]], inc=True)
            bg = [BY2, BS]
            for hd in range(4):
                c.mm(bg[hd // 2][:, (hd % 2) * 256:(hd % 2) * 256 + 256], kdec[u][:, hd * 128:(hd + 1) * 128], vv[s][:, hd * 256:(hd + 1) * 256], True, True,
                     R=[kdec[u], vv[s]], W=[bg[hd // 2]], inc=True)
            for hd in range(4):
                c.stt("dve", gh32[:, hd * 256:(hd + 1) * 256], gh32[:, hd * 256:(hd + 1) * 256], gdec[u][:, hd:hd + 1],
                      bg[hd // 2][:, (hd % 2) * 256:(hd % 2) * 256 + 256], ALU.mult, ALU.add, R=[gh32, gdec[u], bg[hd // 2]], W=[gh32])
            if fwd:
                for hf in range(2):
                    c.tt("dve", on[s2][:, hf * 512:(hf + 1) * 512], bo[hf][:, :], obl[s2][:, hf * 512:(hf + 1) * 512], ALU.add, R=[bo[hf], obl[s2]], W=[on[s2]])
                c.dma(D["on"][t * 128:(t + 1) * 128, :], on[s2][:, :], R=[on[s2]], ds=on[s2].ds)
            else:
                for hf in range(2):
                    c.cp("act", obs[s2][:, hf * 512:(hf + 1) * 512], bo[hf][:, :], R=[bo[hf]], W=[obs[s2]])
                c.dma(D["ob"][t * 128:(t + 1) * 128, :], obs[s2][:, :], R=[obs[s2]], ds=obs[s2].ds)
            c.cp("act", ghbf[:, :], gh32[:, :], R=[gh32], W=[ghbf])

        loadA(0)
        if NT > 1:
            loadA(1)
        if fwd:
            loadB(0)
        for f in stageA(0):
            f()
        for it in range(NT):
            if it + 2 < NT:
                loadA(it + 2)
            if fwd and it + 1 < NT:
                loadB(it + 1)
            stageB(it, stageA(it + 1) if it + 1 < NT else None)
    c.end_phase()


def load_w_res(c, es, D, name, rows, cols):
    kc = rows // 128
    t = c.tile(es, "r_" + name, [128, kc, cols], BF16, dma=True)
    step = max(1, 8192 // cols)
    for k0 in range(0, kc, step):
        k1 = min(kc, k0 + step)
        c.dma(t[:, k0:k1, :], D[name + "_b"][k0 * 128:k1 * 128, :].rearrange("(kc p) c -> p kc c", p=128), W=[t], ds=t.ds)
    return t


def phase_d1(c, D, cfg):
    NT = cfg.NT
    with ExitStack() as es:
        S = lambda n, sh, dt, dma=False: c.tile(es, n, sh, dt, dma)
        ident = load_cst(c, es, D, "ident", BF16)
        n2w = load_cst(c, es, D, "n2w")
        snw = load_cst(c, es, D, "snw")
        gnw = load_cst(c, es, D, "gnw")
        wso = load_w_res(c, es, D, "w_ssd_out", 2048, 1024)
        wgo = load_w_res(c, es, D, "w_gla_out", 1024, 1024)
        wo = load_w_res(c, es, D, "w_o", 1024, 1024)
        yn = [S("yn%d" % i, [128, 2048], BF16, True) for i in range(2)]
        on = [S("on%d" % i, [128, 1024], BF16, True) for i in range(2)]
        gt = [S("gt%d" % i, [128, 2048], BF16, True) for i in range(3)]
        xt = [S("xt%d" % i, [128, 1024], F32, True) for i in range(3)]
        zs = [S("zs%d" % i, [128, 2048], BF16, True) for i in range(2)]
        gs = [S("gs%d" % i, [128, 1024], BF16, True) for i in range(2)]
        y32 = S("y32", [128, 2048], F32)
        o32 = S("o32", [128, 1024], F32)
        ynb2 = [S("ynb%d" % i, [128, 2048], BF16) for i in range(2)]
        onb2 = [S("onb%d" % i, [128, 1024], BF16) for i in range(2)]
        ssq1 = S("ssq1", [128, 8], F32)
        rs1 = S("rs1", [128, 8], F32)
        ynT = S("ynT", [128, 16, 128], BF16)
        onT = S("onT", [128, 8, 128], BF16)
        m1 = S("m1", [128, 1024], F32)
        m2 = S("m2", [128, 1024], F32)
        mx = [S("mx%d" % i, [128, 1024], BF16) for i in range(2)]
        mT = S("mT", [128, 8, 128], BF16)
        x1 = [S("x1_%d" % i, [128, 1024], F32, True) for i in range(2)]
        h2 = S("h2", [128, 1024], BF16)
        h2T = [S("h2T%d" % i, [128, 8, 128], BF16, True) for i in range(2)]
        junk = S("junk", [128, 1024], BF16)
        ssq = S("ssq", [128, 2], F32)
        rs = S("rs", [128, 2], F32)
        banks = c.banks
        bk_i = [0]

        def nbank():
            b = banks[bk_i[0] % 8]
            bk_i[0] += 1
            return b

        def load0(t):
            s = t % 2
            r = slice(t * 128, (t + 1) * 128)
            c.dma(yn[s][:, :], D["yn"][r, :], W=[yn[s]], ds=yn[s].ds)
            c.dma(zs[s][:, :], D["zs"][r, :], W=[zs[s]], ds=zs[s].ds)
            c.dma(on[s][:, :], D["on"][r, :], W=[on[s]], ds=on[s].ds)
            c.dma(gs[s][:, :], D["gs"][r, :], W=[gs[s]], ds=gs[s].ds)

        def load12(t):
            s = t % 3
            r = slice(t * 128, (t + 1) * 128)
            c.dma(gt[s][:, :], D["gts"][r, :], W=[gt[s]], ds=gt[s].ds)
            c.dma(xt[s][:, :], D["xm"][r, :], W=[xt[s]], ds=xt[s].ds)

        def transp(src, ncol, dst, engs):
            for b0 in range(0, ncol, 8):
                bk = nbank()
                pv = bk[:, :].bitcast(BF16).rearrange("p (a b) -> p a b", a=8)
                for k in range(8):
                    c.tr(pv[:, k, :], src[:, (b0 + k) * 128:(b0 + k + 1) * 128], ident[:, :], R=[src, ident], W=[bk], inc=(k == 7))
                c.cp(engs[(b0 // 8) % len(engs)], dst[:, b0:b0 + 8, :], pv[:, :, :], R=[bk], W=[dst])

        def stage0(t):
            s = t % 2
            ynb = ynb2[t % 2]
            onb = onb2[t % 2]
            c.tt("dve", y32[:, :], yn[s][:, :], zs[s][:, :], ALU.mult, R=[yn[s], zs[s]], W=[y32])
            for g in range(4):
                c.act(junk[:, 0:512], y32[:, g * 512:(g + 1) * 512], AF.Square, R=[y32], W=[junk, ssq1], accum_out=ssq1[:, g:g + 1])
            c.act(rs1[:, 0:4], ssq1[:, 0:4], AF.Ln, R=[ssq1], W=[rs1], scale=1.0 / 512, bias=EPS)
            c.act(rs1[:, 0:4], rs1[:, 0:4], AF.Exp, R=[rs1], W=[rs1], scale=-0.5)
            for g in range(4):
                c.stt("dve", ynb[:, g * 512:(g + 1) * 512], y32[:, g * 512:(g + 1) * 512], rs1[:, g:g + 1], snw[:, g * 512:(g + 1) * 512],
                      ALU.mult, ALU.mult, R=[y32, rs1, snw], W=[ynb])
            for hd in range(4):
                c.act(junk[:, 0:256], on[s][:, hd * 256:(hd + 1) * 256], AF.Square, R=[on[s]], W=[junk, ssq1], accum_out=ssq1[:, 4 + hd:5 + hd])
            c.act(rs1[:, 4:8], ssq1[:, 4:8], AF.Ln, R=[ssq1], W=[rs1], scale=1.0 / 256, bias=EPS)
            c.act(rs1[:, 4:8], rs1[:, 4:8], AF.Exp, R=[rs1], W=[rs1], scale=-0.5)
            for hd in range(4):
                c.stt("dve", o32[:, hd * 256:(hd + 1) * 256], on[s][:, hd * 256:(hd + 1) * 256], rs1[:, 4 + hd:5 + hd], gnw[:, hd * 256:(hd + 1) * 256],
                      ALU.mult, ALU.mult, R=[on[s], rs1, gnw], W=[o32])
            c.tt("dve", onb[:, :], o32[:, :], gs[s][:, :], ALU.mult, R=[o32, gs[s]], W=[onb])

        def stage1a(t):
            transp(ynb2[t % 2], 16, ynT, ["act", "dve"])
            transp(onb2[t % 2], 8, onT, ["act"])

        def stage1b(t):
            s = t % 3
            for hf in range(2):
                ba = nbank()
                for kc in range(16):
                    c.mm(ba[:, :], ynT[:, kc, :], wso[:, kc, hf * 512:(hf + 1) * 512], kc == 0, kc == 15, R=[ynT, wso], W=[ba], inc=(kc == 15))
                bb = nbank()
                for kc in range(8):
                    c.mm(bb[:, :], onT[:, kc, :], wgo[:, kc, hf * 512:(hf + 1) * 512], kc == 0, kc == 7, R=[onT, wgo], W=[bb], inc=(kc == 7))
                cs = slice(hf * 512, (hf + 1) * 512)
                c.stt("dve", m1[:, cs], gt[s][:, hf * 512:(hf + 1) * 512], 1.0, ba[:, :], ALU.add, ALU.mult, R=[gt[s], ba], W=[m1])
                c.stt("dve", m2[:, cs], gt[s][:, 1024 + hf * 512:1024 + (hf + 1) * 512], 1.0, bb[:, :], ALU.add, ALU.mult, R=[gt[s], bb], W=[m2])
                c.tt("pool", mx[t % 2][:, cs], m1[:, cs], m2[:, cs], ALU.add, R=[m1, m2], W=[mx[t % 2]])

        def stage2a(t):
            s = t % 3
            transp(mx[t % 2], 8, mT, ["act"])
            xo = x1[t % 2]
            for hf in range(2):
                bo = nbank()
                for kc in range(8):
                    c.mm(bo[:, :], mT[:, kc, :], wo[:, kc, hf * 512:(hf + 1) * 512], kc == 0, kc == 7, R=[mT, wo], W=[bo], inc=(kc == 7))
                c.stt("dve", xo[:, hf * 512:(hf + 1) * 512], bo[:, :], 0.5, xt[s][:, hf * 512:(hf + 1) * 512], ALU.mult, ALU.add, R=[bo, xt[s]], W=[xo])
            c.dma(D["x1"][t * 128:(t + 1) * 128, :], xo[:, :], R=[xo], ds=xo.ds)
            c.act(junk[:, :], xo[:, :], AF.Square, R=[xo], W=[junk, ssq], accum_out=ssq[:, 0:1])
            c.act(rs[:, 0:1], ssq[:, 0:1], AF.Ln, R=[ssq], W=[rs], scale=1.0 / 1024, bias=EPS)
            c.act(rs[:, 0:1], rs[:, 0:1], AF.Exp, R=[rs], W=[rs], scale=-0.5)
            c.stt("dve", h2[:, :], xo[:, :], rs[:, 0:1], n2w[:, :], ALU.mult, ALU.mult, R=[xo, rs, n2w], W=[h2])

        def stage2b(t):
            transp(h2, 8, h2T[t % 2], ["act"])
            c.dma(D["h2T"][:, t * 128:(t + 1) * 128].rearrange("(kc p) t -> p kc t", p=128), h2T[t % 2][:, :, :], R=[h2T[t % 2]], ds=h2T[t % 2].ds)

        load0(0)
        if NT > 1:
            load0(1)
        load12(0)
        if NT > 1:
            load12(1)
        stage0(0)
        if NT > 2:
            load0(2)
        if NT > 1:
            stage0(1)
        stage1a(0)
        stage1b(0)
        for t in range(NT):
            if t + 3 < NT:
                load0(t + 3)
            if t + 2 < NT:
                load12(t + 2)
            if t + 1 < NT:
                stage1a(t + 1)
            stage2a(t)
            if t + 2 < NT:
                stage0(t + 2)
            if t + 1 < NT:
                stage1b(t + 1)
            stage2b(t)
    c.end_phase()


def phase_ffn(c, D, cfg):
    T = cfg.T
    NBK = T // 256
    SEG = cfg.SEG
    with ExitStack() as es:
        S = lambda n, sh, dt, dma=False: c.tile(es, n, sh, dt, dma)
        fcw = load_cst(c, es, D, "fcw")
        fcb = load_cst(c, es, D, "fcb")
        fnw = load_cst(c, es, D, "fnw")
        flags = load_cst(c, es, D, "flags")
        wup = load_w_res(c, es, D, "w_ffn_up", 1024, 2 * D_FF)
        wdn = load_w_res(c, es, D, "w_ffn_down", D_FF, 1024)
        hT = [S("hT%d" % i, [128, 8, 258], BF16, True) for i in range(2)]
        x1 = S("x1", [128, 2, 1024], F32, True)
        actT = [S("actT%d" % i, [128, 22, 256], BF16) for i in range(2)]
        ag = [S("ag%d" % i, [128, 256], F32) for i in range(2)]
        av = [S("av%d" % i, [128, 256], F32) for i in range(2)]
        sg = [S("sg%d" % i, [128, 256], F32) for i in range(2)]
        x2 = [S("x2_%d" % i, [128, 1024], F32, True) for i in range(2)]
        junk = S("junk", [128, 1024], BF16)
        ssq = S("ssq", [128, 2], F32)
        rs = S("rs", [128, 2], F32)
        banks = c.banks
        bk_i = [0]

        def nbank():
            b = banks[bk_i[0] % 8]
            bk_i[0] += 1
            return b

        def load(kb):
            s = kb % 2
            t0 = kb * 256
            lo = 1 if kb == 0 else 0
            hi = 257 if kb == NBK - 1 else 258
            if kb == 0:
                c.op("pool", lambda E: E.memset(hT[s][:, :, 0:1], 0.0), W=[hT[s]])
            if kb == NBK - 1:
                c.op("pool", lambda E: E.memset(hT[s][:, :, 257:258], 0.0), W=[hT[s]])
            c.dma(hT[s][:, :, lo:hi], D["h2T"][:, t0 - 1 + lo:t0 - 1 + hi].rearrange("(kc p) t -> p kc t", p=128), W=[hT[s]], ds=hT[s].ds)
            if t0 % SEG == 0 and t0 != 0:
                bd = t0 // SEG
                c.ts("dve", hT[s][:, :, 0:1], hT[s][:, :, 0:1], flags[:, bd:bd + 1], None, ALU.mult, None, R=[hT[s], flags], W=[hT[s]])
            if (t0 + 256) % SEG == 0 and t0 + 256 != T:
                bd = (t0 + 256) // SEG
                c.ts("dve", hT[s][:, :, 257:258], hT[s][:, :, 257:258], flags[:, bd:bd + 1], None, ALU.mult, None, R=[hT[s], flags], W=[hT[s]])

        def conv(bk, ch, acc):
            c.act(acc[:, :], bk[:, 0:256], AF.Identity, R=[bk, fcw, fcb], W=[acc], bias=fcb[:, ch:ch + 1], scale=fcw[:, ch * 3:ch * 3 + 1])
            for k in (1, 2):
                c.stt("dve", acc[:, :], bk[:, k:k + 256], fcw[:, ch * 3 + k:ch * 3 + k + 1], acc[:, :], ALU.mult, ALU.add, R=[bk, fcw, acc], W=[acc])

        def up(kb):
            s = kb % 2
            aT = actT[kb % 2]
            for cc in range(22):
                bg = nbank()
                for kc in range(8):
                    c.mm(bg[:, 0:258], wup[:, kc, cc * 128:(cc + 1) * 128], hT[s][:, kc, :], kc == 0, kc == 7, R=[wup, hT[s]], W=[bg], inc=(kc == 7))
                bv = nbank()
                for kc in range(8):
                    c.mm(bv[:, 0:258], wup[:, kc, (cc + 22) * 128:(cc + 23) * 128], hT[s][:, kc, :], kc == 0, kc == 7, R=[wup, hT[s]], W=[bv], inc=(kc == 7))
                conv(bg, cc, ag[cc % 2])
                conv(bv, cc + 22, av[cc % 2])
                c.act(sg[cc % 2][:, :], ag[cc % 2][:, :], AF.Silu, R=[ag[cc % 2]], W=[sg[cc % 2]])
                c.tt("pool", aT[:, cc, :], sg[cc % 2][:, :], av[cc % 2][:, :], ALU.mult, R=[sg[cc % 2], av[cc % 2]], W=[aT])

        def down(kb):
            aT = actT[kb % 2]
            c.dma(x1[:, :, :], D["x1"][kb * 256:(kb + 1) * 256, :].rearrange("(j p) d -> p j d", p=128), W=[x1], ds=x1.ds)
            for j in range(2):
                o = x2[j]
                for hf in range(2):
                    bo = nbank()
                    for cc in range(22):
                        c.mm(bo[:, :], aT[:, cc, j * 128:(j + 1) * 128], wdn[:, cc, hf * 512:(hf + 1) * 512], cc == 0, cc == 21, R=[aT, wdn], W=[bo], inc=(cc == 21))
                    c.tt("dve", o[:, hf * 512:(hf + 1) * 512], bo[:, :], x1[:, j, hf * 512:(hf + 1) * 512], ALU.add, R=[bo, x1], W=[o])
                c.act(junk[:, :], o[:, :], AF.Square, R=[o], W=[junk, ssq], accum_out=ssq[:, 0:1])
                c.act(rs[:, 0:1], ssq[:, 0:1], AF.Ln, R=[ssq], W=[rs], scale=1.0 / 1024, bias=EPS)
                c.act(rs[:, 0:1], rs[:, 0:1], AF.Exp, R=[rs], W=[rs], scale=-0.5)
                c.stt("dve", o[:, :], o[:, :], rs[:, 0:1], fnw[:, :], ALU.mult, ALU.mult, R=[o, rs, fnw], W=[o])
                c.dma(D["y"][kb * 256 + j * 128:kb * 256 + (j + 1) * 128, :], o[:, :], R=[o], ds=o.ds)

        load(0)
        if NBK > 1:
            load(1)
        up(0)
        for kb in range(NBK):
            if kb + 1 < NBK:
                up(kb + 1)
            if kb + 2 < NBK:
                load(kb + 2)
            down(kb)
    c.end_phase()


def plan_cores(nb_prompt=16, nb_sample=2, nseg=4, seg=2048):
    cores = []
    for s in range(nb_sample):
        cores.append(([("s", s, i * seg) for i in range(nseg)], [0, 1, 1, 1, 0]))
    rest = [("p", i, 0) for i in range(nb_prompt)]
    ncore = 8 - nb_sample
    per = [len(rest) // ncore + (1 if i < len(rest) % ncore else 0) for i in range(ncore)]
    k = 0
    for i in range(ncore):
        segs = rest[k:k + per[i]]
        k += per[i]
        segs = segs + [None] * (nseg - len(segs))
        cores.append((segs, [0, 0, 0, 0, 0]))
    return cores


def core_inputs(inp, segs, flags, cfg):
    SEG, NSEG, T, NB = cfg.SEG, cfg.NSEG, cfg.T, cfg.NB
    xm = np.zeros((T, 1024), np.float32)
    for i, sg in enumerate(segs):
        if sg is None:
            continue
        src = inp["x_sample"] if sg[0] == "s" else inp["x_prompt"]
        xm[i * SEG:(i + 1) * SEG] = src[sg[1], sg[2]:sg[2] + SEG]
    xh = np.zeros((NB, 4, 1024), np.float32)
    for b in range(NB):
        t0 = b * 512
        for r, t in enumerate((t0 - 2, t0 - 1, t0 + 512, t0 + 513)):
            if t < 0 or t >= T:
                continue
            sb = t // SEG
            so = t0 // SEG
            if sb != so:
                bd = max(sb, so)
                if flags[bd] == 0:
                    continue
            xh[b, r] = xm[t]
    m = {"xm": xm, "xh": xh.reshape(NB * 4, 1024), "cst": build_consts(inp, list(flags) + [0, 0, 0])}
    for n, r, cc in WEIGHTS:
        m[n] = np.ascontiguousarray(inp[n][0])
    return m


_CACHE = {}


def kernel(**inputs):
    inp = {k: np.asarray(v) for k, v in inputs.items()}
    cfg = Cfg()
    if "nc" not in _CACHE:
        _CACHE["nc"] = build(cfg)[0]
    nc = _CACHE["nc"]
    cores = plan_cores()
    in_maps = [core_inputs(inp, segs, flags, cfg) for segs, flags in cores]
    res = run_bass_kernel_spmd(nc, in_maps, core_ids=list(range(8)))
    yp = np.zeros((16, 2048, 1024), np.float32)
    ys = np.zeros((2, 8192, 1024), np.float32)
    for ci, (segs, flags) in enumerate(cores):
        y = res.results[ci]["y"]
        for i, sg in enumerate(segs):
            if sg is None:
                continue
            blk = y[i * cfg.SEG:(i + 1) * cfg.SEG]
            if sg[0] == "s":
                ys[sg[1], sg[2]:sg[2] + cfg.SEG] = blk
            else:
                yp[sg[1]] = blk
    return (yp, ys)
```

```python
import numpy as np
from contextlib import ExitStack
import concourse.bass as bass
import concourse.mybir as mybir
from concourse.bass_utils import run_bass_kernel_spmd

F32 = mybir.dt.float32
BF16 = mybir.dt.bfloat16
AF = mybir.ActivationFunctionType
ALU = mybir.AluOpType

D_MODEL = 1024
D_IN = 10336
D_FF = 2816
EPS = 1e-6
O_Z, O_XBC, O_DT, O_Q, O_K, O_V, O_G, O_GK, O_GT = 0, 2048, 5120, 5184, 5696, 6208, 7232, 8256, 8288


class Tok:
    __slots__ = ("w", "r")

    def __init__(self):
        self.w = None
        self.r = {}


class DSem:
    def __init__(self, key, h):
        self.key = key
        self.h = h
        self.count = 0


class TL:
    def __init__(self, t, ds=None):
        self.t = t
        self.tok = Tok()
        self.ds = ds

    def __getitem__(self, k):
        return self.t[k]


class Ctx:
    def __init__(self, nc):
        self.nc = nc
        self.eng = {"pe": nc.tensor, "dve": nc.vector, "act": nc.scalar, "pool": nc.gpsimd, "sp": nc.sync}
        self.esem = {e: nc.alloc_semaphore(name="e_" + e) for e in ("pe", "dve", "act", "pool")}
        self.ecnt = {e: 0 for e in self.esem}
        self.seen = {e: {} for e in self.eng}
        self.dsems = {}
        self.free_ds = []
        self.phase_ds = []
        self.nds = 0
        self.ninst = 0

    def new_ds(self):
        if self.free_ds:
            d = self.free_ds.pop()
        else:
            key = "d%d" % self.nds
            self.nds += 1
            d = DSem(key, self.nc.alloc_semaphore(name=key))
            self.dsems[key] = d
        self.phase_ds.append(d)
        return d

    def end_phase(self):
        self.barrier()
        self.free_ds.extend(self.phase_ds)
        self.phase_ds = []

    def tile(self, es, name, shape, dt, dma=False):
        self.ntile = getattr(self, "ntile", 0) + 1
        t = es.enter_context(self.nc.sbuf_tensor("%s_%d" % (name, self.ntile), shape, dt))
        return TL(t, self.new_ds() if dma else None)

    def semh(self, k):
        return self.esem[k] if k in self.esem else self.dsems[k].h

    def cur(self, k):
        return self.ecnt[k] if k in self.ecnt else self.dsems[k].count

    def _waits(self, e, R, W):
        need = {}

        def add(ev, kind):
            if ev is None:
                return
            k, v = ev
            if k == e and (e == "pe" or kind != "raw"):
                return
            if k in self.dsems:
                v = self.dsems[k].count
            if need.get(k, 0) < v:
                need[k] = v

        for t in R:
            add(t.tok.w, "raw")
        for t in W:
            add(t.tok.w, "waw")
            for k, v in t.tok.r.items():
                add((k, v), "war")
        sn = self.seen[e]
        for k, v in need.items():
            if sn.get(k, 0) < v:
                self.eng[e].wait_ge(self.semh(k), v)
                sn[k] = v
                self.ninst += 1

    def op(self, e, fn, R=(), W=(), inc=True):
        self._waits(e, R, W)
        ins = fn(self.eng[e])
        self.ninst += 1
        if inc:
            self.ecnt[e] += 1
            ins.then_inc(self.esem[e], 1)
            v = self.ecnt[e]
        else:
            v = self.ecnt[e] + 1
        for t in R:
            if t.tok.r.get(e, 0) < v:
                t.tok.r[e] = v
        for t in W:
            t.tok.w = (e, v)
            t.tok.r = {}
        return ins

    def dma(self, out, in_, R=(), W=(), ds=None, q="sp", **kw):
        self._waits(q, R, W)
        ins = self.eng[q].dma_start(out=out, in_=in_, **kw)
        ins.then_inc(ds.h, 16)
        ds.count += 16
        self.ninst += 1
        for t in R:
            t.tok.r[ds.key] = ds.count
        for t in W:
            t.tok.w = (ds.key, ds.count)
            t.tok.r = {}
        return ins

    def barrier(self):
        keys = list(self.esem.keys()) + [k for k, d in self.dsems.items() if d.count > 0]
        for e in self.eng:
            sn = self.seen[e]
            for k in keys:
                v = self.cur(k)
                if k == e or v == 0:
                    continue
                if sn.get(k, 0) < v:
                    self.eng[e].wait_ge(self.semh(k), v)
                    sn[k] = v
                    self.ninst += 1

    def mm(self, out, lhsT, rhs, start, stop, R, W, inc=False):
        return self.op("pe", lambda E: E.matmul(out, lhsT=lhsT, rhs=rhs, start=start, stop=stop), R, W, inc)

    def tr(self, out, in_, ident, R, W, inc=False):
        return self.op("pe", lambda E: E.transpose(out, in_, ident), R, W, inc)

    def act(self, out, in_, func, R, W, bias=None, scale=None, accum_out=None):
        kw = {}
        if bias is not None:
            kw["bias"] = bias
        if scale is not None:
            kw["scale"] = scale
        if accum_out is not None:
            kw["accum_out"] = accum_out
        return self.op("act", lambda E: E.activation(out=out, in_=in_, func=func, **kw), R, W)

    def tt(self, e, out, in0, in1, op, R, W):
        return self.op(e, lambda E: E.tensor_tensor(out=out, in0=in0, in1=in1, op=op), R, W)

    def ts(self, e, out, in0, s1, s2, op0, op1, R, W):
        if s2 is None:
            return self.op(e, lambda E: E.tensor_scalar(out=out, in0=in0, scalar1=s1, scalar2=None, op0=op0), R, W)
        return self.op(e, lambda E: E.tensor_scalar(out=out, in0=in0, scalar1=s1, scalar2=s2, op0=op0, op1=op1), R, W)

    def stt(self, e, out, in0, scalar, in1, op0, op1, R, W):
        return self.op(e, lambda E: E.scalar_tensor_tensor(out=out, in0=in0, scalar=scalar, in1=in1, op0=op0, op1=op1), R, W)

    def cp(self, e, out, in_, R, W):
        if e == "act":
            return self.op(e, lambda E: E.copy(out=out, in_=in_), R, W)
        return self.op(e, lambda E: E.tensor_copy(out=out, in_=in_), R, W)


CST_ITEMS = [
    ("ident", 128), ("Af", 128), ("Ab", 128), ("Cf", 128), ("Cb", 128), ("Asf", 128), ("Asb", 128),
    ("Csf", 128), ("Csb", 128), ("Bf", 64), ("Bb", 64), ("ch0", 128), ("ch1", 128),
    ("n1w", 1024), ("n2w", 1024), ("fnw", 1024), ("snw", 2048), ("gnw", 1024), ("dtb", 64), ("alog", 64),
    ("dD", 32), ("cw", 120), ("cb", 24), ("fcw", 132), ("fcb", 44), ("w2a", 1024), ("flags", 8),
]
CST_OFF = {}
_o = 0
for _n, _w in CST_ITEMS:
    CST_OFF[_n] = (_o, _w)
    _o += _w
CST_W = _o


def build_consts(inp, flags):
    c = np.zeros((128, CST_W), np.float32)

    def put(name, arr):
        o, w = CST_OFF[name]
        c[:, o:o + w] = arr

    k = np.arange(128)
    same = np.ones((128, 128), bool)
    Af = ((k[:, None] > k[None, :]) & same).astype(np.float32)
    Ab = ((k[:, None] < k[None, :]) & same).astype(np.float32)
    Cf = ((k[:, None] <= k[None, :]) & same).astype(np.float32)
    Cb = ((k[:, None] >= k[None, :]) & same).astype(np.float32)
    put("ident", np.eye(128, dtype=np.float32))
    put("Af", Af); put("Ab", Ab); put("Cf", Cf); put("Cb", Cb)
    put("Asf", Af * (-1.0 / 16)); put("Asb", Ab * (-1.0 / 16)); put("Csf", Cf * (-1.0 / 16)); put("Csb", Cb * (-1.0 / 16))
    il = np.arange(64)
    put("Bf", ((k[:, None] % 64) <= il[None, :]).astype(np.float32))
    put("Bb", ((k[:, None] % 64) >= il[None, :]).astype(np.float32))
    put("ch0", np.ones((128, 128), np.float32))
    put("ch1", np.repeat((k >= 64).astype(np.float32)[:, None], 128, 1))
    rep = lambda v: np.broadcast_to(np.asarray(v, np.float32).reshape(1, -1), (128, np.asarray(v).size))
    put("n1w", rep(inp["norm1_w"][0])); put("n2w", rep(inp["norm2_w"][0])); put("fnw", rep(inp["final_norm_w"]))
    put("snw", rep(inp["ssd_norm_w"][0])); put("gnw", rep(inp["gla_norm_w"][0]))
    put("dtb", rep(inp["ssd_dt_bias"][0])); put("alog", rep(inp["ssd_a_log"][0])); put("dD", rep(inp["ssd_d"][0]))
    cw = inp["ssd_conv_w"][0]
    put("cw", cw.reshape(5, 24, 128).transpose(2, 1, 0).reshape(128, 120))
    put("cb", inp["ssd_conv_b"][0].reshape(24, 128).T)
    fw = inp["ffn_conv_w"][0]
    put("fcw", fw.reshape(3, 44, 128).transpose(2, 1, 0).reshape(128, 132))
    put("fcb", inp["ffn_conv_b"][0].reshape(44, 128).T)
    w2a = np.zeros((128, 2, 512), np.float32)
    w2a[0:16] = inp["gla_gate_w2"][0].transpose(1, 0, 2)
    w2a[32] = inp["gla_gate_b"][0]
    put("w2a", w2a.reshape(128, 1024))
    put("flags", np.broadcast_to(np.asarray(flags, np.float32).reshape(1, 8), (128, 8)))
    return c


class Cfg:
    def __init__(self, SEG=2048, NSEG=4, debug=False, phases=(0, 1, 2, 3, 4, 5)):
        self.SEG, self.NSEG = SEG, NSEG
        self.T = SEG * NSEG
        self.NB = self.T // 512
        self.NT = self.T // 128
        self.debug = debug
        self.phases = phases


WEIGHTS = [("w_in", 1024, D_IN), ("w_ssd_out", 2048, 1024), ("w_gla_out", 1024, 1024), ("w_o", 1024, 1024),
           ("w_ffn_up", 1024, 2 * D_FF), ("w_ffn_down", D_FF, 1024)]


def build(cfg):
    nc = bass.Bass("TRN2", target_bir_lowering=False)
    T, NB = cfg.T, cfg.NB
    D = {}
    D["xm"] = nc.dram_tensor("xm", [T, 1024], F32, kind="ExternalInput").ap()
    D["xh"] = nc.dram_tensor("xh", [NB * 4, 1024], F32, kind="ExternalInput").ap()
    D["cst"] = nc.dram_tensor("cst", [128, CST_W], F32, kind="ExternalInput").ap()
    for n, r, cc in WEIGHTS:
        D[n] = nc.dram_tensor(n, [r, cc], F32, kind="ExternalInput").ap()
        D[n + "_b"] = nc.dram_tensor(n + "_b", [r, cc], BF16, kind="Internal").ap()
    D["y"] = nc.dram_tensor("y", [T, 1024], F32, kind="ExternalOutput").ap()
    sk = "ExternalOutput" if cfg.debug else "Internal"

    def scr(name, shape, dt):
        D[name] = nc.dram_tensor(name, shape, dt, kind=sk).ap()

    scr("zs", [T, 2048], BF16); scr("gts", [T, 2048], BF16); scr("vs", [T, 1024], BF16); scr("gs", [T, 1024], BF16)
    scr("xs", [T, 2048], BF16); scr("Bm", [T, 512], BF16); scr("km", [T, 512], BF16)
    scr("BT", [NB, 512, 512], BF16); scr("CT", [NB, 512, 512], BF16); scr("qT", [NB, 512, 512], BF16); scr("kT", [NB, 512, 512], BF16)
    scr("la", [T, 1024], BF16); scr("dts", [T, 64], F32)
    scr("yb", [T, 2048], BF16); scr("ob", [T, 1024], BF16)
    scr("yn", [T, 2048], BF16); scr("on", [T, 1024], BF16)
    scr("x1", [T, 1024], F32); scr("h2T", [1024, T], BF16)

    c = Ctx(nc)
    c.banks = [TL(nc.alloc_psum_tensor("bank%d" % i, [128, 512], F32)) for i in range(8)]
    if 0 in cfg.phases:
        phase0(c, D, cfg)
    if 1 in cfg.phases:
        phase1(c, D, cfg)
    if 2 in cfg.phases:
        scan_phase(c, D, cfg, "b")
    if 3 in cfg.phases:
        scan_phase(c, D, cfg, "f")
    if 4 in cfg.phases:
        phase_d1(c, D, cfg)
    if 5 in cfg.phases:
        phase_ffn(c, D, cfg)
    c.barrier()
    return nc, c


def cst_ap(D, name, rows=128):
    o, w = CST_OFF[name]
    return D["cst"][0:rows, o:o + w]


def load_cst(c, es, D, name, dt=F32, rows=128):
    o, w = CST_OFF[name]
    t32 = c.tile(es, "c_" + name, [128, w], F32, dma=True)
    c.dma(t32[0:rows, :], cst_ap(D, name, rows), W=[t32], ds=t32.ds)
    if dt == F32:
        return t32
    tb = c.tile(es, "cb_" + name, [128, w], BF16)
    c.cp("pool", tb[0:rows, :], t32[0:rows, :], R=[t32], W=[tb])
    return tb


def phase0(c, D, cfg):
    ds_in = c.new_ds()
    ds = c.new_ds()
    for n, r, cc in WEIGHTS:
        for r0 in range(0, r, 128):
            c.dma(D[n + "_b"][r0:r0 + 128, :], D[n][r0:r0 + 128, :], ds=(ds_in if n == "w_in" else ds), q="pool")
    c.ds_win = ds_in


def phase1(c, D, cfg):
    nc = c.nc
    NB = cfg.NB
    with ExitStack() as es:
        S = lambda n, sh, dt, dma=False: c.tile(es, n, sh, dt, dma)
        identf = load_cst(c, es, D, "ident")
        ident = S("identb", [128, 128], BF16)
        c.cp("pool", ident[:, :], identf[:, :], R=[identf], W=[ident])
        n1w = load_cst(c, es, D, "n1w")
        cw = load_cst(c, es, D, "cw")
        dg = S("dg", [128, 120, 128], BF16)
        for idx in range(120):
            c.ts("dve", dg[:, idx, :], identf[:, :], cw[:, idx:idx + 1], None, ALU.mult, None, R=[identf, cw], W=[dg])
        xin = [S("xin%d" % i, [128, 2, 260], BF16) for i in range(2)]
        cb = load_cst(c, es, D, "cb")
        dtb = load_cst(c, es, D, "dtb")
        w2a = load_cst(c, es, D, "w2a", BF16, rows=64)
        xt = [S("xt%d" % i, [128, 4, 1024], F32, True) for i in range(2)]
        xh = [S("xh%d" % i, [4, 1024], F32, True) for i in range(2)]
        hb = [S("hb%d" % i, [128, 4, 1024], BF16) for i in range(2)]
        hbh = [S("hbh%d" % i, [4, 1024], BF16) for i in range(2)]
        hT = [S("hT%d" % i, [128, 8, 516], BF16) for i in range(2)]
        wt = [S("wt%d" % i, [128, 8, 512], BF16, True) for i in range(3)]
        st = [S("st%d" % i, [128, 4, 512], BF16, True) for i in range(3)]
        xo = [S("xo%d" % i, [128, 4, 512], BF16, True) for i in range(2)]
        acc = [S("acc%d" % i, [128, 2, 256], F32) for i in range(3)]
        junk = S("junk", [128, 1024], BF16)
        ss = [S("ss%d" % i, [128, 8], F32) for i in range(2)]
        rstd = [S("rstd%d" % i, [128, 8], F32) for i in range(2)]
        gkT = [S("gkT%d" % z, [64, 512], BF16) for z in range(2)]
        lat = [S("lat%d" % i, [128, 1024], BF16, True) for i in range(2)]
        et = [S("et%d" % i, [128, 1024], F32) for i in range(2)]
        dtt = S("dtt", [128, 4, 64], F32, True)
        dte = S("dte", [128, 4, 64], F32)
        banks = c.banks
        st_i = [0]
        wt_i = [0]
        bk_i = [0]

        def nbank():
            b = banks[bk_i[0] % 8]
            bk_i[0] += 1
            return b

        for z in range(2):
            c.op("pool", lambda E, z=z: E.memset(gkT[z][0:64, :], 0.0), W=[gkT[z]])
            c.op("pool", lambda E, z=z: E.memset(gkT[z][32:33, :], 1.0), W=[gkT[z]])

        def load_x(b):
            s = b % 2
            c.dma(xt[s][:, :, :], D["xm"][b * 512:(b + 1) * 512, :].rearrange("(j p) d -> p j d", p=128), W=[xt[s]], ds=xt[s].ds)
            c.dma(xh[s][0:4, :], D["xh"][b * 4:(b + 1) * 4, :], W=[xh[s]], ds=xh[s].ds)

        def prepA(b):
            s = b % 2
            for j in range(4):
                c.act(junk[:, :], xt[s][:, j, :], AF.Square, R=[xt[s]], W=[junk, ss[s]], accum_out=ss[s][:, j:j + 1])
            c.act(junk[0:4, :], xh[s][0:4, :], AF.Square, R=[xh[s]], W=[junk, ss[s]], accum_out=ss[s][0:4, 4:5])
            c.act(rstd[s][:, 0:5], ss[s][:, 0:5], AF.Ln, R=[ss[s]], W=[rstd[s]], scale=1.0 / 1024, bias=EPS)
            c.act(rstd[s][:, 0:5], rstd[s][:, 0:5], AF.Exp, R=[rstd[s]], W=[rstd[s]], scale=-0.5)
            for j in range(4):
                c.stt("dve", hb[s][:, j, :], xt[s][:, j, :], rstd[s][:, j:j + 1], n1w[:, :], ALU.mult, ALU.mult,
                      R=[xt[s], rstd[s], n1w], W=[hb[s]])
            c.stt("dve", hbh[s][0:4, :], xh[s][0:4, :], rstd[s][0:4, 4:5], n1w[0:4, :], ALU.mult, ALU.mult,
                  R=[xh[s], rstd[s], n1w], W=[hbh[s]])

        def prepB(b):
            s = b % 2
            for j in range(4):
                bk = nbank()
                pv = bk[:, :].bitcast(BF16).rearrange("p (a b) -> p a b", a=8)
                for kc in range(8):
                    c.tr(pv[:, kc, :], hb[s][:, j, kc * 128:(kc + 1) * 128], ident[:, :], R=[hb[s], ident], W=[bk], inc=(kc == 7))
                c.cp("dve" if j % 2 == 0 else "act", hT[s][:, :, 2 + 128 * j:2 + 128 * (j + 1)], pv[:, :, :], R=[bk], W=[hT[s]])
            bk = nbank()
            pv = bk[:, :].bitcast(BF16).rearrange("p (a b) -> p a b", a=8)
            for kc in range(8):
                c.tr(pv[:, kc, 0:4], hbh[s][0:4, kc * 128:(kc + 1) * 128], ident[0:4, 0:4], R=[hbh[s], ident], W=[bk], inc=(kc == 7))
            c.cp("dve", hT[s][:, :, 0:2], pv[:, :, 0:2], R=[bk], W=[hT[s]])
            c.cp("dve", hT[s][:, :, 514:516], pv[:, :, 2:4], R=[bk], W=[hT[s]])

        def load_w(c0, w):
            t = wt[wt_i[0] % 3]
            wt_i[0] += 1
            c.dma(t[:, :, 0:w], D["w_in_b"][:, c0:c0 + w].rearrange("(kc p) c -> p kc c", p=128), W=[t], ds=t.ds)
            return t

        def tok_group(t, b, c0, w, func, dst, dcol, scale=None):
            s = b % 2
            so = st[st_i[0] % 3]
            st_i[0] += 1
            for j in range(4):
                bk = nbank()
                for kc in range(8):
                    c.mm(bk[:, 0:w], hT[s][:, kc, 2 + 128 * j:2 + 128 * (j + 1)], t[:, kc, 0:w], kc == 0, kc == 7,
                         R=[hT[s], t], W=[bk], inc=(kc == 7))
                if func is None:
                    c.cp("dve", so[:, j, 0:w], bk[:, 0:w], R=[bk], W=[so])
                else:
                    c.act(so[:, j, 0:w], bk[:, 0:w], func, R=[bk], W=[so], scale=scale)
            c.dma(dst[b * 512:(b + 1) * 512, dcol:dcol + w].rearrange("(j p) c -> p j c", p=128), so[:, :, 0:w], R=[so], ds=so.ds)

        def fm_store(b, src, dst):
            c.dma(dst[b, :, :].rearrange("(cc p) t -> p cc t", p=128), src[:, :, :], R=[src], ds=src.ds)

        def tm_from_fm(b, src, dst, dcol):
            so = st[st_i[0] % 3]
            st_i[0] += 1
            for j in range(4):
                bk = nbank()
                pv = bk[:, :].bitcast(BF16).rearrange("p (a b) -> p a b", a=8)
                for cc in range(4):
                    c.tr(pv[:, cc, :], src[:, cc, j * 128:(j + 1) * 128], ident[:, :], R=[src, ident], W=[bk], inc=(cc == 3))
                c.cp("dve" if j % 2 == 0 else "act", so[:, j, :].rearrange("p (a b) -> p a b", a=4), pv[:, 0:4, :], R=[bk], W=[so])
            c.dma(dst[b * 512:(b + 1) * 512, dcol:dcol + 512].rearrange("(j p) c -> p j c", p=128), so[:, :, :], R=[so], ds=so.ds)

        def xbc_group(t, b, g):
            s = b % 2
            o = xo[g % 2]
            def proj(cc):
                ch = g * 4 + cc
                bks = [nbank(), nbank()]
                for hf in range(2):
                    for kc in range(8):
                        c.mm(bks[hf][:, 0:260], t[:, kc, cc * 128:(cc + 1) * 128], hT[s][:, kc, hf * 256:hf * 256 + 260], kc == 0, kc == 7,
                             R=[hT[s], t], W=[bks[hf]], inc=(kc == 7))
                xi = xin[ch % 2]
                c.cp("act", xi[:, 0, :], bks[0][:, 0:260], R=[bks[0]], W=[xi])
                c.cp("dve", xi[:, 1, :], bks[1][:, 0:260], R=[bks[1]], W=[xi])

            def conv(cc):
                ch = g * 4 + cc
                xi = xin[ch % 2]
                bk2 = nbank()
                for hf in range(2):
                    for k in range(5):
                        c.mm(bk2[:, hf * 256:(hf + 1) * 256], dg[:, ch * 5 + k, :], xi[:, hf, k:k + 256], k == 0, k == 4,
                             R=[dg, xi], W=[bk2], inc=(k == 4 and hf == 1))
                c.act(o[:, cc, :], bk2[:, :], AF.Silu, R=[bk2, cb], W=[o], bias=cb[:, ch:ch + 1])

            proj(0)
            for cc in range(4):
                if cc + 1 < 4:
                    proj(cc + 1)
                conv(cc)
            def fin():
                if g < 4:
                    tm_from_fm(b, o, D["xs"], g * 512)
                elif g == 4:
                    fm_store(b, o, D["BT"])
                    tm_from_fm(b, o, D["Bm"], 0)
                else:
                    fm_store(b, o, D["CT"])

            pending.append(fin)

        def fm_group(t, b, c0, dst, scale):
            s = b % 2
            o = xo[0] if dst is D["qT"] else xo[1]
            for cc in range(4):
                bk = nbank()
                for kc in range(8):
                    c.mm(bk[:, :], t[:, kc, cc * 128:(cc + 1) * 128], hT[s][:, kc, 2:514], kc == 0, kc == 7, R=[hT[s], t], W=[bk], inc=(kc == 7))
                c.act(o[:, cc, :], bk[:, :], AF.Copy, R=[bk], W=[o], scale=scale)
            fm_store(b, o, dst)
            if dst is D["kT"]:
                tm_from_fm(b, o, D["km"], 0)

        def gk_group(t, b):
            s = b % 2
            for z in range(2):
                bk = nbank()
                for kc in range(8):
                    c.mm(bk[0:16, :], t[:, kc, z * 16:(z + 1) * 16], hT[s][:, kc, 2:514], kc == 0, kc == 7, R=[hT[s], t], W=[bk], inc=(kc == 7))
                c.cp("dve", gkT[z][0:16, :], bk[0:16, :], R=[bk], W=[gkT[z]])
            for j in range(4):
                l = lat[j % 2]
                e = et[j % 2]
                for z in range(2):
                    bk = nbank()
                    c.mm(bk[:, :], gkT[z][0:33, j * 128:(j + 1) * 128], w2a[0:33, z * 512:(z + 1) * 512], True, True, R=[gkT[z], w2a], W=[bk], inc=True)
                    c.act(e[:, z * 512:(z + 1) * 512], bk[:, :], AF.Exp, R=[bk], W=[e], scale=-1.0)
                c.act(l[:, :], e[:, :], AF.Ln, R=[e], W=[l], bias=1.0)
                c.dma(D["la"][b * 512 + j * 128:b * 512 + (j + 1) * 128, :], l[:, :], R=[l], ds=l.ds)

        def dt_group(t, b):
            s = b % 2
            for j in range(4):
                bk = nbank()
                for kc in range(8):
                    c.mm(bk[:, 0:64], hT[s][:, kc, 2 + 128 * j:2 + 128 * (j + 1)], t[:, kc, 0:64], kc == 0, kc == 7, R=[hT[s], t], W=[bk], inc=(kc == 7))
                c.tt("dve", dte[:, j, :], bk[:, 0:64], dtb[:, :], ALU.add, R=[bk, dtb], W=[dte])
            c.act(dte[:, :, :], dte[:, :, :], AF.Exp, R=[dte], W=[dte])
            c.act(dtt[:, :, :], dte[:, :, :], AF.Ln, R=[dte], W=[dtt], bias=1.0)
            c.dma(D["dts"][b * 512:(b + 1) * 512, :].rearrange("(j p) c -> p j c", p=128), dtt[:, :, :], R=[dtt], ds=dtt.ds)

        seq = []
        for b in range(NB):
            if b + 1 < NB:
                seq.append((None, lambda t, b=b: load_x(b + 1)))
            for g in range(4):
                seq.append(((O_Z + g * 512, 512), lambda t, b=b, g=g: tok_group(t, b, O_Z + g * 512, 512, AF.Silu, D["zs"], g * 512)))
            if b + 1 < NB:
                seq.append((None, lambda t, b=b: prepA(b + 1)))
            seq.append(((O_GK, 32), lambda t, b=b: gk_group(t, b)))
            seq.append(((O_DT, 64), lambda t, b=b: dt_group(t, b)))
            if b + 1 < NB:
                seq.append((None, lambda t, b=b: prepB(b + 1)))
            for g in range(2):
                seq.append(((O_G + g * 512, 512), lambda t, b=b, g=g: tok_group(t, b, O_G + g * 512, 512, AF.Silu, D["gs"], g * 512)))
            for g in range(6):
                seq.append(((O_XBC + g * 512, 512), lambda t, b=b, g=g: xbc_group(t, b, g)))
            for g in range(4):
                seq.append(((O_GT + g * 512, 512), lambda t, b=b, g=g: tok_group(t, b, O_GT + g * 512, 512, AF.Tanh, D["gts"], g * 512, scale=0.5)))
            for g in range(2):
                seq.append(((O_V + g * 512, 512), lambda t, b=b, g=g: tok_group(t, b, O_V + g * 512, 512, None, D["vs"], g * 512)))
            seq.append(((O_Q, 512), lambda t, b=b: fm_group(t, b, O_Q, D["qT"], 128.0 ** -0.5)))
            seq.append(((O_K, 512), lambda t, b=b: fm_group(t, b, O_K, D["kT"], 1.0)))
        loaded = {}
        pending = []

        def ensure(i):
            if i < len(seq) and seq[i][0] is not None and i not in loaded:
                loaded[i] = load_w(*seq[i][0])

        load_x(0)
        if getattr(c, "ds_win", None) is not None:
            c.eng["sp"].wait_ge(c.ds_win.h, c.ds_win.count)
            c.seen["sp"][c.ds_win.key] = c.ds_win.count
        prepA(0)
        prepB(0)
        for i, (wl, fn) in enumerate(seq):
            ensure(i)
            ensure(i + 1)
            ensure(i + 2)
            prev = list(pending)
            del pending[:]
            fn(loaded.pop(i) if wl is not None else None)
            for f in prev:
                f()
        for f in pending:
            f()
    c.end_phase()


def scan_phase(c, D, cfg, d):
    NT = cfg.NT
    TPS = cfg.SEG // 128
    fwd = d == "f"
    z = 0 if fwd else 1
    with ExitStack() as es:
        S = lambda n, sh, dt, dma=False: c.tile(es, n, sh, dt, dma)
        MA = load_cst(c, es, D, "A" + d, BF16)
        MAs = load_cst(c, es, D, "As" + d, BF16)
        MC = load_cst(c, es, D, "C" + d, F32)
        MCs = load_cst(c, es, D, "Cs" + d, BF16)
        ones = load_cst(c, es, D, "ch0", BF16)
        alog = load_cst(c, es, D, "alog")
        flags = load_cst(c, es, D, "flags")
        MCb = S("MCb", [128, 128], BF16)
        c.cp("pool", MCb[:, :], MC[:, :], R=[MC], W=[MCb])
        negA = S("negA", [128, 32], F32)
        c.act(negA[:, :], alog[:, z * 32:(z + 1) * 32], AF.Exp, R=[alog], W=[negA])
        c.ts("dve", negA[:, :], negA[:, :], -1.0, None, ALU.mult, None, R=[negA], W=[negA])
        identf = load_cst(c, es, D, "ident")
        ident = S("identb", [128, 128], BF16)
        c.cp("pool", ident[:, :], identf[:, :], R=[identf], W=[ident])
        if fwd:
            dD = load_cst(c, es, D, "dD")
            Dg = S("Dg", [128, 32, 128], BF16)
            for h in range(32):
                c.ts("dve", Dg[:, h, :], identf[:, :], dD[:, h:h + 1], None, ALU.mult, None, R=[identf, dD], W=[Dg])
        NS = 3
        xs = [S("xs%d" % i, [128, 2048], BF16, True) for i in range(NS)]
        Bm = [S("Bm%d" % i, [128, 512], BF16, True) for i in range(NS)]
        km = [S("km%d" % i, [128, 512], BF16, True) for i in range(NS)]
        vv = [S("vv%d" % i, [128, 1024], BF16, True) for i in range(NS)]
        la = [S("la%d" % i, [128, 512], BF16, True) for i in range(NS)]
        dt = [S("dt%d" % i, [128, 32], F32, True) for i in range(NS)]
        BT = [S("BT%d" % i, [128, 4, 128], BF16, True) for i in range(NS)]
        CT = [S("CT%d" % i, [128, 4, 128], BF16, True) for i in range(NS)]
        qT = [S("qT%d" % i, [128, 4, 128], BF16, True) for i in range(NS)]
        kT = [S("kT%d" % i, [128, 4, 128], BF16, True) for i in range(NS)]
        if fwd:
            ybl = [S("ybl%d" % i, [128, 2048], BF16, True) for i in range(2)]
            obl = [S("obl%d" % i, [128, 1024], BF16, True) for i in range(2)]
            yn = [S("yn%d" % i, [128, 2048], BF16, True) for i in range(2)]
            on = [S("on%d" % i, [128, 1024], BF16, True) for i in range(2)]
        else:
            ybs = [S("ybs%d" % i, [128, 2048], BF16, True) for i in range(2)]
            obs = [S("obs%d" % i, [128, 1024], BF16, True) for i in range(2)]
        abf = [S("abf%d" % i, [128, 32], BF16) for i in range(2)]
        a32 = [S("a32_%d" % i, [128, 32], F32) for i in range(2)]
        ty16 = [S("ty16_%d" % i, [128, 512], BF16) for i in range(2)]
        zer = S("zer", [128, 128], BF16)
        c.op("pool", lambda E: E.memset(zer[:, :], 0.0), W=[zer])
        Eall = [S("Eall%d" % i, [128, 96], F32) for i in range(2)]
        Rt = [S("Rt%d" % i, [128, 32, 128], BF16) for i in range(2)]
        xd = [S("xd%d" % i, [128, 2048], BF16) for i in range(2)]
        xdd = [S("xdd%d" % i, [128, 2048], BF16) for i in range(2)]
        CBm = [S("CBm%d" % i, [128, 4, 128], BF16) for i in range(2)]
        Ep = [S("Ep%d" % i, [128, 512], BF16) for i in range(2)]
        Em = [S("Em%d" % i, [128, 512], BF16) for i in range(2)]
        Er = [S("Er%d" % i, [128, 512], BF16) for i in range(2)]
        gdec = [S("gdec%d" % i, [128, 4], F32) for i in range(2)]
        qin = [S("qin%d" % i, [128, 4, 128], BF16) for i in range(2)]
        kin = [S("kin%d" % i, [128, 4, 128], BF16) for i in range(2)]
        kdec = [S("kdec%d" % i, [128, 512], BF16) for i in range(2)]
        attm = [S("attm%d" % i, [128, 4, 128], BF16) for i in range(2)]
        Eg = [S("Eg%d" % i, [128, 512], BF16) for i in range(3)]
        WT = [S("WT%d" % i, [128, 4, 128], BF16) for i in range(3)]
        tmpy = [S("tmpy%d" % i, [128, 512], F32) for i in range(2)]
        tmps = [S("tmps%d" % i, [128, 512], F32) for i in range(2)]
        h32 = S("h32", [128, 2048], F32)
        hbf = S("hbf", [128, 2048], BF16)
        gh32 = S("gh32", [128, 1024], F32)
        ghbf = S("ghbf", [128, 1024], BF16)
        for tl in (h32, hbf, gh32, ghbf):
            c.op("pool", lambda E, tl=tl: E.memset(tl[:, :], 0.0), W=[tl])
        banks = c.banks
        bk_i = [0]

        def nbank():
            b = banks[bk_i[0] % 8]
            bk_i[0] += 1
            return b

        order = list(range(NT)) if fwd else list(range(NT - 1, -1, -1))
        last = 127 if fwd else 0

        def loadA(it):
            t = order[it]
            s = it % NS
            r = slice(t * 128, (t + 1) * 128)
            c.dma(dt[s][:, :], D["dts"][r, z * 32:(z + 1) * 32], W=[dt[s]], ds=dt[s].ds)
            c.dma(xs[s][:, :], D["xs"][r, :], W=[xs[s]], ds=xs[s].ds)
            c.dma(la[s][:, :], D["la"][r, z * 512:(z + 1) * 512], W=[la[s]], ds=la[s].ds)
            b, j = t // 4, t % 4
            for nm, tl in (("BT", BT[s]), ("CT", CT[s]), ("qT", qT[s]), ("kT", kT[s])):
                c.dma(tl[:, :, :], D[nm][b, :, j * 128:(j + 1) * 128].rearrange("(cc p) t -> p cc t", p=128), W=[tl], ds=tl.ds)
            c.dma(km[s][:, :], D["km"][r, :], W=[km[s]], ds=km[s].ds)
            c.dma(Bm[s][:, :], D["Bm"][r, :], W=[Bm[s]], ds=Bm[s].ds)
            c.dma(vv[s][:, :], D["vs"][r, :], W=[vv[s]], ds=vv[s].ds)

        def loadB(it):
            t = order[it]
            s2 = it % 2
            r = slice(t * 128, (t + 1) * 128)
            c.dma(ybl[s2][:, :], D["yb"][r, :], W=[ybl[s2]], ds=ybl[s2].ds)
            c.dma(obl[s2][:, :], D["ob"][r, :], W=[obl[s2]], ds=obl[s2].ds)

        def bc(ap, n, m):
            return ap.unsqueeze(2).to_broadcast([128, n, m])

        BY = [banks[0], banks[1]]
        BSEG = [banks[2], banks[3]]
        BY2, BS = banks[4], banks[5]
        BM = [banks[6], banks[7]]

        def stageA(it):
            s = it % NS
            u = it % 2
            xs3 = xs[s][:, :].rearrange("p (h q) -> p h q", h=32)
            xd3 = xd[u][:, :].rearrange("p (h q) -> p h q", h=32)
            xdd3 = xdd[u][:, :].rearrange("p (h q) -> p h q", h=32)

            def rpart(h0, h1):
                for h in range(h0, h1):
                    c.act(Rt[u][:, h, :], MCb[:, :], AF.Identity, R=[MCb, a32[u]], W=[Rt[u]], scale=a32[u][:, h:h + 1])

            def p0():
                c.tt("dve", a32[u][:, :], dt[s][:, :], negA[:, :], ALU.mult, R=[dt[s], negA], W=[a32[u]])
                c.cp("dve", abf[u][:, :], a32[u][:, :], R=[a32[u]], W=[abf[u]])
                b1 = BM[0]
                c.mm(b1[:, 0:32], MCb[:, :], abf[u][:, :], True, True, R=[MCb, abf[u]], W=[b1])
                c.mm(b1[:, 32:64], MA[:, :], abf[u][:, :], True, True, R=[MA, abf[u]], W=[b1])
                c.mm(b1[:, 64:96], ones[:, :], abf[u][:, :], True, True, R=[ones, abf[u]], W=[b1], inc=True)
                c.act(Eall[u][:, :], b1[:, 0:96], AF.Exp, R=[b1], W=[Eall[u]])
                rpart(0, 8)

            def p1():
                rpart(8, 16)
                rpart(16, 24)

            def p2():
                rpart(24, 32)
                bcb = BM[1]
                for g in range(4):
                    c.mm(bcb[:, g * 128:(g + 1) * 128], BT[s][:, g, :], CT[s][:, g, :], True, True, R=[BT[s], CT[s]], W=[bcb], inc=(g == 3))
                c.tt("dve", CBm[u][:, :, :], bcb[:, :].rearrange("p (g i) -> p g i", g=4), MC[:, :].unsqueeze(1).to_broadcast([128, 4, 128]), ALU.mult,
                     R=[bcb, MC], W=[CBm[u]])

            def p3():
                for h0 in (0, 16):
                    c.tt("dve", xd3[:, h0:h0 + 16, :], xs3[:, h0:h0 + 16, :], bc(dt[s][:, h0:h0 + 16], 16, 64), ALU.mult, R=[xs[s], dt[s]], W=[xd[u]])

            def p4():
                for h0 in (0, 16):
                    c.tt("dve", xdd3[:, h0:h0 + 16, :], xd3[:, h0:h0 + 16, :], bc(Eall[u][:, 32 + h0:32 + h0 + 16], 16, 64), ALU.mult,
                         R=[xd[u], Eall[u]], W=[xdd[u]])

            def p5():
                bb = BM[0]
                for hd in range(4):
                    c.mm(bb[:, hd * 128:(hd + 1) * 128], la[s][:, hd * 128:(hd + 1) * 128], MCs[:, :], True, True, R=[la[s], MCs], W=[bb], inc=(hd == 3))
                br = BM[1]
                c.mm(br[:, :], MAs[:, :], la[s][:, :], True, True, R=[MAs, la[s]], W=[br], inc=True)
                c.act(Ep[u][:, :], bb[:, :], AF.Exp, R=[bb], W=[Ep[u]])
                c.act(Em[u][:, :], bb[:, :], AF.Exp, R=[bb], W=[Em[u]], scale=-1.0)
                c.act(gdec[u][:, :].rearrange("p (a b) -> p a b", b=1), bb[:, :].rearrange("p (a b) -> p a b", a=4)[:, :, last:last + 1], AF.Exp,
                      R=[bb], W=[gdec[u]])
                c.act(Er[u][:, :], br[:, :], AF.Exp, R=[br], W=[Er[u]])

            def p6():
                c.tt("dve", qin[u][:, :, :], qT[s][:, :, :], Ep[u][:, :].rearrange("p (a b) -> p a b", a=4), ALU.mult, R=[qT[s], Ep[u]], W=[qin[u]])
                c.tt("dve", kin[u][:, :, :], kT[s][:, :, :], Em[u][:, :].rearrange("p (a b) -> p a b", a=4), ALU.mult, R=[kT[s], Em[u]], W=[kin[u]])
                c.tt("dve", kdec[u][:, :], km[s][:, :], Er[u][:, :], ALU.mult, R=[km[s], Er[u]], W=[kdec[u]])

            def p7():
                ba = BM[0]
                for hd in range(4):
                    c.mm(ba[:, hd * 128:(hd + 1) * 128], kin[u][:, hd, :], qin[u][:, hd, :], True, True, R=[kin[u], qin[u]], W=[ba], inc=(hd == 3))
                c.tt("dve", attm[u][:, :, :], ba[:, :].rearrange("p (a b) -> p a b", a=4), MC[:, :].unsqueeze(1).to_broadcast([128, 4, 128]), ALU.mult,
                     R=[ba, MC], W=[attm[u]])

            return [p0, p1, p2, p3, p4, p5, p6, p7]

        def seg_mm(u, sg):
            bk = BSEG[sg % 2]
            c.mm(bk[:, :], MA[:, :], Rt[u][:, sg * 4:(sg + 1) * 4, :].rearrange("p a b -> p (a b)"), True, True, R=[MA, Rt[u]], W=[bk], inc=True)
            e = Eg[sg % 3]
            c.act(e[:, :], bk[:, :], AF.Exp, R=[bk], W=[e])

        def stageB(it, fill):
            t = order[it]
            s = it % NS
            u = it % 2
            s2 = it % 2
            bd = None
            if fwd and t % TPS == 0 and t != 0:
                bd = t // TPS
            if (not fwd) and (t + 1) % TPS == 0 and t != NT - 1:
                bd = (t + 1) // TPS
            if bd is not None:
                c.ts("dve", h32[:, :], h32[:, :], flags[:, bd:bd + 1], None, ALU.mult, None, R=[h32, flags], W=[h32])
                c.cp("act", hbf[:, :], h32[:, :], R=[h32], W=[hbf])
                c.ts("dve", gh32[:, :], gh32[:, :], flags[:, bd:bd + 1], None, ALU.mult, None, R=[gh32, flags], W=[gh32])
                c.cp("act", ghbf[:, :], gh32[:, :], R=[gh32], W=[ghbf])
            yo = yn[s2] if fwd else ybs[s2]

            def tail(g):
                by = BY[g % 2]
                c.mm(BY2[:, :], CT[s][:, g, :], hbf[:, g * 512:(g + 1) * 512], True, True, R=[CT[s], hbf], W=[BY2], inc=True)
                c.mm(BS[:, :], Bm[s][:, g * 128:(g + 1) * 128], xdd[u][:, g * 512:(g + 1) * 512], True, True, R=[Bm[s], xdd[u]], W=[BS], inc=True)
                ty = ty16[g % 2]
                c.tt("dve", ty[:, :].rearrange("p (h q) -> p h q", h=8), BY2[:, :].rearrange("p (h q) -> p h q", h=8),
                     bc(Eall[u][:, g * 8:(g + 1) * 8], 8, 64), ALU.mult, R=[BY2, Eall[u]], W=[ty])
                c.mm(by[:, :], ident[:, :], ty[:, :], False, True, R=[ident, ty], W=[by], inc=True)
                c.cp("act", yo[:, g * 512:(g + 1) * 512], by[:, :], R=[by], W=[yo])
                tsb = tmps[g % 2]
                c.tt("pool", tsb[:, :].rearrange("p (h q) -> p h q", h=8), h32[:, g * 512:(g + 1) * 512].rearrange("p (h q) -> p h q", h=8),
                     bc(Eall[u][:, 64 + g * 8:64 + (g + 1) * 8], 8, 64), ALU.mult, R=[h32, Eall[u]], W=[tsb])
                c.tt("dve", h32[:, g * 512:(g + 1) * 512], tsb[:, :], BS[:, :], ALU.add, R=[tsb, BS], W=[h32])
                c.cp("act", hbf[:, g * 512:(g + 1) * 512], h32[:, g * 512:(g + 1) * 512], R=[h32], W=[hbf])

            seg_mm(u, 0)
            seg_mm(u, 1)
            for g in range(4):
                by = BY[g % 2]
                for sg in (2 * g, 2 * g + 1):
                    w = WT[sg % 3]
                    c.tt("dve", w[:, :, :], Eg[sg % 3][:, :].rearrange("p (a b) -> p a b", a=4), CBm[u][:, g:g + 1, :].to_broadcast([128, 4, 128]), ALU.mult,
                         R=[Eg[sg % 3], CBm[u]], W=[w])
                    if sg % 2 == 0:
                        if fwd:
                            c.mm(by[:, :], ident[:, :], ybl[s2][:, g * 512:(g + 1) * 512], True, False, R=[ident, ybl[s2]], W=[by])
                        else:
                            c.mm(by[:, :], zer[:, :], xd[u][:, g * 512:(g + 1) * 512], True, False, R=[zer, xd[u]], W=[by])
                    for e in range(4):
                        h = sg * 4 + e
                        o_ap = by[:, (h % 8) * 64:(h % 8) * 64 + 64]
                        if fwd:
                            c.mm(o_ap, w[:, e, :], xd[u][:, h * 64:(h + 1) * 64], False, False, R=[w, xd[u]], W=[by])
                            c.mm(o_ap, Dg[:, h, :], xs[s][:, h * 64:(h + 1) * 64], False, False, R=[Dg, xs[s]], W=[by], inc=(e == 3))
                        else:
                            c.mm(o_ap, w[:, e, :], xd[u][:, h * 64:(h + 1) * 64], False, False, R=[w, xd[u]], W=[by], inc=(e == 3))
                    if sg + 2 < 8:
                        seg_mm(u, sg + 2)
                    if fill:
                        fill[sg]()
                if g >= 1:
                    tail(g - 1)
            tail(3)
            c.dma(D["yn" if fwd else "yb"][t * 128:(t + 1) * 128, :], yo[:, :], R=[yo], ds=yo.ds)
            bo = [BM[0], BM[1]]
            for hd in range(4):
                o_ap = bo[hd // 2][:, (hd % 2) * 256:(hd % 2) * 256 + 256]
                c.mm(o_ap, attm[u][:, hd, :], vv[s][:, hd * 256:(hd + 1) * 256], True, False, R=[attm[u], vv[s]], W=[bo[hd // 2]])
                c.mm(o_ap, qin[u][:, hd, :], ghbf[:, hd * 256:(hd + 1) * 256], False, True, R=[qin[u], ghbf], W=[bo[hd // 2]], inc=True)
            bg = [BY2, BS]
            for hd in range(4):
                c.mm(bg[hd // 2][:, (hd % 2) * 256:(hd % 2) * 256 + 256], kdec[u][:, hd * 128:(hd + 1) * 128], vv[s][:, hd * 256:(hd + 1) * 256], True, True,
                     R=[kdec[u], vv[s]], W=[bg[hd // 2]], inc=True)
            for hd in range(4):
                c.stt("dve", gh32[:, hd * 256:(hd + 1) * 256], gh32[:, hd * 256:(hd + 1) * 256], gdec[u][:, hd:hd + 1],
                      bg[hd // 2][:, (hd % 2) * 256:(hd % 2) * 256 + 256], ALU.mult, ALU.add, R=[gh32, gdec[u], bg[hd // 2]], W=[gh32])
            if fwd:
                for hf in range(2):
                    c.tt("dve", on[s2][:, hf * 512:(hf + 1) * 512], bo[hf][:, :], obl[s2][:, hf * 512:(hf + 1) * 512], ALU.add, R=[bo[hf], obl[s2]], W=[on[s2]])
                c.dma(D["on"][t * 128:(t + 1) * 128, :], on[s2][:, :], R=[on[s2]], ds=on[s2].ds)
            else:
                for hf in range(2):
                    c.cp("act", obs[s2][:, hf * 512:(hf + 1) * 512], bo[hf][:, :], R=[bo[hf]], W=[obs[s2]])
                c.dma(D["ob"][t * 128:(t + 1) * 128, :], obs[s2][:, :], R=[obs[s2]], ds=obs[s2].ds)
            c.cp("act", ghbf[:, :], gh32[:, :], R=[gh32], W=[ghbf])

        loadA(0)
        if NT > 1:
            loadA(1)
        if fwd:
            loadB(0)
        for f in stageA(0):
            f()
        for it in range(NT):
            if it + 2 < NT:
                loadA(it + 2)
            if fwd and it + 1 < NT:
                loadB(it + 1)
            stageB(it, stageA(it + 1) if it + 1 < NT else None)
    c.end_phase()


def load_w_res(c, es, D, name, rows, cols):
    kc = rows // 128
    t = c.tile(es, "r_" + name, [128, kc, cols], BF16, dma=True)
    step = max(1, 8192 // cols)
    for k0 in range(0, kc, step):
        k1 = min(kc, k0 + step)
        c.dma(t[:, k0:k1, :], D[name + "_b"][k0 * 128:k1 * 128, :].rearrange("(kc p) c -> p kc c", p=128), W=[t], ds=t.ds)
    return t


def phase_d1(c, D, cfg):
    NT = cfg.NT
    with ExitStack() as es:
        S = lambda n, sh, dt, dma=False: c.tile(es, n, sh, dt, dma)
        ident = load_cst(c, es, D, "ident", BF16)
        n2w = load_cst(c, es, D, "n2w")
        snw = load_cst(c, es, D, "snw")
        gnw = load_cst(c, es, D, "gnw")
        wso = load_w_res(c, es, D, "w_ssd_out", 2048, 1024)
        wgo = load_w_res(c, es, D, "w_gla_out", 1024, 1024)
        wo = load_w_res(c, es, D, "w_o", 1024, 1024)
        yn = [S("yn%d" % i, [128, 2048], BF16, True) for i in range(2)]
        on = [S("on%d" % i, [128, 1024], BF16, True) for i in range(2)]
        gt = [S("gt%d" % i, [128, 2048], BF16, True) for i in range(3)]
        xt = [S("xt%d" % i, [128, 1024], F32, True) for i in range(3)]
        zs = [S("zs%d" % i, [128, 2048], BF16, True) for i in range(2)]
        gs = [S("gs%d" % i, [128, 1024], BF16, True) for i in range(2)]
        y32 = S("y32", [128, 2048], F32)
        o32 = S("o32", [128, 1024], F32)
        ynb2 = [S("ynb%d" % i, [128, 2048], BF16) for i in range(2)]
        onb2 = [S("onb%d" % i, [128, 1024], BF16) for i in range(2)]
        ssq1 = S("ssq1", [128, 8], F32)
        rs1 = S("rs1", [128, 8], F32)
        ynT = S("ynT", [128, 16, 128], BF16)
        onT = S("onT", [128, 8, 128], BF16)
        m1 = S("m1", [128, 1024], F32)
        m2 = S("m2", [128, 1024], F32)
        mx = [S("mx%d" % i, [128, 1024], BF16) for i in range(2)]
        mT = S("mT", [128, 8, 128], BF16)
        x1 = [S("x1_%d" % i, [128, 1024], F32, True) for i in range(2)]
        h2 = S("h2", [128, 1024], BF16)
        h2T = [S("h2T%d" % i, [128, 8, 128], BF16, True) for i in range(2)]
        junk = S("junk", [128, 1024], BF16)
        ssq = S("ssq", [128, 2], F32)
        rs = S("rs", [128, 2], F32)
        banks = c.banks
        bk_i = [0]

        def nbank():
            b = banks[bk_i[0] % 8]
            bk_i[0] += 1
            return b

        def load0(t):
            s = t % 2
            r = slice(t * 128, (t + 1) * 128)
            c.dma(yn[s][:, :], D["yn"][r, :], W=[yn[s]], ds=yn[s].ds)
            c.dma(zs[s][:, :], D["zs"][r, :], W=[zs[s]], ds=zs[s].ds)
            c.dma(on[s][:, :], D["on"][r, :], W=[on[s]], ds=on[s].ds)
            c.dma(gs[s][:, :], D["gs"][r, :], W=[gs[s]], ds=gs[s].ds)

        def load12(t):
            s = t % 3
            r = slice(t * 128, (t + 1) * 128)
            c.dma(gt[s][:, :], D["gts"][r, :], W=[gt[s]], ds=gt[s].ds)
            c.dma(xt[s][:, :], D["xm"][r, :], W=[xt[s]], ds=xt[s].ds)

        def transp(src, ncol, dst, engs):
            for b0 in range(0, ncol, 8):
                bk = nbank()
                pv = bk[:, :].bitcast(BF16).rearrange("p (a b) -> p a b", a=8)
                for k in range(8):
                    c.tr(pv[:, k, :], src[:, (b0 + k) * 128:(b0 + k + 1) * 128], ident[:, :], R=[src, ident], W=[bk], inc=(k == 7))
                c.cp(engs[(b0 // 8) % len(engs)], dst[:, b0:b0 + 8, :], pv[:, :, :], R=[bk], W=[dst])

        def stage0(t):
            s = t % 2
            ynb = ynb2[t % 2]
            onb = onb2[t % 2]
            c.tt("dve", y32[:, :], yn[s][:, :], zs[s][:, :], ALU.mult, R=[yn[s], zs[s]], W=[y32])
            for g in range(4):
                c.act(junk[:, 0:512], y32[:, g * 512:(g + 1) * 512], AF.Square, R=[y32], W=[junk, ssq1], accum_out=ssq1[:, g:g + 1])
            c.act(rs1[:, 0:4], ssq1[:, 0:4], AF.Ln, R=[ssq1], W=[rs1], scale=1.0 / 512, bias=EPS)
            c.act(rs1[:, 0:4], rs1[:, 0:4], AF.Exp, R=[rs1], W=[rs1], scale=-0.5)
            for g in range(4):
                c.stt("dve", ynb[:, g * 512:(g + 1) * 512], y32[:, g * 512:(g + 1) * 512], rs1[:, g:g + 1], snw[:, g * 512:(g + 1) * 512],
                      ALU.mult, ALU.mult, R=[y32, rs1, snw], W=[ynb])
            for hd in range(4):
                c.act(junk[:, 0:256], on[s][:, hd * 256:(hd + 1) * 256], AF.Square, R=[on[s]], W=[junk, ssq1], accum_out=ssq1[:, 4 + hd:5 + hd])
            c.act(rs1[:, 4:8], ssq1[:, 4:8], AF.Ln, R=[ssq1], W=[rs1], scale=1.0 / 256, bias=EPS)
            c.act(rs1[:, 4:8], rs1[:, 4:8], AF.Exp, R=[rs1], W=[rs1], scale=-0.5)
            for hd in range(4):
                c.stt("dve", o32[:, hd * 256:(hd + 1) * 256], on[s][:, hd * 256:(hd + 1) * 256], rs1[:, 4 + hd:5 + hd], gnw[:, hd * 256:(hd + 1) * 256],
                      ALU.mult, ALU.mult, R=[on[s], rs1, gnw], W=[o32])
            c.tt("dve", onb[:, :], o32[:, :], gs[s][:, :], ALU.mult, R=[o32, gs[s]], W=[onb])

        def stage1a(t):
            transp(ynb2[t % 2], 16, ynT, ["act", "dve"])
            transp(onb2[t % 2], 8, onT, ["act"])

        def stage1b(t):
            s = t % 3
            for hf in range(2):
                ba = nbank()
                for kc in range(16):
                    c.mm(ba[:, :], ynT[:, kc, :], wso[:, kc, hf * 512:(hf + 1) * 512], kc == 0, kc == 15, R=[ynT, wso], W=[ba], inc=(kc == 15))
                bb = nbank()
                for kc in range(8):
                    c.mm(bb[:, :], onT[:, kc, :], wgo[:, kc, hf * 512:(hf + 1) * 512], kc == 0, kc == 7, R=[onT, wgo], W=[bb], inc=(kc == 7))
                cs = slice(hf * 512, (hf + 1) * 512)
                c.stt("dve", m1[:, cs], gt[s][:, hf * 512:(hf + 1) * 512], 1.0, ba[:, :], ALU.add, ALU.mult, R=[gt[s], ba], W=[m1])
                c.stt("dve", m2[:, cs], gt[s][:, 1024 + hf * 512:1024 + (hf + 1) * 512], 1.0, bb[:, :], ALU.add, ALU.mult, R=[gt[s], bb], W=[m2])
                c.tt("pool", mx[t % 2][:, cs], m1[:, cs], m2[:, cs], ALU.add, R=[m1, m2], W=[mx[t % 2]])

        def stage2a(t):
            s = t % 3
            transp(mx[t % 2], 8, mT, ["act"])
            xo = x1[t % 2]
            for hf in range(2):
                bo = nbank()
                for kc in range(8):
                    c.mm(bo[:, :], mT[:, kc, :], wo[:, kc, hf * 512:(hf + 1) * 512], kc == 0, kc == 7, R=[mT, wo], W=[bo], inc=(kc == 7))
                c.stt("dve", xo[:, hf * 512:(hf + 1) * 512], bo[:, :], 0.5, xt[s][:, hf * 512:(hf + 1) * 512], ALU.mult, ALU.add, R=[bo, xt[s]], W=[xo])
            c.dma(D["x1"][t * 128:(t + 1) * 128, :], xo[:, :], R=[xo], ds=xo.ds)
            c.act(junk[:, :], xo[:, :], AF.Square, R=[xo], W=[junk, ssq], accum_out=ssq[:, 0:1])
            c.act(rs[:, 0:1], ssq[:, 0:1], AF.Ln, R=[ssq], W=[rs], scale=1.0 / 1024, bias=EPS)
            c.act(rs[:, 0:1], rs[:, 0:1], AF.Exp, R=[rs], W=[rs], scale=-0.5)
            c.stt("dve", h2[:, :], xo[:, :], rs[:, 0:1], n2w[:, :], ALU.mult, ALU.mult, R=[xo, rs, n2w], W=[h2])

        def stage2b(t):
            transp(h2, 8, h2T[t % 2], ["act"])
            c.dma(D["h2T"][:, t * 128:(t + 1) * 128].rearrange("(kc p) t -> p kc t", p=128), h2T[t % 2][:, :, :], R=[h2T[t % 2]], ds=h2T[t % 2].ds)

        load0(0)
        if NT > 1:
            load0(1)
        load12(0)
        if NT > 1:
            load12(1)
        stage0(0)
        if NT > 2:
            load0(2)
        if NT > 1:
            stage0(1)
        stage1a(0)
        stage1b(0)
        for t in range(NT):
            if t + 3 < NT:
                load0(t + 3)
            if t + 2 < NT:
                load12(t + 2)
            if t + 1 < NT:
                stage1a(t + 1)
            stage2a(t)
            if t + 2 < NT:
                stage0(t + 2)
            if t + 1 < NT:
                stage1b(t + 1)
            stage2b(t)
    c.end_phase()


def phase_ffn(c, D, cfg):
    T = cfg.T
    NBK = T // 256
    SEG = cfg.SEG
    with ExitStack() as es:
        S = lambda n, sh, dt, dma=False: c.tile(es, n, sh, dt, dma)
        fcw = load_cst(c, es, D, "fcw")
        fcb = load_cst(c, es, D, "fcb")
        fnw = load_cst(c, es, D, "fnw")
        flags = load_cst(c, es, D, "flags")
        wup = load_w_res(c, es, D, "w_ffn_up", 1024, 2 * D_FF)
        wdn = load_w_res(c, es, D, "w_ffn_down", D_FF, 1024)
        hT = [S("hT%d" % i, [128, 8, 258], BF16, True) for i in range(2)]
        x1 = S("x1", [128, 2, 1024], F32, True)
        actT = [S("actT%d" % i, [128, 22, 256], BF16) for i in range(2)]
        ag = [S("ag%d" % i, [128, 256], F32) for i in range(2)]
        av = [S("av%d" % i, [128, 256], F32) for i in range(2)]
        sg = [S("sg%d" % i, [128, 256], F32) for i in range(2)]
        x2 = [S("x2_%d" % i, [128, 1024], F32, True) for i in range(2)]
        junk = S("junk", [128, 1024], BF16)
        ssq = S("ssq", [128, 2], F32)
        rs = S("rs", [128, 2], F32)
        banks = c.banks
        bk_i = [0]

        def nbank():
            b = banks[bk_i[0] % 8]
            bk_i[0] += 1
            return b

        def load(kb):
            s = kb % 2
            t0 = kb * 256
            lo = 1 if kb == 0 else 0
            hi = 257 if kb == NBK - 1 else 258
            if kb == 0:
                c.op("pool", lambda E: E.memset(hT[s][:, :, 0:1], 0.0), W=[hT[s]])
            if kb == NBK - 1:
                c.op("pool", lambda E: E.memset(hT[s][:, :, 257:258], 0.0), W=[hT[s]])
            c.dma(hT[s][:, :, lo:hi], D["h2T"][:, t0 - 1 + lo:t0 - 1 + hi].rearrange("(kc p) t -> p kc t", p=128), W=[hT[s]], ds=hT[s].ds)
            if t0 % SEG == 0 and t0 != 0:
                bd = t0 // SEG
                c.ts("dve", hT[s][:, :, 0:1], hT[s][:, :, 0:1], flags[:, bd:bd + 1], None, ALU.mult, None, R=[hT[s], flags], W=[hT[s]])
            if (t0 + 256) % SEG == 0 and t0 + 256 != T:
                bd = (t0 + 256) // SEG
                c.ts("dve", hT[s][:, :, 257:258], hT[s][:, :, 257:258], flags[:, bd:bd + 1], None, ALU.mult, None, R=[hT[s], flags], W=[hT[s]])

        def conv(bk, ch, acc):
            c.act(acc[:, :], bk[:, 0:256], AF.Identity, R=[bk, fcw, fcb], W=[acc], bias=fcb[:, ch:ch + 1], scale=fcw[:, ch * 3:ch * 3 + 1])
            for k in (1, 2):
                c.stt("dve", acc[:, :], bk[:, k:k + 256], fcw[:, ch * 3 + k:ch * 3 + k + 1], acc[:, :], ALU.mult, ALU.add, R=[bk, fcw, acc], W=[acc])

        def up(kb):
            s = kb % 2
            aT = actT[kb % 2]
            for cc in range(22):
                bg = nbank()
                for kc in range(8):
                    c.mm(bg[:, 0:258], wup[:, kc, cc * 128:(cc + 1) * 128], hT[s][:, kc, :], kc == 0, kc == 7, R=[wup, hT[s]], W=[bg], inc=(kc == 7))
                bv = nbank()
                for kc in range(8):
                    c.mm(bv[:, 0:258], wup[:, kc, (cc + 22) * 128:(cc + 23) * 128], hT[s][:, kc, :], kc == 0, kc == 7, R=[wup, hT[s]], W=[bv], inc=(kc == 7))
                conv(bg, cc, ag[cc % 2])
                conv(bv, cc + 22, av[cc % 2])
                c.act(sg[cc % 2][:, :], ag[cc % 2][:, :], AF.Silu, R=[ag[cc % 2]], W=[sg[cc % 2]])
                c.tt("pool", aT[:, cc, :], sg[cc % 2][:, :], av[cc % 2][:, :], ALU.mult, R=[sg[cc % 2], av[cc % 2]], W=[aT])

        def down(kb):
            aT = actT[kb % 2]
            c.dma(x1[:, :, :], D["x1"][kb * 256:(kb + 1) * 256, :].rearrange("(j p) d -> p j d", p=128), W=[x1], ds=x1.ds)
            for j in range(2):
                o = x2[j]
                for hf in range(2):
                    bo = nbank()
                    for cc in range(22):
                        c.mm(bo[:, :], aT[:, cc, j * 128:(j + 1) * 128], wdn[:, cc, hf * 512:(hf + 1) * 512], cc == 0, cc == 21, R=[aT, wdn], W=[bo], inc=(cc == 21))
                    c.tt("dve", o[:, hf * 512:(hf + 1) * 512], bo[:, :], x1[:, j, hf * 512:(hf + 1) * 512], ALU.add, R=[bo, x1], W=[o])
                c.act(junk[:, :], o[:, :], AF.Square, R=[o], W=[junk, ssq], accum_out=ssq[:, 0:1])
                c.act(rs[:, 0:1], ssq[:, 0:1], AF.Ln, R=[ssq], W=[rs], scale=1.0 / 1024, bias=EPS)
                c.act(rs[:, 0:1], rs[:, 0:1], AF.Exp, R=[rs], W=[rs], scale=-0.5)
                c.stt("dve", o[:, :], o[:, :], rs[:, 0:1], fnw[:, :], ALU.mult, ALU.mult, R=[o, rs, fnw], W=[o])
                c.dma(D["y"][kb * 256 + j * 128:kb * 256 + (j + 1) * 128, :], o[:, :], R=[o], ds=o.ds)

        load(0)
        if NBK > 1:
            load(1)
        up(0)
        for kb in range(NBK):
            if kb + 1 < NBK:
                up(kb + 1)
            if kb + 2 < NBK:
                load(kb + 2)
            down(kb)
    c.end_phase()


def plan_cores(nb_prompt=16, nb_sample=2, nseg=4, seg=2048):
    cores = []
    for s in range(nb_sample):
        cores.append(([("s", s, i * seg) for i in range(nseg)], [0, 1, 1, 1, 0]))
    rest = [("p", i, 0) for i in range(nb_prompt)]
    ncore = 8 - nb_sample
    per = [len(rest) // ncore + (1 if i < len(rest) % ncore else 0) for i in range(ncore)]
    k = 0
    for i in range(ncore):
        segs = rest[k:k + per[i]]
        k += per[i]
        segs = segs + [None] * (nseg - len(segs))
        cores.append((segs, [0, 0, 0, 0, 0]))
    return cores


def core_inputs(inp, segs, flags, cfg):
    SEG, NSEG, T, NB = cfg.SEG, cfg.NSEG, cfg.T, cfg.NB
    xm = np.zeros((T, 1024), np.float32)
    for i, sg in enumerate(segs):
        if sg is None:
            continue
        src = inp["x_sample"] if sg[0] == "s" else inp["x_prompt"]
        xm[i * SEG:(i + 1) * SEG] = src[sg[1], sg[2]:sg[2] + SEG]
    xh = np.zeros((NB, 4, 1024), np.float32)
    for b in range(NB):
        t0 = b * 512
        for r, t in enumerate((t0 - 2, t0 - 1, t0 + 512, t0 + 513)):
            if t < 0 or t >= T:
                continue
            sb = t // SEG
            so = t0 // SEG
            if sb != so:
                bd = max(sb, so)
                if flags[bd] == 0:
                    continue
            xh[b, r] = xm[t]
    m = {"xm": xm, "xh": xh.reshape(NB * 4, 1024), "cst": build_consts(inp, list(flags) + [0, 0, 0])}
    for n, r, cc in WEIGHTS:
        m[n] = np.ascontiguousarray(inp[n][0])
    return m


_CACHE = {}


def kernel(**inputs):
    inp = {k: np.asarray(v) for k, v in inputs.items()}
    cfg = Cfg()
    if "nc" not in _CACHE:
        _CACHE["nc"] = build(cfg)[0]
    nc = _CACHE["nc"]
    cores = plan_cores()
    in_maps = [core_inputs(inp, segs, flags, cfg) for segs, flags in cores]
    res = run_bass_kernel_spmd(nc, in_maps, core_ids=list(range(8)))
    yp = np.zeros((16, 2048, 1024), np.float32)
    ys = np.zeros((2, 8192, 1024), np.float32)
    for ci, (segs, flags) in enumerate(cores):
        y = res.results[ci]["y"]
        for i, sg in enumerate(segs):
            if sg is None:
                continue
            blk = y[i * cfg.SEG:(i + 1) * cfg.SEG]
            if sg[0] == "s":
                ys[sg[1], sg[2]:sg[2] + cfg.SEG] = blk
            else:
                yp[sg[1]] = blk
    return (yp, ys)
```
